# Optimizing a Trainium2 kernel written in Bass

```python
import math, functools
import jax, jax.numpy as jnp
from jax import lax
import numpy as np

D_MODEL = 1024
BATCH = 4
SEQ = 4096
DEPTH = 1
DEC_BATCH = 128
DEC_SEQ = 4
PAST_LEN = 2048
PAGE_SIZE = 128

A_GROUPS = ((128, 1), (512, 4), (2048, 16))
A_HEADS_PER_GROUP = 4
A_HEADS = 12
A_HEAD_DIM = 128
A_WIDTH = A_HEADS * A_HEAD_DIM
A_OUT = A_HEADS_PER_GROUP * A_HEAD_DIM
A_BLOCK = 128
G_HEADS = 8
G_DK = 128
G_DV = 128
G_QK = G_HEADS * G_DK
G_V = G_HEADS * G_DV
G_CONV_CH = 2 * G_QK + G_V
G_CONV = 4
G_CHUNK = 64
N_BRANCH = 2
D_FF = 2816
NORM_EPS = 1e-6
IN_SIZES = (A_WIDTH, A_WIDTH, A_WIDTH, G_CONV_CH, G_V, G_HEADS, G_HEADS, N_BRANCH * D_MODEL)
IN_WIDTH = 3 * A_WIDTH + G_CONV_CH + G_V + 2 * G_HEADS + N_BRANCH * D_MODEL

kernel_name = 'hybrid_dilated_attn_gated_deltanet_step'


def _rms(x, w):
    xf = x.astype(jnp.float32)
    y = xf * lax.rsqrt(jnp.mean(xf * xf, axis=-1, keepdims=True) + NORM_EPS)
    return (y * w.astype(jnp.float32)).astype(x.dtype)


def _swiglu(x, w_gu, w_down):
    gate, up = jnp.split(x @ w_gu, 2, axis=-1)
    return (jax.nn.silu(gate) * up) @ w_down


def _split_cols(u, sizes):
    idx, acc = [], 0
    for s in sizes[:-1]:
        acc += s
        idx.append(acc)
    return jnp.split(u, idx, axis=-1)


def _alibi_slopes():
    return jnp.exp2(-8.0 * jnp.arange(1, A_HEADS + 1, dtype=jnp.float32) / A_HEADS)


def _dilated_prompt(q, k, v, window, dil, slopes):
    B, T, H, E = q.shape
    n = T // dil
    nb = -(-n // A_BLOCK)
    npad = nb * A_BLOCK
    wsub = window // dil

    def to_sub(a):
        a = a.reshape(B, n, dil, H, E).transpose(0, 2, 1, 3, 4)
        return jnp.pad(a, ((0, 0), (0, 0), (A_BLOCK, npad - n), (0, 0), (0, 0)))

    def band(a):
        a = to_sub(a)
        prev = a[:, :, :npad].reshape(B, dil, nb, A_BLOCK, H, E)
        cur = a[:, :, A_BLOCK:].reshape(B, dil, nb, A_BLOCK, H, E)
        return jnp.concatenate([prev, cur], axis=3)

    qs = to_sub(q)[:, :, A_BLOCK:].reshape(B, dil, nb, A_BLOCK, H, E)
    kb, vb = band(k), band(v)
    s = jnp.einsum('brnqhe,brnkhe->brnhqk', qs, kb, preferred_element_type=jnp.float32) * (E ** -0.5)
    qi = jnp.arange(A_BLOCK)[:, None]
    kj = jnp.arange(2 * A_BLOCK)[None, :]
    delta = A_BLOCK + qi - kj
    kpos = jnp.arange(nb)[:, None, None] * A_BLOCK + kj[None] - A_BLOCK
    valid = (delta >= 0) & (delta <= wsub) & (kpos >= 0)
    bias = -slopes.astype(jnp.float32)[:, None, None] * (dil * delta).astype(jnp.float32)
    s = jnp.where(valid[:, None], s + bias, -jnp.inf)
    m = jnp.max(s, axis=-1, keepdims=True)
    p = jnp.exp(s - m)
    den = jnp.sum(p, axis=-1)
    o = jnp.einsum('brnhqk,brnkhe->brnqhe', p, vb.astype(jnp.float32)) / jnp.swapaxes(den, -1, -2)[..., None]
    lse = jnp.swapaxes(m[..., 0] + jnp.log(den), -1, -2)
    o = o.reshape(B, dil, npad, H, E)[:, :, :n].transpose(0, 2, 1, 3, 4).reshape(B, T, H, E)
    lse = lse.reshape(B, dil, npad, H)[:, :, :n].transpose(0, 2, 1, 3).reshape(B, T, H)
    return o, lse


def _dilated_sample(q, k, v, kv_buf, window, dil, slopes):
    Bd, S, H, E = q.shape
    wb = kv_buf.shape[1]
    kc = jnp.concatenate([kv_buf[:, :, 0].astype(k.dtype), k], axis=1)
    vc = jnp.concatenate([kv_buf[:, :, 1].astype(v.dtype), v], axis=1)
    j = jnp.arange(window // dil + 1)
    idx = wb + jnp.arange(S)[:, None] - dil * j[None, :]
    valid = idx >= 0
    idx = jnp.maximum(idx, 0)
    kg = kc[:, idx]
    vg = vc[:, idx]
    s = jnp.einsum('bshe,bsjhe->bhsj', q, kg, preferred_element_type=jnp.float32) * (E ** -0.5)
    s = jnp.where(valid, s - slopes.astype(jnp.float32)[:, None, None] * (dil * j).astype(jnp.float32), -jnp.inf)
    m = jnp.max(s, axis=-1, keepdims=True)
    p = jnp.exp(s - m)
    den = jnp.sum(p, axis=-1)
    o = jnp.einsum('bhsj,bsjhe->bshe', p, vg.astype(jnp.float32)) / jnp.swapaxes(den, 1, 2)[..., None]
    lse = jnp.swapaxes(m[..., 0] + jnp.log(den), 1, 2)
    return o, lse


def _merge_groups(outs, lses, dtype):
    o = jnp.stack(outs, axis=0)
    w = jax.nn.softmax(jnp.stack(lses, axis=0), axis=0)
    o = jnp.sum(w[..., None] * o, axis=0)
    B, T = o.shape[:2]
    return o.reshape(B, T, A_OUT).astype(dtype)


def _group_heads(gi):
    return slice(gi * A_HEADS_PER_GROUP, (gi + 1) * A_HEADS_PER_GROUP)


def _attn_prompt(q, k, v, slopes):
    T = q.shape[1]
    outs, lses, rows = [], [], []
    for gi, (win, dil) in enumerate(A_GROUPS):
        hs = _group_heads(gi)
        kg, vg = k[:, :, hs], v[:, :, hs]
        o, l = _dilated_prompt(q[:, :, hs], kg, vg, win, dil, slopes[hs])
        keep = min(win, T)
        rows.append(jnp.stack([kg[:, T - keep:], vg[:, T - keep:]], axis=2))
        outs.append(o)
        lses.append(l)
    return _merge_groups(outs, lses, q.dtype), rows


def _attn_sample(q, k, v, bufs, slopes):
    outs, lses, rows = [], [], []
    for gi, (win, dil) in enumerate(A_GROUPS):
        hs = _group_heads(gi)
        kg, vg = k[:, :, hs], v[:, :, hs]
        o, l = _dilated_sample(q[:, :, hs], kg, vg, bufs[gi], win, dil, slopes[hs])
        rows.append(jnp.stack([kg, vg], axis=2))
        outs.append(o)
        lses.append(l)
    return _merge_groups(outs, lses, q.dtype), rows


def _short_conv(u, buf, w):
    T = u.shape[1]
    full = jnp.concatenate([buf, u], axis=1)
    y = full[:, :T] * w[0]
    for j in range(1, G_CONV):
        y = y + full[:, j:j + T] * w[j]
    return y, full[:, T:]


def _l2norm(x):
    return x * lax.rsqrt(jnp.sum(x * x, axis=-1, keepdims=True) + NORM_EPS)


def _gated_delta_chunked(q, k, v, g, beta, s0):
    B, T, H, _ = q.shape
    C = min(G_CHUNK, T)
    nc = -(-T // C)
    pad = nc * C - T

    def chunks(a):
        a = jnp.pad(a, ((0, 0), (0, pad)) + ((0, 0),) * (a.ndim - 2))
        a = a.reshape((B, nc, C) + a.shape[2:])
        return jnp.moveaxis(a, (1, 3), (0, 2))

    qc, kc, vc, gc, bc = (chunks(a) for a in (q, k, v, g, beta))
    gc = jnp.cumsum(gc, axis=-1)
    incl = jnp.tril(jnp.ones((C, C), bool))
    strict = jnp.tril(jnp.ones((C, C), bool), -1)
    decay = jnp.exp(jnp.where(incl, gc[..., :, None] - gc[..., None, :], -jnp.inf))
    kb = kc * bc[..., None]
    a_low = jnp.where(strict, jnp.einsum('nbhik,nbhjk->nbhij', kb, kc) * decay, 0.0)
    t_mat = a_low + jnp.eye(C, dtype=a_low.dtype)
    solve = functools.partial(lax.linalg.triangular_solve, left_side=True, lower=True, unit_diagonal=True)
    u = solve(t_mat, vc * bc[..., None])
    w = solve(t_mat, kb * jnp.exp(gc)[..., None])
    qk = jnp.einsum('nbhik,nbhjk->nbhij', qc, kc) * decay
    qg = qc * jnp.exp(gc)[..., None]
    kd = kc * jnp.exp(gc[..., -1:] - gc)[..., None]
    g_end = jnp.exp(gc[..., -1])

    def step(s, xs):
        u_i, w_i, qk_i, qg_i, kd_i, ge_i = xs
        v_new = u_i - jnp.einsum('bhck,bhkv->bhcv', w_i, s)
        o_i = jnp.einsum('bhck,bhkv->bhcv', qg_i, s) + jnp.einsum('bhcj,bhjv->bhcv', qk_i, v_new)
        s = s * ge_i[..., None, None] + jnp.einsum('bhck,bhcv->bhkv', kd_i, v_new)
        return s, o_i

    s_end, o = lax.scan(step, s0, (u, w, qk, qg, kd, g_end))
    o = jnp.transpose(o, (1, 0, 3, 2, 4)).reshape(B, nc * C, H, -1)[:, :T]
    return o, s_end


def _gated_deltanet(qkv, z, b_logit, a_logit, conv_buf, s0, conv_w, a_log, dt_bias, norm_w):
    B, T, _ = qkv.shape
    y, conv_new = _short_conv(qkv, conv_buf.astype(qkv.dtype), conv_w)
    y = jax.nn.silu(y.astype(jnp.float32))
    q, k, v = jnp.split(y, [G_QK, 2 * G_QK], axis=-1)
    q = _l2norm(q.reshape(B, T, G_HEADS, G_DK)) * (G_DK ** -0.5)
    k = _l2norm(k.reshape(B, T, G_HEADS, G_DK))
    v = v.reshape(B, T, G_HEADS, G_DV)
    beta = jax.nn.sigmoid(b_logit.astype(jnp.float32))
    g = -jnp.exp(a_log.astype(jnp.float32)) * jax.nn.softplus(a_logit.astype(jnp.float32) + dt_bias.astype(jnp.float32))
    o, s_new = _gated_delta_chunked(q, k, v, g, beta, s0.astype(jnp.float32))
    o = o * lax.rsqrt(jnp.mean(o * o, axis=-1, keepdims=True) + NORM_EPS) * norm_w.astype(jnp.float32)
    o = o * jax.nn.silu(z.astype(jnp.float32).reshape(B, T, G_HEADS, G_DV))
    return o.reshape(B, T, G_V).astype(qkv.dtype), conv_new, s_new


def _layer(x, attn_a, conv_buf, s0, norm_ffn1, w_ffn1_gu, w_ffn1_down, norm_mix, w_in, conv_w,
           gdn_a_log, gdn_dt_bias, gdn_norm, w_proj_a, w_proj_b, w_out, norm_ffn2, w_ffn2_gu, w_ffn2_down):
    B, T, _ = x.shape
    x = x + 0.5 * _swiglu(_rms(x, norm_ffn1), w_ffn1_gu, w_ffn1_down)
    h = _rms(x, norm_mix)
    qa, ka, va, qkv_b, z_b, b_b, a_b, gate = _split_cols(h @ w_in, IN_SIZES)
    heads = lambda t: t.reshape(B, T, A_HEADS, A_HEAD_DIM)
    o_a, kv_rows = attn_a(heads(qa), heads(ka), heads(va))
    o_b, conv_new, s_new = _gated_deltanet(qkv_b, z_b, b_b, a_b, conv_buf, s0, conv_w,
                                           gdn_a_log, gdn_dt_bias, gdn_norm)
    gt = jax.nn.sigmoid(gate.astype(jnp.float32))
    merged = (gt[..., :D_MODEL] * (o_a @ w_proj_a).astype(jnp.float32)
              + gt[..., D_MODEL:] * (o_b @ w_proj_b).astype(jnp.float32))
    x = x + merged.astype(x.dtype) @ w_out
    x = x + 0.5 * _swiglu(_rms(x, norm_ffn2), w_ffn2_gu, w_ffn2_down)
    return x, kv_rows, conv_new, s_new


def setup_inputs(seed: int = 0) -> dict:
    key = jax.random.key(seed)
    ks = jax.random.split(key, 24)
    f32 = jnp.float32
    nrm = lambda k, shape, scale: jax.random.normal(k, shape, f32) * scale
    gain = lambda k, shape: 1.0 + 0.02 * jax.random.normal(k, shape, f32)
    wins = [min(w, PAST_LEN) for (w, _) in A_GROUPS]
    kv_shape = lambda wb: (DEPTH, DEC_BATCH, wb, 2, A_HEADS_PER_GROUP, A_HEAD_DIM)
    dt = jnp.exp(jax.random.uniform(ks[14], (DEPTH, G_HEADS), f32, math.log(1e-3), math.log(0.1)))
    return {
        'x_prompt': nrm(ks[0], (BATCH, SEQ, D_MODEL), 1.0),
        'x_sample': nrm(ks[1], (DEC_BATCH, DEC_SEQ, D_MODEL), 1.0),
        'cache_kv_w128': nrm(ks[2], kv_shape(wins[0]), 1.0),
        'cache_kv_w512': nrm(ks[3], kv_shape(wins[1]), 1.0),
        'cache_kv_w2048': nrm(ks[4], kv_shape(wins[2]), 1.0),
        'state_conv': nrm(ks[5], (DEPTH, DEC_BATCH, G_CONV - 1, G_CONV_CH), 1.0),
        'state_ssm': nrm(ks[6], (DEPTH, DEC_BATCH, G_HEADS, G_DK, G_DV), 0.1),
        'norm_ffn1': gain(ks[7], (DEPTH, D_MODEL)),
        'w_ffn1_gu': nrm(ks[8], (DEPTH, D_MODEL, 2 * D_FF), D_MODEL ** -0.5),
        'w_ffn1_down': nrm(ks[9], (DEPTH, D_FF, D_MODEL), D_FF ** -0.5),
        'norm_mix': gain(ks[10], (DEPTH, D_MODEL)),
        'w_in': nrm(ks[11], (DEPTH, D_MODEL, IN_WIDTH), D_MODEL ** -0.5),
        'conv_w': nrm(ks[12], (DEPTH, G_CONV, G_CONV_CH), G_CONV ** -0.5),
        'gdn_a_log': jnp.log(jax.random.uniform(ks[13], (DEPTH, G_HEADS), f32, 1.0, 16.0)),
        'gdn_dt_bias': dt + jnp.log(-jnp.expm1(-dt)),
        'gdn_norm': gain(ks[15], (DEPTH, G_DV)),
        'w_proj_a': nrm(ks[16], (DEPTH, A_OUT, D_MODEL), A_OUT ** -0.5),
        'w_proj_b': nrm(ks[17], (DEPTH, G_V, D_MODEL), G_V ** -0.5),
        'w_out': nrm(ks[18], (DEPTH, D_MODEL, D_MODEL), D_MODEL ** -0.5),
        'norm_ffn2': gain(ks[19], (DEPTH, D_MODEL)),
        'w_ffn2_gu': nrm(ks[20], (DEPTH, D_MODEL, 2 * D_FF), D_MODEL ** -0.5),
        'w_ffn2_down': nrm(ks[21], (DEPTH, D_FF, D_MODEL), D_FF ** -0.5),
        'norm_out': gain(ks[22], (D_MODEL,)),
    }


def reference(x_prompt, x_sample, cache_kv_w128, cache_kv_w512, cache_kv_w2048, state_conv, state_ssm,
              norm_ffn1, w_ffn1_gu, w_ffn1_down, norm_mix, w_in, conv_w, gdn_a_log, gdn_dt_bias, gdn_norm,
              w_proj_a, w_proj_b, w_out, norm_ffn2, w_ffn2_gu, w_ffn2_down, norm_out):
    slopes = _alibi_slopes()
    xp, xs = x_prompt, x_sample
    bp = xp.shape[0]
    p_rows = [[], [], []]
    s_rows = [[], [], []]
    p_conv, p_ssm, s_conv, s_ssm = [], [], [], []
    for l in range(DEPTH):
        lw = (norm_ffn1[l], w_ffn1_gu[l], w_ffn1_down[l], norm_mix[l], w_in[l], conv_w[l],
              gdn_a_log[l], gdn_dt_bias[l], gdn_norm[l], w_proj_a[l], w_proj_b[l], w_out[l],
              norm_ffn2[l], w_ffn2_gu[l], w_ffn2_down[l])
        conv0 = jnp.zeros((bp, G_CONV - 1, G_CONV_CH), xp.dtype)
        ssm0 = jnp.zeros((bp, G_HEADS, G_DK, G_DV), jnp.float32)
        attn_p = functools.partial(_attn_prompt, slopes=slopes)
        xp, rows, cb, sb = _layer(xp, attn_p, conv0, ssm0, *lw)
        for gi in range(len(A_GROUPS)):
            p_rows[gi].append(rows[gi])
        p_conv.append(cb)
        p_ssm.append(sb)
        bufs = (cache_kv_w128[l], cache_kv_w512[l], cache_kv_w2048[l])
        attn_s = functools.partial(_attn_sample, bufs=bufs, slopes=slopes)
        xs, rows, cb, sb = _layer(xs, attn_s, state_conv[l], state_ssm[l], *lw)
        for gi in range(len(A_GROUPS)):
            s_rows[gi].append(rows[gi])
        s_conv.append(cb)
        s_ssm.append(sb)
    y_prompt = _rms(xp, norm_out)
    y_sample = _rms(xs, norm_out)
    return (y_prompt, y_sample,
            jnp.stack(p_rows[0]), jnp.stack(p_rows[1]), jnp.stack(p_rows[2]), jnp.stack(p_conv), jnp.stack(p_ssm),
            jnp.stack(s_rows[0]), jnp.stack(s_rows[1]), jnp.stack(s_rows[2]), jnp.stack(s_conv), jnp.stack(s_ssm))
```

```python
import math
import os
from contextlib import ExitStack
import numpy as np
import concourse.bass as bass
import concourse.mybir as mybir
from concourse.bass_utils import run_bass_kernel_spmd

F32 = mybir.dt.float32
BF16 = mybir.dt.bfloat16
AF = mybir.ActivationFunctionType
ALU = mybir.AluOpType

NEG = -30000.0
D = 1024
DFF = 2816
INW = 10768
C_QA, C_KA, C_VA, C_QKV, C_Z, C_B, C_G = 0, 1536, 3072, 4608, 7680, 8704, 8720
EPS = 1e-6
ST = 512
NPRE = 4
NMAIN = 4
SEQX = 4096
NB_S = 16
NTS = 64

STAGE = 9


def sl(start, n, step=1):
    return slice(start, start + (n - 1) * step + 1, step)


class Buf:
    __slots__ = ("name", "w", "r", "dsem", "dcnt", "excl")

    def __init__(self, name="", excl=False):
        self.excl = excl
        self.name = name
        self.w = None
        self.r = []
        self.dsem = None
        self.dcnt = 0


class Em:
    def __init__(self, nc):
        self.nc = nc
        self.engs = {"pe": nc.tensor, "act": nc.scalar, "dve": nc.vector,
                     "pool": nc.gpsimd, "sp": nc.sync}
        self.sem = {k: nc.alloc_semaphore("s_" + k) for k in self.engs}
        self.cnt = {k: 0 for k in self.engs}
        self.waited = {k: {} for k in self.engs}
        self.n_dsem = 0
        self.ninst = 0
        self.dbufs = []

    def _waits(self, e, reads, writes):
        deps = {}
        for b in reads:
            if b.w is not None:
                s, v = b.w
                if deps.get(id(s), (None, 0))[1] < v:
                    deps[id(s)] = (s, v)
        for b in writes:
            if b.w is not None:
                s, v = b.w
                if deps.get(id(s), (None, 0))[1] < v:
                    deps[id(s)] = (s, v)
            for (s, v) in b.r:
                if deps.get(id(s), (None, 0))[1] < v:
                    deps[id(s)] = (s, v)
        eng = self.engs[e]
        wd = self.waited[e]
        for key, (s, v) in deps.items():
            if e == "pe" and s is self.sem["pe"]:
                continue
            if wd.get(key, 0) >= v:
                continue
            eng.wait_ge(s, v)
            wd[key] = v
            self.ninst += 1

    def _post(self, ev, reads, writes):
        for b in reads:
            b.r.append(ev)
            if len(b.r) > 24:
                mx = {}
                for (s, v) in b.r:
                    if mx.get(id(s), (None, 0))[1] < v:
                        mx[id(s)] = (s, v)
                b.r = list(mx.values())
        for b in writes:
            b.w = ev
            b.r = []

    def op(self, e, fn, reads=(), writes=()):
        if any(b.excl for b in reads):
            writes = list(writes) + [b for b in reads if b.excl]
            reads = [b for b in reads if not b.excl]
        self._waits(e, reads, writes)
        ins = fn(self.engs[e])
        self.cnt[e] += 1
        ev = (self.sem[e], self.cnt[e])
        ins.then_inc(self.sem[e], 1)
        self.ninst += 1
        self._post(ev, reads, writes)
        return ins

    def dma(self, e, out, in_, reads=(), writes=(), owner=None, **kw):
        self._waits(e, reads, writes)
        if owner is None:
            owner = writes[0] if len(writes) else reads[0]
        if owner.dsem is None:
            owner.dsem = self.nc.alloc_semaphore("d%d" % self.n_dsem)
            self.n_dsem += 1
            self.dbufs.append(owner)
        ins = self.engs[e].dma_start(out=out, in_=in_, **kw)
        owner.dcnt += 16
        ins.then_inc(owner.dsem, 16)
        ev = (owner.dsem, owner.dcnt)
        self.ninst += 1
        self._post(ev, reads, writes)
        return ins

    def finish(self, e, bufs):
        self._waits(e, (), bufs)

    def barrier(self, dma_bufs=None):
        if dma_bufs is None:
            dma_bufs = list(self.dbufs)
        for e in ("pe", "act", "dve", "pool", "sp"):
            eng = self.engs[e]
            wd = self.waited[e]
            for o in ("pe", "act", "dve", "pool"):
                if o == e or self.cnt[o] == 0:
                    continue
                if wd.get(id(self.sem[o]), 0) < self.cnt[o]:
                    eng.wait_ge(self.sem[o], self.cnt[o])
                    wd[id(self.sem[o])] = self.cnt[o]
                    self.ninst += 1
            for b in dma_bufs:
                if b.dsem is not None and b.dcnt > 0 and wd.get(id(b.dsem), 0) < b.dcnt:
                    eng.wait_ge(b.dsem, b.dcnt)
                    wd[id(b.dsem)] = b.dcnt
                    self.ninst += 1


def build_program():
    nc = bass.Bass("TRN2", target_bir_lowering=False)
    em = Em(nc)

    def din(name, shape, dt=F32):
        return nc.dram_tensor(name, list(shape), dt, kind="ExternalInput").ap()

    def dout(name, shape, dt=F32):
        return nc.dram_tensor(name, list(shape), dt, kind="ExternalOutput").ap()

    def dscr(name, shape, dt=BF16):
        return nc.dram_tensor(name, list(shape), dt, kind="Internal").ap()

    xseq = din("xseq", [SEQX, D])
    xs_in = din("xs", [NTS, D])
    ck0 = din("ck0", [NB_S, 128, 1024])
    ck1 = din("ck1", [NB_S, 512, 1024])
    import os
    CK2R = 64 if os.environ.get('DBG_SMALL') else 2048
    ck2 = din("ck2", [NB_S, CK2R, 1024])
    sconv = din("sconv", [NB_S * 3, 3072])
    sssm = din("sssm", [NB_S, 8, 128, 128])
    w_gu1 = din("w_gu1", [D, 2 * DFF]); w_d1 = din("w_d1", [DFF, D])
    w_in = din("w_in", [D, INW])
    w_pa = din("w_pa", [512, D]); w_pb = din("w_pb", [D, D]); w_out = din("w_out", [D, D])
    w_gu2 = din("w_gu2", [D, 2 * DFF]); w_d2 = din("w_d2", [DFF, D])
    normT_in = din("normT", [128, 3, 8])
    normout_in = din("normout", [128, D])
    convw_in = din("convw", [128, 24, 4])
    gnorm_in = din("gnorm", [128, 1])
    alog_in = din("alog", [128, 8]); dtb_in = din("dtb", [128, 8])
    pmask_in = din("pmask", [128, 8])
    cst_f_in = din("cst_f", [128, 5, 128])
    ab01_in = din("ab01", [128, 8, 2, 128])
    ab2a_in = din("ab2a", [128, 4, 32])
    ab2b_in = din("ab2b", [32, 4, 32])
    sb_in = din("sbias", [128, 9, 16])
    sbn_in = din("sbiasn", [4, 3, 16])

    y_out = dout("y", [NMAIN * ST, D])
    ys_out = dout("ys", [NTS, D])
    kv0_out = dout("kv0", [128, 2, 512]); kv1_out = dout("kv1", [512, 2, 512]); kv2_out = dout("kv2", [2048, 2, 512])
    convp_out = dout("convp", [3, 3072])
    ssmp_out = dout("ssmp", [8, 128, 128])
    kvs_out = [dout("kvs%d" % g, [NTS, 2, 512]) for g in range(3)]
    convs_out = dout("convs", [NB_S, 3, 3072])
    ssms_out = dout("ssms", [NB_S, 8, 128, 128])

    gu_pan = [(fg * 128, min(4, 22 - fg) * 128) for fg in range(0, 22, 4)]
    gu_pan = gu_pan + [(DFF + c0, n) for (c0, n) in gu_pan]
    in_pan = ([(C_QA + g * 512, 512) for g in range(3)] + [(C_KA + g * 512, 512) for g in range(3)]
              + [(C_VA + g * 512, 512) for g in range(3)] + [(C_QKV + g * 512, 512) for g in range(6)]
              + [(C_Z + g * 512, 512) for g in range(2)] + [(C_B, 16)] + [(C_G + g * 512, 512) for g in range(4)])
    two = [(0, 512), (512, 512)]
    PAN = {"gu1": gu_pan, "gu2": gu_pan, "d1": two, "d2": two, "in": in_pan, "pa": two, "pb": two, "out": two}
    KCS = {"gu1": 8, "gu2": 8, "d1": 22, "d2": 22, "in": 8, "pa": 4, "pb": 8, "out": 8}
    POFF = {}
    scr1d = {}
    for nm_ in PAN:
        off = 0
        for (c0_, n_) in PAN[nm_]:
            POFF[(nm_, c0_, n_)] = off
            off += KCS[nm_] * 128 * n_
        scr1d[nm_] = dscr("s_" + nm_, [off])
    vscr = dscr("vscr", [SEQX, 1536])
    B_scr = {}

    def sb(name, shape, dt=F32):
        return nc.alloc_sbuf_tensor("sb_" + name, list(shape), dt)

    pstack = ExitStack()

    def sbp(name, shape, dt=F32):
        return pstack.enter_context(nc.sbuf_tensor("sb_" + name, list(shape), dt))

    cst_f = sb("cst_f", [128, 5, 128]); B_cst = Buf("cst")
    ident_f = cst_f[:, 0, :]; ones_f = cst_f[:, 1, :]; triU = cst_f[:, 2, :]
    nm_strict = cst_f[:, 3, :]; nm_inclT = cst_f[:, 4, :]
    cst_b = sb("cst_b", [128, 2, 128], BF16)
    ident_b = cst_b[:, 0, :]; ones_b = cst_b[:, 1, :]
    normT = sb("normT", [128, 3, 8]); normout = sb("normout", [128, D])
    convw = sb("convw", [128, 24, 4]); gnorm = sb("gnorm", [128, 1])
    alog = sb("alog", [128, 8]); dtb = sb("dtb", [128, 8]); negA = sb("negA", [128, 8])
    pmask = sb("pmask", [128, 8])
    epsc = sb("epsc", [128, 1])
    sbias = sb("sbias", [128, 9, 16]); sbiasn = sb("sbiasn", [4, 3, 16])

    with nc.Block() as block:
        for (dst, src) in [(cst_f, cst_f_in), (normT, normT_in), (normout, normout_in), (convw, convw_in),
                           (gnorm, gnorm_in), (alog, alog_in), (dtb, dtb_in), (pmask, pmask_in),
                           (sbias, sb_in), (sbiasn, sbn_in)]:
            em.dma("sp", dst[:], src, writes=[B_cst])
        em.op("pool", lambda e: e.memset(epsc[:], EPS), writes=[B_cst])
        em.op("dve", lambda e: e.tensor_copy(out=cst_b[:, 0, :], in_=ident_f), reads=[B_cst], writes=[B_cst])
        em.op("dve", lambda e: e.tensor_copy(out=cst_b[:, 1, :], in_=ones_f), reads=[B_cst], writes=[B_cst])
        em.op("act", lambda e: e.activation(out=negA[:], in_=alog[:], func=AF.Exp), reads=[B_cst], writes=[B_cst])
        em.op("dve", lambda e: e.tensor_scalar(out=negA[:], in0=negA[:], scalar1=-1.0, scalar2=None, op0=ALU.mult),
              reads=[B_cst], writes=[B_cst])

        def scr_view(nm, c0, n):
            o = POFF[(nm, c0, n)]
            kc = KCS[nm]
            return scr1d[nm][o:o + kc * 128 * n].rearrange("(p k c) -> p k c", p=128, k=kc, c=n)

        for (src, nm) in [(w_gu1, "gu1"), (w_d1, "d1"), (w_in, "in"), (w_pa, "pa"),
                          (w_pb, "pb"), (w_out, "out"), (w_gu2, "gu2"), (w_d2, "d2")]:
            bb = Buf("scr_" + nm)
            B_scr[nm] = bb
            for (c0_, n_) in PAN[nm]:
                em.dma("pool", scr_view(nm, c0_, n_), src[:, c0_:c0_ + n_].rearrange("(k p) c -> p k c", p=128),
                       writes=(), owner=bb, max_dma_last_dim=4096)
            bb.w = (bb.dsem, bb.dcnt)

        NPS = 6
        psf = [nc.alloc_psum_tensor("psf%d" % i, [128, 512], F32) for i in range(NPS)]
        B_psf = [Buf("psf%d" % i, excl=True) for i in range(NPS)]
        psb = [nc.alloc_psum_tensor("psb%d" % i, [128, 1024], BF16) for i in range(2)]
        B_psb = [Buf("psb%d" % i, excl=True) for i in range(2)]
        rr = {"f": 0, "b": 0, "ev": 0}

        held = set()

        def pget(hold=False):
            while True:
                i = rr["f"] % NPS
                rr["f"] += 1
                if i not in held:
                    break
            if hold:
                held.add(i)
            return psf[i], B_psf[i]

        def prelease(bp):
            held.discard(B_psf.index(bp))

        def pgetb():
            i = rr["b"] % 2
            rr["b"] += 1
            return psb[i], B_psb[i]

        def ev_eng():
            rr["ev"] += 1
            return "act" if rr["ev"] % 2 else "dve"

        def copy_op(eng, out, in_, reads, writes, scale=None):
            if eng == "act":
                if scale is None:
                    em.op("act", lambda e: e.activation(out=out, in_=in_, func=AF.Copy), reads=reads, writes=writes)
                else:
                    em.op("act", lambda e: e.activation(out=out, in_=in_, func=AF.Copy, scale=float(scale)),
                          reads=reads, writes=writes)
            else:
                if scale is None:
                    em.op(eng, lambda e: e.tensor_copy(out=out, in_=in_), reads=reads, writes=writes)
                else:
                    em.op(eng, lambda e: e.tensor_scalar(out=out, in0=in_, scalar1=float(scale), scalar2=None,
                                                         op0=ALU.mult), reads=reads, writes=writes)

        NSLOT = 4
        wslots = [sb("wslot%d" % i, [128, 8, 512], BF16) for i in range(NSLOT)]
        B_ws = [Buf("ws%d" % i) for i in range(NSLOT)]
        wst = {"i": 0, "pref": {}}

        def wload(nm, k0, nk, c0, ncols):
            key = (nm, k0, nk, c0, ncols)
            if key in wst["pref"]:
                return wst["pref"].pop(key)
            i = wst["i"] % NSLOT
            wst["i"] += 1
            src = scr_view(nm, c0, ncols)[:, k0:k0 + nk, :]
            em.dma("sp", wslots[i][:, 0:nk, 0:ncols], src, reads=[B_scr[nm]], writes=[B_ws[i]])
            return wslots[i], B_ws[i]

        def wprefetch(nm, k0, nk, c0, ncols):
            key = (nm, k0, nk, c0, ncols)
            if key not in wst["pref"]:
                wst["pref"][key] = wload(nm, k0, nk, c0, ncols)

        xt = sb("xt", [128, 4, D]); B_xt = [Buf("xt%d" % i) for i in range(4)]
        actT = [sb("actT%d" % i, [128, 8, ST], BF16) for i in range(2)]
        B_actT = [Buf("actT%d" % i) for i in range(2)]
        big = sb("big", [128, 24, ST], BF16); B_big = Buf("big")
        hidT = big[:, 0:22, :]; B_hid = B_big
        cy = [sb("cy0", [128, ST])] * 2; B_cy = [Buf("cy0")] * 2
        xn_tmp = cy[0][:, :].bitcast(BF16); B_xn = B_cy[0]
        sm = sb("sm", [128, 8]); B_sm = Buf("sm")
        sp_y = sb("sp_y", [128, 4, 8]); sp_p = sb("sp_p", [128, 4, 8]); sp_m = sb("sp_m", [128, 4, 8]); B_spt = Buf("sp_tmp")
        junk = xn_tmp; B_junk = B_xn

        def rms_rstd(x_ap, xb, npart, col):
            em.op("act", lambda e: e.activation(out=junk[0:npart, :], in_=x_ap, func=AF.Square,
                                                accum_out=sm[0:npart, col:col + 1]),
                  reads=[xb], writes=[B_junk, B_sm])
            em.op("act", lambda e: e.activation(out=sm[0:npart, col:col + 1], in_=sm[0:npart, col:col + 1], func=AF.Sqrt,
                                                bias=epsc[0:npart, 0:1], scale=1.0 / D), reads=[B_sm, B_cst], writes=[B_sm])
            em.op("dve", lambda e: e.reciprocal(out=sm[0:npart, col:col + 1], in_=sm[0:npart, col:col + 1]),
                  reads=[B_sm], writes=[B_sm])

        def norm_T(tiles, npart, nidx, dstT, B_dst):
            for t, (x_ap, xb) in enumerate(tiles):
                rms_rstd(x_ap, xb, npart, 0)
                em.op("dve", lambda e: e.tensor_scalar(out=xn_tmp[0:npart, :], in0=x_ap, scalar1=sm[0:npart, 0:1],
                                                       scalar2=None, op0=ALU.mult),
                      reads=[xb, B_sm], writes=[B_xn])
                pt, bpt = pgetb()
                for kc in range(8):
                    em.op("pe", lambda e: e.transpose(out=pt[:, kc * 128:kc * 128 + npart],
                                                      in_=xn_tmp[0:npart, kc * 128:(kc + 1) * 128],
                                                      identity=ident_b[0:npart, 0:npart]),
                          reads=[B_xn, B_cst], writes=[bpt])
                for kc in range(8):
                    eng = ev_eng()
                    o = dstT[:, kc, t * npart:(t + 1) * npart]
                    i_ = pt[:, kc * 128:kc * 128 + npart]
                    if eng == "act":
                        em.op("act", lambda e: e.activation(out=o, in_=i_, func=AF.Copy,
                                                            scale=normT[:, nidx, kc:kc + 1]),
                              reads=[bpt, B_cst], writes=[B_dst])
                    else:
                        em.op("dve", lambda e: e.tensor_scalar(out=o, in0=i_, scalar1=normT[:, nidx, kc:kc + 1],
                                                               scalar2=None, op0=ALU.mult),
                              reads=[bpt, B_cst], writes=[B_dst])

        def mm_fm(srcT, B_src, wsl, B_w, nk, col0, ncol, N, ps, bps, first=True, last=True, k_off=0):
            for kc in range(nk):
                em.op("pe", lambda e: e.matmul(ps[0:ncol, 0:N], lhsT=wsl[:, kc, col0:col0 + ncol],
                                               rhs=srcT[:, k_off + kc, 0:N],
                                               start=(first and kc == 0), stop=(last and kc == nk - 1)),
                      reads=[B_src, B_w], writes=[bps])

        def mm_tm(srcT, B_src, tok0, ntok, wsl, B_w, nk, col0, ncol, ps, bps, first=True, last=True, k_off=0):
            for kc in range(nk):
                em.op("pe", lambda e: e.matmul(ps[0:ntok, 0:ncol], lhsT=srcT[:, k_off + kc, tok0:tok0 + ntok],
                                               rhs=wsl[:, kc, col0:col0 + ncol],
                                               start=(first and kc == 0), stop=(last and kc == nk - 1)),
                      reads=[B_src, B_w], writes=[bps])

        def ffn(tiles, npart, nidx, gu, dn, srcT, B_srcT):
            N = npart * len(tiles)
            norm_T(tiles, npart, nidx, srcT, B_srcT)
            for fg in range(0, 22, 4):
                nf = min(4, 22 - fg)
                wg, bwg = wload(gu, 0, 8, fg * 128, nf * 128)
                wu, bwu = wload(gu, 0, 8, DFF + fg * 128, nf * 128)
                for j in range(nf):
                    fh = fg + j
                    pg, bpg = pget()
                    mm_fm(srcT, B_srcT, wg, bwg, 8, j * 128, 128, N, pg, bpg)
                    pu, bpu = pget()
                    mm_fm(srcT, B_srcT, wu, bwu, 8, j * 128, 128, N, pu, bpu)
                    sgi = fh % 2
                    em.op("act", lambda e: e.activation(out=sg_tmp[sgi][:, 0:N], in_=pg[:, 0:N], func=AF.Silu),
                          reads=[bpg], writes=[B_sg[sgi]])
                    em.op("dve", lambda e: e.tensor_tensor(out=hidT[:, fh, 0:N], in0=sg_tmp[sgi][:, 0:N],
                                                           in1=pu[:, 0:N], op=ALU.mult),
                          reads=[B_sg[sgi], bpu], writes=[B_hid])
            for half in range(2):
                wds = [wload(dn, kg * 8, min(8, 22 - kg * 8), half * 512, 512) for kg in range(3)]
                for t, (x_ap, xb) in enumerate(tiles):
                    po, bpo = pget()
                    for kg in range(3):
                        nk = min(8, 22 - kg * 8)
                        mm_tm(hidT, B_hid, t * npart, npart, wds[kg][0], wds[kg][1], nk, 0, 512, po, bpo,
                              first=(kg == 0), last=(kg == 2), k_off=kg * 8)
                    xo = x_ap[:, half * 512:(half + 1) * 512]
                    em.op("dve", lambda e: e.scalar_tensor_tensor(out=xo, in0=po[0:npart, :], scalar=0.5, in1=xo,
                                                                  op0=ALU.mult, op1=ALU.add),
                          reads=[bpo, xb], writes=[xb])

        QT = sb("QT", [128, 4, ST], BF16); B_QT = Buf("QT")
        denT = big[:, 8:16, :].bitcast(F32).rearrange("p (a b) n -> p a (b n)", b=2); B_nd = B_big
        numT = big[:, 16:24, :].bitcast(F32).rearrange("p (a b) n -> p a (b n)", b=2)
        oaT = sb("oaT", [128, 4, ST], BF16); B_oaT = Buf("oaT")
        qbT = big[:, 0:8, :]; kbT = big[:, 8:16, :]; vbT = big[:, 16:24, :]
        B_qb = B_big; B_kb = B_big; B_vb = B_big
        ztmp = sb("ztmp", [128, ST], BF16); B_zt = Buf("ztmp")
        oTr = actT[0]; B_oTr = B_actT[0]
        obT = actT[0]; B_obT = B_actT[0]
        cb = [sb("cb0", [128, ST + 3])] * 2; B_cb = [Buf("cb0")] * 2
        sg_tmp = cy; B_sg = B_cy
        csq = [sb("csq0", [128, ST], BF16)] * 2; B_csq = [Buf("csq0")] * 2
        crn = [sb("crn0", [128, ST])] * 2; B_crn = [Buf("crn0")] * 2
        ba_all = sb("ba_all", [128, 4, 16]); B_ba = Buf("ba")
        beta_all = sb("beta_all", [128, 4, 8]); g_all = sb("g_all", [128, 4, 8]); B_bg = Buf("betag")
        stage_f = crn; B_stf = B_crn
        stage_b = csq; B_stb = B_csq
        gd = {}
        for nm_, shp, dt_ in [("gc", [128, 8], F32), ("ngc", [128, 8], F32), ("ekd", [128, 8], F32), ("bgc", [128, 8], F32),
                              ("GC", [128, 8, 128], F32), ("EGC", [128, 8, 128], F32)]:
            gd[nm_] = sb("gd_" + nm_, shp, dt_)
        gd["R"] = gd["EGC"]
        B_gdt = Buf("gd_tile")
        B_GC = Buf("gd_GC"); B_R = B_GC
        hb = []

        def make_hb(par):
            d_ = {}
            for nm_, shp, dt_ in [("arg", [128, 128], F32), ("decT", [128, 128], F32),
                                  ("A_f", [128, 128], F32), ("Y_f", [128, 128], F32),
                                  ("AT_b", [128, 128], F32),
                                  ("P0", [128, 128], F32), ("PT0", [128, 128], F32),
                                  ("P1", [128, 128], F32), ("PT1", [128, 128], F32),
                                  ("vn", [128, 128], F32), ("QKT", [128, 128], F32),
                                  ("QgT", [128, 128], F32)]:
                d_[nm_] = sb("gh%d_%s" % (par, nm_), shp, dt_)
                d_["B_" + nm_] = Buf("gh%d_%s" % (par, nm_))
            d_["decs"] = d_["arg"]; d_["B_decs"] = d_["B_arg"]
            for a_, b_ in (("Kbg", "P0"), ("Kd", "PT0"), ("Vb", "P1"), ("WTn", "PT1")):
                d_[a_] = d_[b_]; d_["B_" + a_] = d_["B_" + b_]
            d_["A_b"] = d_["A_f"]; d_["B_A_b"] = d_["B_A_f"]
            d_["Y_b"] = d_["Y_f"]; d_["B_Y_b"] = d_["B_Y_f"]
            hb.append(d_)

        for par_ in range(3):
            make_hb(par_)
        ab01 = sbp("ab01", [128, 8, 2, 128]); ab2a = sbp("ab2a", [128, 4, 32]); ab2b = sbp("ab2b", [32, 4, 32])
        for (dst, src) in [(ab01, ab01_in), (ab2a, ab2a_in), (ab2b, ab2b_in)]:
            em.dma("sp", dst[:], src, writes=[B_cst])
        KT2 = sbp("KT2", [128, 4, SEQX], BF16); B_KT2 = Buf("KT2")
        KT01 = sbp("KT01", [128, 8, 2 * ST], BF16); B_KT01 = [Buf("KT01a"), Buf("KT01b")]
        S_f = sbp("S_f", [128, 8, 128]); S_b = S_f
        B_S = [Buf("S%d" % h) for h in range(8)]
        halo = sbp("halo", [128, 24, 3]); B_halo = Buf("halo")
        vt = [sbp("vt%d" % i, [128, 512], BF16) for i in range(2)]; B_vt = [Buf("vt%d" % i) for i in range(2)]
        scs = [sbp("scs0", [128, 512])] * 2; B_scs = [Buf("scs0")] * 2
        pts = [sbp("pts%d" % i, [128, 512], BF16) for i in range(2)]; B_pts = [Buf("pts%d" % i) for i in range(2)]
        cnt = {"st": 0, "sb": 0, "cb": 0}

        em.op("pool", lambda e: e.memset(S_f[:], 0.0), writes=B_S)
        em.op("pool", lambda e: e.memset(halo[:], 0.0), writes=[B_halo])

        def out_dma(dst, src, reads):
            if os.environ.get('DBG_NOOUT'):
                return
            em.dma("pool", dst, src, reads=reads, writes=(), owner=reads[0])

        gcount = {"h": 0}

        def gdn_tile(C, qT, kT, vT, B_q, B_k, B_v, beta, g, B_bgin, Sf, Sb, B_Sl, need_o, oT_dst, B_oT):
            nsq = min(4, max(1, int(math.ceil(math.log2(C))) - 1))
            p0, bp0 = pget()
            em.op("pe", lambda e: e.matmul(p0[0:C, 0:8], lhsT=triU[0:C, 0:C], rhs=g, start=True, stop=True),
                  reads=[B_cst, B_bgin], writes=[bp0])
            em.op("act", lambda e: e.activation(out=gd["gc"][0:C, :], in_=p0[0:C, 0:8], func=AF.Copy),
                  reads=[bp0], writes=[B_gdt])
            em.op("dve", lambda e: e.tensor_scalar(out=gd["ngc"][0:C, :], in0=p0[0:C, 0:8], scalar1=-1.0, scalar2=None,
                                                   op0=ALU.mult), reads=[bp0], writes=[B_gdt])
            for h in range(8):
                em.op("dve", lambda e: e.tensor_scalar(out=gd["R"][0:C, h, 0:C], in0=triU[0:C, 0:C],
                                                       scalar1=g[:, h:h + 1], scalar2=None, op0=ALU.mult),
                      reads=[B_cst, B_bgin], writes=[B_R])
            hpb = max(1, 512 // C)
            for h0 in range(0, 8, min(8, hpb)):
                nh = min(8, hpb)
                pg_, bpg_ = pget()
                src = pg_[:, 0:nh * C].rearrange("p (h c) -> p h c", c=C)
                em.op("pe", lambda e: e.matmul(src, lhsT=ones_f[0:C, :], rhs=gd["R"][0:C, h0:h0 + nh, 0:C],
                                               start=True, stop=True), reads=[B_cst, B_R], writes=[bpg_])
                em.op("dve", lambda e: e.tensor_copy(out=gd["GC"][:, h0:h0 + nh, 0:C], in_=src),
                      reads=[bpg_], writes=[B_GC])
                em.op("act", lambda e: e.activation(out=gd["EGC"][:, h0:h0 + nh, 0:C], in_=src, func=AF.Exp),
                      reads=[bpg_], writes=[B_GC])
            em.op("dve", lambda e: e.tensor_tensor(out=gd["ekd"][0:C, :], in0=gd["GC"][0:C, :, C - 1],
                                                   in1=gd["gc"][0:C, :], op=ALU.subtract),
                  reads=[B_GC, B_gdt], writes=[B_gdt])
            em.op("act", lambda e: e.activation(out=gd["ekd"][0:C, :], in_=gd["ekd"][0:C, :], func=AF.Exp),
                  reads=[B_gdt], writes=[B_gdt])
            em.op("act", lambda e: e.activation(out=gd["bgc"][0:C, :], in_=gd["gc"][0:C, :], func=AF.Exp),
                  reads=[B_gdt], writes=[B_gdt])
            em.op("dve", lambda e: e.tensor_tensor(out=gd["bgc"][0:C, :], in0=gd["bgc"][0:C, :], in1=beta, op=ALU.mult),
                  reads=[B_gdt, B_bgin], writes=[B_gdt])
            def head_gen(h, T):
                kTh = kT(h)
                pkk, bpkk = pget()
                em.op("pe", lambda e: e.matmul(pkk[0:C, 0:C], lhsT=kTh, rhs=kTh, start=True, stop=True),
                      reads=[B_k], writes=[bpkk])
                em.op("dve", lambda e: e.scalar_tensor_tensor(out=T["arg"][0:C, 0:C], in0=gd["GC"][0:C, h, 0:C],
                                                              scalar=-1.0, in1=nm_strict[0:C, 0:C],
                                                              op0=ALU.mult, op1=ALU.add),
                      reads=[B_GC, B_cst], writes=[T["B_arg"]])
                em.op("act", lambda e: e.activation(out=T["decs"][0:C, 0:C], in_=T["arg"][0:C, 0:C], func=AF.Exp,
                                                    bias=gd["gc"][0:C, h:h + 1]),
                      reads=[T["B_arg"], B_gdt], writes=[T["B_decs"]])
                em.op("dve", lambda e: e.scalar_tensor_tensor(out=T["A_f"][0:C, 0:C], in0=pkk[0:C, 0:C],
                                                              scalar=beta[:, h:h + 1], in1=T["decs"][0:C, 0:C],
                                                              op0=ALU.mult, op1=ALU.mult),
                      reads=[bpkk, B_bgin, T["B_decs"]], writes=[T["B_A_f"]])
                if need_o:
                    em.op("dve", lambda e: e.tensor_tensor(out=T["arg"][0:C, 0:C], in0=gd["GC"][0:C, h, 0:C],
                                                           in1=nm_inclT[0:C, 0:C], op=ALU.add),
                          reads=[B_GC, B_cst], writes=[T["B_arg"]])
                    em.op("act", lambda e: e.activation(out=T["decT"][0:C, 0:C], in_=T["arg"][0:C, 0:C], func=AF.Exp,
                                                        bias=gd["ngc"][0:C, h:h + 1]),
                          reads=[T["B_arg"], B_gdt], writes=[T["B_decT"]])
                yield
                pat, bpat = pget()
                em.op("pe", lambda e: e.transpose(out=pat[0:C, 0:C], in_=T["A_f"][0:C, 0:C], identity=ident_f[0:C, 0:C]),
                      reads=[T["B_A_f"], B_cst], writes=[bpat])
                em.op("dve", lambda e: e.scalar_tensor_tensor(out=T["Y_f"][0:C, 0:C], in0=pat[0:C, 0:C], scalar=-1.0,
                                                              in1=ident_f[0:C, 0:C], op0=ALU.mult, op1=ALU.add),
                      reads=[bpat, B_cst], writes=[T["B_Y_f"]])
                em.op("act", lambda e: e.activation(out=T["AT_b"][0:C, 0:C], in_=pat[0:C, 0:C], func=AF.Copy),
                      reads=[bpat], writes=[T["B_AT_b"]])
                P, PT_, BP, BPT = T["A_b"], T["AT_b"], T["B_A_b"], T["B_AT_b"]
                for s in range(nsq):
                    nP, nPT = T["P%d" % (s % 2)], T["PT%d" % (s % 2)]
                    BnP, BnPT = T["B_P%d" % (s % 2)], T["B_PT%d" % (s % 2)]
                    yield
                    pp, bpp = pget()
                    em.op("pe", lambda e: e.matmul(pp[0:C, 0:C], lhsT=PT_[0:C, 0:C], rhs=P[0:C, 0:C], start=True, stop=True),
                          reads=[BP, BPT], writes=[bpp])
                    copy_op("act", nP[0:C, 0:C], pp[0:C, 0:C], [bpp], [BnP])
                    if s < nsq - 1:
                        yield
                        pq, bpq = pget()
                        em.op("pe", lambda e: e.matmul(pq[0:C, 0:C], lhsT=P[0:C, 0:C], rhs=PT_[0:C, 0:C], start=True, stop=True),
                              reads=[BP, BPT], writes=[bpq])
                        copy_op("dve", nPT[0:C, 0:C], pq[0:C, 0:C], [bpq], [BnPT])
                    yield
                    py, bpy = pget()
                    em.op("pe", lambda e: e.matmul(py[0:C, 0:C], lhsT=nP[0:C, 0:C], rhs=T["Y_b"][0:C, 0:C], start=True, stop=True),
                          reads=[BnP, T["B_Y_b"]], writes=[bpy])
                    em.op("dve", lambda e: e.tensor_tensor(out=T["Y_f"][0:C, 0:C], in0=py[0:C, 0:C], in1=T["Y_f"][0:C, 0:C],
                                                           op=ALU.add), reads=[bpy, T["B_Y_f"]], writes=[T["B_Y_f"]])
                    P, PT_, BP, BPT = nP, nPT, BnP, BnPT
                yield
                ptk, bptk = pgetb()
                em.op("pe", lambda e: e.transpose(out=ptk[0:C, 0:128], in_=kTh, identity=ident_b[:, :]),
                      reads=[B_k, B_cst], writes=[bptk])
                em.op("pe", lambda e: e.transpose(out=ptk[0:C, 128:256], in_=vT(h), identity=ident_b[:, :]),
                      reads=[B_v, B_cst], writes=[bptk])
                em.op("dve", lambda e: e.tensor_scalar(out=T["Kbg"][0:C, :], in0=ptk[0:C, 0:128], scalar1=gd["bgc"][0:C, h:h + 1],
                                                       scalar2=None, op0=ALU.mult), reads=[bptk, B_gdt], writes=[T["B_Kbg"]])
                em.op("act", lambda e: e.activation(out=T["Kd"][0:C, :], in_=ptk[0:C, 0:128], func=AF.Copy,
                                                    scale=gd["ekd"][0:C, h:h + 1]), reads=[bptk, B_gdt], writes=[T["B_Kd"]])
                em.op("dve", lambda e: e.tensor_scalar(out=T["Vb"][0:C, :], in0=ptk[0:C, 128:256], scalar1=beta[:, h:h + 1],
                                                       scalar2=None, op0=ALU.mult), reads=[bptk, B_bgin], writes=[T["B_Vb"]])
                yield
                pw, bpw = pget()
                em.op("pe", lambda e: e.matmul(pw[:, 0:C], lhsT=T["Kbg"][0:C, :], rhs=T["Y_f"][0:C, 0:C], start=True, stop=True),
                      reads=[T["B_Kbg"], T["B_Y_f"]], writes=[bpw])
                copy_op("act", T["WTn"][:, 0:C], pw[:, 0:C], [bpw], [T["B_WTn"]], scale=-1.0)
                yield
                pv, bpv = pget()
                em.op("pe", lambda e: e.matmul(pv[0:C, 0:128], lhsT=T["Y_f"][0:C, 0:C], rhs=T["Vb"][0:C, :], start=True, stop=False),
                      reads=[T["B_Y_f"], T["B_Vb"]], writes=[bpv])
                em.op("pe", lambda e: e.matmul(pv[0:C, 0:128], lhsT=T["WTn"][:, 0:C], rhs=Sf(h), start=False, stop=True),
                      reads=[T["B_WTn"], B_Sl[h]], writes=[bpv])
                copy_op("dve", T["vn"][0:C, :], pv[0:C, 0:128], [bpv], [T["B_vn"]])
                if need_o:
                    yield
                    pqk, bpqk = pget()
                    em.op("pe", lambda e: e.matmul(pqk[0:C, 0:C], lhsT=kTh, rhs=qT(h), start=True, stop=True),
                          reads=[B_k, B_q], writes=[bpqk])
                    em.op("dve", lambda e: e.tensor_tensor(out=T["QKT"][0:C, 0:C], in0=pqk[0:C, 0:C], in1=T["decT"][0:C, 0:C],
                                                           op=ALU.mult), reads=[bpqk, T["B_decT"]], writes=[T["B_QKT"]])
                    em.op("pool", lambda e: e.tensor_tensor(out=T["QgT"][:, 0:C], in0=qT(h), in1=gd["EGC"][:, h, 0:C],
                                                            op=ALU.mult), reads=[B_q, B_GC], writes=[T["B_QgT"]])
                    yield
                    po_, bpo_ = pget()
                    em.op("pe", lambda e: e.matmul(po_[:, 0:C], lhsT=Sf(h), rhs=T["QgT"][:, 0:C], start=True, stop=False),
                          reads=[B_Sl[h], T["B_QgT"]], writes=[bpo_])
                    em.op("pe", lambda e: e.matmul(po_[:, 0:C], lhsT=T["vn"][0:C, :], rhs=T["QKT"][0:C, 0:C], start=False, stop=True),
                          reads=[T["B_vn"], T["B_QKT"]], writes=[bpo_])
                    copy_op("act", oT_dst(h), po_[:, 0:C], [bpo_], [B_oT])
                yield
                ps_, bps_ = pget()
                em.op("pe", lambda e: e.matmul(ps_[:, 0:128], lhsT=T["Kd"][0:C, :], rhs=T["vn"][0:C, :], start=True, stop=True),
                      reads=[T["B_Kd"], T["B_vn"]], writes=[bps_])
                em.op("dve", lambda e: e.scalar_tensor_tensor(out=Sf(h), in0=Sf(h), scalar=gd["EGC"][:, h, C - 1:C],
                                                              in1=ps_[:, 0:128], op0=ALU.mult, op1=ALU.add),
                      reads=[bps_, B_GC, B_Sl[h]], writes=[B_Sl[h]])
                yield

            NIL = len(hb)
            for h0 in range(0, 8, NIL):
                alive = [head_gen(h0 + j, hb[j]) for j in range(NIL) if h0 + j < 8]
                while alive:
                    for g_ in list(alive):
                        try:
                            next(g_)
                        except StopIteration:
                            alive.remove(g_)

        def beta_g(ba_ap, beta_o, g_o, npart, reads):
            n = ba_ap.shape[1]
            y = sp_y[0:npart, 0:n, :]; p = sp_p[0:npart, 0:n, :]; m = sp_m[0:npart, 0:n, :]
            em.op("act", lambda e: e.activation(out=beta_o, in_=ba_ap[:, :, 0:8], func=AF.Sigmoid), reads=reads, writes=[B_bg])
            for j in range(n):
                em.op("dve", lambda e: e.tensor_tensor(out=y[:, j, :], in0=ba_ap[:, j, 8:16], in1=dtb[0:npart, :], op=ALU.add),
                      reads=reads + [B_cst], writes=[B_spt])
            em.op("act", lambda e: e.activation(out=y, in_=y, func=AF.Exp), reads=[B_spt], writes=[B_spt])
            em.op("act", lambda e: e.activation(out=g_o, in_=y, func=AF.Ln, bias=1.0), reads=[B_spt], writes=[B_bg])
            em.op("dve", lambda e: e.tensor_scalar(out=p, in0=y, scalar1=1.0 / 7, scalar2=None, op0=ALU.mult),
                  reads=[B_spt], writes=[B_spt])
            for ck in (-1.0 / 6, 1.0 / 5, -1.0 / 4, 1.0 / 3, -1.0 / 2, 1.0):
                em.op("dve", lambda e: e.scalar_tensor_tensor(out=p, in0=p, scalar=float(ck), in1=y, op0=ALU.add, op1=ALU.mult),
                      reads=[B_spt], writes=[B_spt])
            em.op("dve", lambda e: e.tensor_single_scalar(out=m, in_=y, scalar=0.3, op=ALU.is_lt), reads=[B_spt], writes=[B_spt])
            em.op("dve", lambda e: e.tensor_tensor(out=p, in0=p, in1=g_o, op=ALU.subtract), reads=[B_spt, B_bg], writes=[B_spt])
            em.op("dve", lambda e: e.tensor_tensor(out=p, in0=p, in1=m, op=ALU.mult), reads=[B_spt], writes=[B_spt])
            em.op("dve", lambda e: e.tensor_tensor(out=g_o, in0=g_o, in1=p, op=ALU.add), reads=[B_spt, B_bg], writes=[B_bg])
            for j in range(n):
                em.op("dve", lambda e: e.tensor_tensor(out=g_o[:, j, :], in0=g_o[:, j, :], in1=negA[0:npart, :], op=ALU.mult),
                      reads=[B_bg, B_cst], writes=[B_bg])

        def conv_chunk(c, ps, bps, N, sample=False):
            i = cnt["cb"] % 2
            cnt["cb"] += 1
            if not sample:
                em.op("pool", lambda e: e.tensor_copy(out=cb[i][:, 0:3], in_=halo[:, c, :]), reads=[B_halo], writes=[B_cb[i]])
                em.op("act", lambda e: e.activation(out=cb[i][:, 3:3 + N], in_=ps[:, 0:N], func=AF.Copy), reads=[bps], writes=[B_cb[i]])
                em.op("pool", lambda e: e.tensor_copy(out=halo[:, c, :], in_=cb[i][:, N:N + 3]), reads=[B_cb[i]], writes=[B_halo])
                win = lambda j: cb[i][:, j:j + N]
                yv = cy[i][:, 0:N]
                rbuf = [B_cb[i]]
            else:
                cbs = scb[:, c, :, :]
                em.op("act", lambda e: e.activation(out=cbs[:, :, 3:7], in_=ps[:, 0:N].rearrange("p (b s) -> p b s", s=4),
                                                    func=AF.Copy), reads=[bps], writes=[B_scb])
                win = lambda j: cbs[:, :, j:j + 4]
                yv = cy[i][:, 0:N].rearrange("p (b s) -> p b s", s=4)
                rbuf = [B_scb]
            em.op("dve", lambda e: e.tensor_scalar(out=yv, in0=win(0), scalar1=convw[:, c, 0:1], scalar2=None, op0=ALU.mult),
                  reads=rbuf + [B_cst], writes=[B_cy[i]])
            for j in range(1, 4):
                em.op("dve", lambda e: e.scalar_tensor_tensor(out=yv, in0=win(j), scalar=convw[:, c, j:j + 1], in1=yv,
                                                              op0=ALU.mult, op1=ALU.add),
                      reads=rbuf + [B_cst, B_cy[i]], writes=[B_cy[i]])
            y2 = cy[i][:, 0:N]
            if c >= 16:
                em.op("act", lambda e: e.activation(out=vbT[:, c - 16, 0:N], in_=y2, func=AF.Silu), reads=[B_cy[i]], writes=[B_vb])
                return
            em.op("act", lambda e: e.activation(out=y2, in_=y2, func=AF.Silu), reads=[B_cy[i]], writes=[B_cy[i]])
            em.op("act", lambda e: e.activation(out=csq[i][:, 0:N], in_=y2, func=AF.Square), reads=[B_cy[i]], writes=[B_csq[i]])
            pss, bpss = pget()
            em.op("pe", lambda e: e.matmul(pss[:, 0:N], lhsT=ones_b, rhs=csq[i][:, 0:N], start=True, stop=True),
                  reads=[B_cst, B_csq[i]], writes=[bpss])
            em.op("act", lambda e: e.activation(out=crn[i][:, 0:N], in_=pss[:, 0:N], func=AF.Sqrt, bias=epsc[:, 0:1]),
                  reads=[bpss, B_cst], writes=[B_crn[i]])
            em.op("dve", lambda e: e.reciprocal(out=crn[i][:, 0:N], in_=crn[i][:, 0:N]), reads=[B_crn[i]], writes=[B_crn[i]])
            if c < 8:
                dst, bd, scl = qbT[:, c, 0:N], B_qb, 128.0 ** -0.5
            else:
                dst, bd, scl = kbT[:, c - 8, 0:N], B_kb, 1.0
            em.op("dve", lambda e: e.scalar_tensor_tensor(out=dst, in0=y2, scalar=scl, in1=crn[i][:, 0:N],
                                                          op0=ALU.mult, op1=ALU.mult),
                  reads=[B_cy[i], B_crn[i]], writes=[bd])

        def gated_norm(N, hT, B_hT):
            for h in range(8):
                if h % 4 == 0:
                    wz, bwz = wload("in", 0, 8, C_Z + (h // 4) * 512, 512)
                pz_, bpz_ = pget()
                mm_fm(hT, B_hT, wz, bwz, 8, (h % 4) * 128, 128, N, pz_, bpz_)
                em.op("act", lambda e: e.activation(out=ztmp[:, 0:N], in_=pz_[:, 0:N], func=AF.Silu),
                      reads=[bpz_], writes=[B_zt])
                i = 0
                em.op("act", lambda e: e.activation(out=csq[i][:, 0:N], in_=oTr[:, h, 0:N], func=AF.Square),
                      reads=[B_oTr], writes=[B_csq[i]])
                pss, bpss = pget()
                em.op("pe", lambda e: e.matmul(pss[:, 0:N], lhsT=ones_b, rhs=csq[i][:, 0:N], start=True, stop=True),
                      reads=[B_cst, B_csq[i]], writes=[bpss])
                em.op("act", lambda e: e.activation(out=crn[i][:, 0:N], in_=pss[:, 0:N], func=AF.Sqrt, bias=epsc[:, 0:1],
                                                    scale=1.0 / 128), reads=[bpss, B_cst], writes=[B_crn[i]])
                em.op("dve", lambda e: e.reciprocal(out=crn[i][:, 0:N], in_=crn[i][:, 0:N]), reads=[B_crn[i]], writes=[B_crn[i]])
                em.op("dve", lambda e: e.tensor_tensor(out=crn[i][:, 0:N], in0=crn[i][:, 0:N], in1=oTr[:, h, 0:N], op=ALU.mult),
                      reads=[B_crn[i], B_oTr], writes=[B_crn[i]])
                em.op("dve", lambda e: e.scalar_tensor_tensor(out=obT[:, h, 0:N], in0=crn[i][:, 0:N], scalar=gnorm[:, 0:1],
                                                              in1=ztmp[:, 0:N], op0=ALU.mult, op1=ALU.mult),
                      reads=[B_crn[i], B_cst, B_zt], writes=[B_obT])

        acnt = {"v": 0, "s": 0, "p": 0}

        def attn_block(g, nq, qcols, tiles, first_group, scat):
            ptl = []
            for (nk, kTf, bk, vrows, bias, mcol) in tiles:
                vi = acnt["v"] % 2; acnt["v"] += 1
                em.dma("sp", vt[vi][0:nk, :], vrows, reads=(), writes=[B_vt[vi]])
                pS, bpS = pget()
                for h in range(4):
                    em.op("pe", lambda e: e.matmul(pS[0:nk, h * nq:(h + 1) * nq], lhsT=kTf(h), rhs=qcols(h),
                                                   start=True, stop=True), reads=[bk, B_QT], writes=[bpS])
                si = acnt["s"] % 2; acnt["s"] += 1
                em.op("dve", lambda e: e.tensor_tensor(out=scs[si][0:nk, 0:4 * nq].rearrange("p (h q) -> p h q", q=nq),
                                                       in0=pS[0:nk, 0:4 * nq].rearrange("p (h q) -> p h q", q=nq),
                                                       in1=bias, op=ALU.add),
                      reads=[bpS, B_cst], writes=[B_scs[si]])
                pi = acnt["p"] % 2; acnt["p"] += 1
                em.op("act", lambda e: e.activation(out=pts[pi][0:nk, 0:4 * nq], in_=scs[si][0:nk, 0:4 * nq], func=AF.Exp,
                                                    bias=mcol), reads=[B_scs[si], B_cst], writes=[B_pts[pi]])
                ptl.append((nk, vi, pi))
            pN, bpN = pget()
            pD, bpD = pget()
            first = True
            for ti, (nk, vi, pi) in enumerate(ptl):
                lastt = ti == len(ptl) - 1
                for h in range(4):
                    em.op("pe", lambda e: e.matmul(pN[:, h * nq:(h + 1) * nq], lhsT=vt[vi][0:nk, h * 128:(h + 1) * 128],
                                                   rhs=pts[pi][0:nk, h * nq:(h + 1) * nq], start=(first and h == 0),
                                                   stop=(lastt and h == 3)), reads=[B_vt[vi], B_pts[pi]], writes=[bpN])
                em.op("pe", lambda e: e.matmul(pD[:, 0:4 * nq], lhsT=ones_b[0:nk, :], rhs=pts[pi][0:nk, 0:4 * nq],
                                               start=first, stop=lastt), reads=[B_cst, B_pts[pi]], writes=[bpD])
                first = False
            srcN = pN[:, 0:4 * nq].rearrange("p (h q) -> p h q", q=nq)
            srcD = pD[:, 0:4 * nq].rearrange("p (h q) -> p h q", q=nq)
            if first_group:
                em.op("act", lambda e: e.activation(out=scat(numT), in_=srcN, func=AF.Copy), reads=[bpN], writes=[B_nd])
                em.op("dve", lambda e: e.tensor_copy(out=scat(denT), in_=srcD), reads=[bpD], writes=[B_nd])
            else:
                em.op("dve", lambda e: e.tensor_tensor(out=scat(numT), in0=srcN, in1=scat(numT), op=ALU.add),
                      reads=[bpN, B_nd], writes=[B_nd])
                em.op("dve", lambda e: e.tensor_tensor(out=scat(denT), in0=srcD, in1=scat(denT), op=ALU.add),
                      reads=[bpD, B_nd], writes=[B_nd])

        def q_group(g, hT, B_hT, N, dst, B_dst):
            wq, bwq = wload("in", 0, 8, C_QA + g * 512, 512)
            for j in range(4):
                pq_, bpq_ = pget()
                mm_fm(hT, B_hT, wq, bwq, 8, j * 128, 128, N, pq_, bpq_)
                copy_op(ev_eng(), dst(j), pq_[:, 0:N], [bpq_], [B_dst], scale=128.0 ** -0.5)

        def attention_st(s, hT, B_hT):
            T0 = (NPRE + s) * ST
            cur = (NPRE + s) % 2
            prv = 1 - cur
            zc = pmask[:, 4:5]
            em.finish("sp", B_stb)
            q_group(0, hT, B_hT, ST, lambda j: QT[:, j, :], B_QT)
            for qb in range(4):
                if qb == 0:
                    kprev = lambda h: KT01[:, h, prv * ST + 384: prv * ST + 512]; bkp = B_KT01[prv]
                    mc = pmask[:, 0:1] if s == 0 else zc
                else:
                    kprev = (lambda qb_: (lambda h: KT01[:, h, cur * ST + (qb_ - 1) * 128: cur * ST + qb_ * 128]))(qb); bkp = B_KT01[cur]
                    mc = zc
                kcur = (lambda qb_: (lambda h: KT01[:, h, cur * ST + qb_ * 128: cur * ST + (qb_ + 1) * 128]))(qb)
                tl = [(128, kprev, bkp, vscr[T0 + qb * 128 - 128:T0 + qb * 128, 0:512], ab01[:, 0:4, 0, :], mc),
                      (128, kcur, B_KT01[cur], vscr[T0 + qb * 128:T0 + qb * 128 + 128, 0:512], ab01[:, 0:4, 1, :], zc)]
                attn_block(0, 128, (lambda qb_: (lambda h: QT[:, h, qb_ * 128:(qb_ + 1) * 128]))(qb), tl, True,
                           (lambda qb_: (lambda tns: tns[:, :, qb_ * 128:(qb_ + 1) * 128]))(qb))
            q_group(1, hT, B_hT, ST, lambda j: QT[:, j, :], B_QT)
            for r in range(4):
                kprev = (lambda r_: (lambda h: KT01[:, 4 + h, sl(prv * ST + r_, 128, 4)]))(r)
                kcur = (lambda r_: (lambda h: KT01[:, 4 + h, sl(cur * ST + r_, 128, 4)]))(r)
                mc = pmask[:, 0:1] if s == 0 else zc
                tl = [(128, kprev, B_KT01[prv], vscr[sl(T0 - ST + r, 128, 4), 512:1024], ab01[:, 4:8, 0, :], mc),
                      (128, kcur, B_KT01[cur], vscr[sl(T0 + r, 128, 4), 512:1024], ab01[:, 4:8, 1, :], zc)]
                attn_block(1, 128, (lambda r_: (lambda h: QT[:, h, sl(r_, 128, 4)]))(r), tl, False,
                           (lambda r_: (lambda tns: tns[:, :, sl(r_, 128, 4)]))(r))
            q_group(2, hT, B_hT, ST, lambda j: QT[:, j, :], B_QT)
            for r in range(16):
                ka = (lambda r_: (lambda h: KT2[:, h, sl(T0 - 2048 + r_, 128, 16)]))(r)
                kb_ = (lambda r_: (lambda h: KT2[:, h, sl(T0 + r_, 32, 16)]))(r)
                mc = pmask[:, s:s + 1]
                tl = [(128, ka, B_KT2, vscr[sl(T0 - 2048 + r, 128, 16), 1024:1536], ab2a[:, :, :], mc),
                      (32, kb_, B_KT2, vscr[sl(T0 + r, 32, 16), 1024:1536], ab2b[:, :, :], pmask[0:32, 4:5])]
                attn_block(2, 32, (lambda r_: (lambda h: QT[:, h, sl(r_, 32, 16)]))(r), tl, False,
                           (lambda r_: (lambda tns: tns[:, :, sl(r_, 32, 16)]))(r))
            em.op("dve", lambda e: e.reciprocal(out=denT[:], in_=denT[:]), reads=[B_nd], writes=[B_nd])
            em.op("dve", lambda e: e.tensor_tensor(out=oaT[:], in0=numT[:], in1=denT[:], op=ALU.mult),
                  reads=[B_nd], writes=[B_oaT])

        sgm = big[:, 8:12, :]; B_sgm = B_big
        mrg = big[:, 0:8, :].rearrange("p k n -> p (k n)").rearrange("p (t d) -> p t d", d=D); B_mrg = B_big

        def post_mixer(tiles, npart, hT, B_hT, mT, B_mT, fT, B_fT, ydst):
            N = npart * len(tiles)
            nt = len(tiles)
            for half in range(2):
                for br in range(2):
                    wg, bwg = wload("in", 0, 8, C_G + br * D + half * 512, 512)
                    for t in range(nt):
                        pg_, bpg_ = pget()
                        mm_tm(hT, B_hT, t * npart, npart, wg, bwg, 8, 0, 512, pg_, bpg_)
                        em.op("act", lambda e: e.activation(out=sgm[0:npart, t, :], in_=pg_[0:npart, :], func=AF.Sigmoid),
                              reads=[bpg_], writes=[B_sgm])
                    if br == 0:
                        wp, bwp = wload("pa", 0, 4, half * 512, 512)
                        src, bsrc, nk = oaT, B_oaT, 4
                    else:
                        wp, bwp = wload("pb", 0, 8, half * 512, 512)
                        src, bsrc, nk = obT, B_obT, 8
                    for t in range(nt):
                        pp_, bpp_ = pget()
                        mm_tm(src, bsrc, t * npart, npart, wp, bwp, nk, 0, 512, pp_, bpp_)
                        mo = mrg[0:npart, t, half * 512:(half + 1) * 512]
                        if br == 0:
                            em.op("dve", lambda e: e.tensor_tensor(out=mo, in0=pp_[0:npart, :], in1=sgm[0:npart, t, :], op=ALU.mult),
                                  reads=[bpp_, B_sgm], writes=[B_mrg])
                        else:
                            em.op("dve", lambda e: e.tensor_tensor(out=sgm[0:npart, t, :], in0=pp_[0:npart, :], in1=sgm[0:npart, t, :],
                                                                   op=ALU.mult), reads=[bpp_, B_sgm], writes=[B_sgm])
                            em.op("pool", lambda e: e.tensor_tensor(out=mo, in0=mo, in1=sgm[0:npart, t, :], op=ALU.add),
                                  reads=[B_sgm, B_mrg], writes=[B_mrg])
            for t in range(nt):
                em.op("act", lambda e: e.activation(out=xn_tmp[0:npart, :], in_=mrg[0:npart, t, :], func=AF.Copy),
                      reads=[B_mrg], writes=[B_xn])
                pt, bpt = pgetb()
                for kc in range(8):
                    em.op("pe", lambda e: e.transpose(out=pt[:, kc * 128:kc * 128 + npart], in_=xn_tmp[0:npart, kc * 128:(kc + 1) * 128],
                                                      identity=ident_b[0:npart, 0:npart]), reads=[B_xn, B_cst], writes=[bpt])
                copy_op(ev_eng(), mT[:, :, t * npart:(t + 1) * npart],
                        pt[:, :].rearrange("p (k c) -> p k c", c=128)[:, :, 0:npart], [bpt], [B_mT])
            for half in range(2):
                wo, bwo = wload("out", 0, 8, half * 512, 512)
                for t, (x_ap, xb) in enumerate(tiles):
                    po, bpo = pget()
                    mm_tm(mT, B_mT, t * npart, npart, wo, bwo, 8, 0, 512, po, bpo)
                    xo = x_ap[:, half * 512:(half + 1) * 512]
                    em.op("dve", lambda e: e.tensor_tensor(out=xo, in0=po[0:npart, :], in1=xo, op=ALU.add),
                          reads=[bpo, xb], writes=[xb])
            ffn(tiles, npart, 2, "gu2", "d2", fT, B_fT)
            for t, (x_ap, xb) in enumerate(tiles):
                rms_rstd(x_ap, xb, npart, 1)
                em.op("dve", lambda e: e.scalar_tensor_tensor(out=x_ap, in0=x_ap, scalar=sm[0:npart, 1:2], in1=normout[0:npart, :],
                                                              op0=ALU.mult, op1=ALU.mult), reads=[xb, B_sm, B_cst], writes=[xb])
                out_dma(ydst(t), x_ap, [xb])

        n_st = NPRE + NMAIN if STAGE >= 2 else 1
        import os
        if os.environ.get('DBG_NST'):
            n_st = int(os.environ['DBG_NST'])
        for st in range(n_st):
            main = st >= NPRE and not os.environ.get('DBG_NOMAIN')
            T0 = st * ST
            tiles = [(xt[:, t, :], B_xt[t]) for t in range(4)]
            for t in range(4):
                em.dma("sp", xt[:, t, :], xseq[T0 + t * 128:T0 + (t + 1) * 128, :], writes=[B_xt[t]])
            ffn(tiles, 128, 0, "gu1", "d1", actT[0], B_actT[0])
            hT, B_hT = actT[1], B_actT[1]
            norm_T(tiles, 128, 1, hT, B_hT)
            cur = st % 2
            for hg in range(3):
                wk, bwk = wload("in", 0, 8, C_KA + hg * 512, 512)
                for j in range(4):
                    pk_, bpk_ = pget()
                    mm_fm(hT, B_hT, wk, bwk, 8, j * 128, 128, ST, pk_, bpk_)
                    hd = hg * 4 + j
                    if hd < 8:
                        copy_op(ev_eng(), KT01[:, hd, cur * ST:(cur + 1) * ST], pk_[:, :], [bpk_], [B_KT01[cur]])
                    else:
                        copy_op(ev_eng(), KT2[:, hd - 8, T0:T0 + ST], pk_[:, :], [bpk_], [B_KT2])
            for g in range(3):
                need_k = main and ((g == 2) or (g == 1 and st == n_st - 1) or (g == 0 and st == n_st - 1))
                if need_k:
                    wk, bwk = wload("in", 0, 8, C_KA + g * 512, 512)
                wv, bwv = wload("in", 0, 8, C_VA + g * 512, 512)
                for t in range(4):
                    mrow = (st - NPRE) * ST + t * 128
                    if g == 2:
                        dk_, ok = kv2_out, main
                        r0 = mrow
                    elif g == 1:
                        dk_, ok = kv1_out, main and st == n_st - 1
                        r0 = t * 128
                    else:
                        dk_, ok = kv0_out, main and st == n_st - 1 and t == 3
                        r0 = 0
                    if need_k and ok:
                        pk_, bpk_ = pget()
                        mm_tm(hT, B_hT, t * 128, 128, wk, bwk, 8, 0, 512, pk_, bpk_)
                        i = cnt["st"] % 2; cnt["st"] += 1
                        copy_op("act", stage_f[i][:, :], pk_[:, :], [bpk_], [B_stf[i]])
                        out_dma(dk_[r0:r0 + 128, 0, :], stage_f[i][:, :], [B_stf[i]])
                    pv_, bpv_ = pget()
                    mm_tm(hT, B_hT, t * 128, 128, wv, bwv, 8, 0, 512, pv_, bpv_)
                    i = cnt["sb"] % 2; cnt["sb"] += 1
                    copy_op("dve", stage_b[i][:, :], pv_[:, :], [bpv_], [B_stb[i]])
                    em.dma("pool", vscr[T0 + t * 128:T0 + (t + 1) * 128, g * 512:(g + 1) * 512], stage_b[i][:, :],
                           reads=[B_stb[i]], writes=(), owner=B_stb[i])
                    if ok:
                        i = cnt["st"] % 2; cnt["st"] += 1
                        copy_op("act", stage_f[i][:, :], pv_[:, :], [bpv_], [B_stf[i]])
                        out_dma(dk_[r0:r0 + 128, 1, :], stage_f[i][:, :], [B_stf[i]])
            wb_, bwb_ = wload("in", 0, 8, C_B, 16)
            for t in range(4):
                pb_, bpb_ = pget()
                mm_tm(hT, B_hT, t * 128, 128, wb_, bwb_, 8, 0, 16, pb_, bpb_)
                copy_op("act", ba_all[:, t, :], pb_[:, 0:16], [bpb_], [B_ba])
            beta_g(ba_all[:, :, :], beta_all[:, :, :], g_all[:, :, :], 128, [B_ba])
            for cg in range(6):
                wc, bwc = wload("in", 0, 8, C_QKV + cg * 512, 512)
                for j in range(4):
                    c = cg * 4 + j
                    if st < NPRE - 1 and c < 8:
                        continue
                    pc_, bpc_ = pget()
                    mm_fm(hT, B_hT, wc, bwc, 8, j * 128, 128, ST, pc_, bpc_)
                    conv_chunk(c, pc_, bpc_, ST)
            if STAGE >= 3:
                for t in range(4):
                    cs = slice(t * 128, (t + 1) * 128)
                    gdn_tile(128,
                             (lambda cs_: (lambda h: qbT[:, h, cs_]))(cs), (lambda cs_: (lambda h: kbT[:, h, cs_]))(cs),
                             (lambda cs_: (lambda h: vbT[:, h, cs_]))(cs), B_qb, B_kb, B_vb,
                             beta_all[:, t, :], g_all[:, t, :], B_bg,
                             lambda h: S_f[:, h, :], lambda h: S_b[:, h, :], B_S, main,
                             (lambda cs_: (lambda h: oTr[:, h, cs_]))(cs), B_oTr)
            if main and STAGE >= 4:
                gated_norm(ST, hT, B_hT)
                attention_st(st - NPRE, hT, B_hT)
            if main and STAGE >= 5:
                post_mixer(tiles, 128, hT, B_hT, actT[0], B_actT[0], actT[1], B_actT[1],
                           lambda t: y_out[(st - NPRE) * ST + t * 128:(st - NPRE) * ST + (t + 1) * 128, :])
        if STAGE >= 3:
            for h in range(8):
                out_dma(ssmp_out[h], S_f[:, h, :], [B_S[h]])
        for c in range(24):
            em.dma("sp", convp_out[:, c * 128:(c + 1) * 128].rearrange("t p -> p t"), halo[:, c, :], reads=[B_halo], writes=(),
                   owner=B_halo, allow_slow_non_contiguous=True)

        em.barrier()
        pstack.close()
        scb = sb("scb", [128, 24, NB_S, 7]); B_scb = Buf("scb")
        sample_branch = STAGE >= 6
        if sample_branch:
            xs_t = xt[0:NTS, 0, :]; B_xs = B_xt[0]
            stiles = [(xs_t, B_xs)]
            em.dma("sp", xs_t, xs_in, writes=[B_xs])
            sc_tm = sb("sc_tm", [48, 3072]); B_sct = Buf("sc_tm")
            em.dma("sp", sc_tm[:, :], sconv, writes=[B_sct])
            for c in range(24):
                pt_, bpt_ = pget()
                em.op("pe", lambda e: e.transpose(out=pt_[:, 0:48], in_=sc_tm[:, c * 128:(c + 1) * 128], identity=ident_f[0:48, 0:48]),
                      reads=[B_sct, B_cst], writes=[bpt_])
                copy_op(ev_eng(), scb[:, c, :, 0:3], pt_[:, 0:48].rearrange("p (b s) -> p b s", s=3), [bpt_], [B_scb])
            ffn(stiles, NTS, 0, "gu1", "d1", actT[0], B_actT[0])
            hT, B_hT = actT[1], B_actT[1]
            norm_T(stiles, NTS, 1, hT, B_hT)
            N = NTS
            QTs = sb("QsT", [128, 12, NTS], BF16)
            KsT = sb("KsT", [128, 12, NTS], BF16); B_KsT = Buf("KsT")
            for hg in range(3):
                wq, bwq = wload("in", 0, 8, C_QA + hg * 512, 512)
                for j in range(4):
                    pq_, bpq_ = pget()
                    mm_fm(hT, B_hT, wq, bwq, 8, j * 128, 128, N, pq_, bpq_)
                    copy_op(ev_eng(), QTs[:, hg * 4 + j, 0:N], pq_[:, 0:N], [bpq_], [B_QT], scale=128.0 ** -0.5)
            for hg in range(3):
                wk, bwk = wload("in", 0, 8, C_KA + hg * 512, 512)
                for j in range(4):
                    pk_, bpk_ = pget()
                    mm_fm(hT, B_hT, wk, bwk, 8, j * 128, 128, N, pk_, bpk_)
                    copy_op(ev_eng(), KsT[:, hg * 4 + j, 0:N], pk_[:, 0:N], [bpk_], [B_KsT])
            Vs_tm = sb("Vs_tm", [NTS, 1536]); B_Vs = Buf("Vs_tm")
            for g in range(3):
                wk, bwk = wload("in", 0, 8, C_KA + g * 512, 512)
                pk_, bpk_ = pget()
                mm_tm(hT, B_hT, 0, N, wk, bwk, 8, 0, 512, pk_, bpk_)
                i = cnt["st"] % 2; cnt["st"] += 1
                copy_op("act", stage_f[i][0:N, :], pk_[0:N, :], [bpk_], [B_stf[i]])
                out_dma(kvs_out[g][:, 0, :], stage_f[i][0:N, :], [B_stf[i]])
                wv, bwv = wload("in", 0, 8, C_VA + g * 512, 512)
                pv_, bpv_ = pget()
                mm_tm(hT, B_hT, 0, N, wv, bwv, 8, 0, 512, pv_, bpv_)
                copy_op("dve", Vs_tm[:, g * 512:(g + 1) * 512], pv_[0:N, :], [bpv_], [B_Vs])
                out_dma(kvs_out[g][:, 1, :], Vs_tm[:, g * 512:(g + 1) * 512], [B_Vs])
            for cg in range(6):
                wc, bwc = wload("in", 0, 8, C_QKV + cg * 512, 512)
                for j in range(4):
                    c = cg * 4 + j
                    pc_, bpc_ = pget()
                    mm_fm(hT, B_hT, wc, bwc, 8, j * 128, 128, N, pc_, bpc_)
                    conv_chunk(c, pc_, bpc_, N, sample=True)
            for c in range(24):
                for s_ in range(3):
                    em.dma("sp", convs_out[:, s_, c * 128:(c + 1) * 128].rearrange("b p -> p b"), scb[:, c, :, 4 + s_],
                           reads=[B_scb], writes=(), owner=B_scb, allow_slow_non_contiguous=True)
            if STAGE >= 7:
                wb_, bwb_ = wload("in", 0, 8, C_B, 16)
                Ss_f = [sb("Ss_f%d" % i, [128, 8, 128]) for i in range(2)]
                Ss_b = Ss_f
                B_Ss = [[Buf("Ss%d_%d" % (i, h)) for h in range(8)] for i in range(2)]
                ba_s = [sb("ba_s%d" % i, [4, 1, 16]) for i in range(2)]; B_bas = [Buf("bas0"), Buf("bas1")]
                bt_s = [sb("bt_s%d" % i, [4, 1, 8]) for i in range(2)]
                g_s = [sb("g_s%d" % i, [4, 1, 8]) for i in range(2)]
                make_hb(3)
                make_hb(4)
                for b in range(NB_S):
                    i = b % 2
                    for h in range(8):
                        em.dma("sp", Ss_f[i][:, h, :], sssm[b, h], writes=[B_Ss[i][h]])
                    pb_, bpb_ = pget()
                    mm_tm(hT, B_hT, 4 * b, 4, wb_, bwb_, 8, 0, 16, pb_, bpb_)
                    copy_op("act", ba_s[i][:, 0, :], pb_[0:4, 0:16], [bpb_], [B_bas[i]])
                    beta_g(ba_s[i][:, :, :], bt_s[i][:, :, :], g_s[i][:, :, :], 4, [B_bas[i]])
                    cs = slice(4 * b, 4 * b + 4)
                    gdn_tile(4,
                             (lambda cs_: (lambda h: qbT[:, h, cs_]))(cs), (lambda cs_: (lambda h: kbT[:, h, cs_]))(cs),
                             (lambda cs_: (lambda h: vbT[:, h, cs_]))(cs), B_qb, B_kb, B_vb,
                             bt_s[i][:, 0, :], g_s[i][:, 0, :], B_bg,
                             (lambda i_: (lambda h: Ss_f[i_][:, h, :]))(i), (lambda i_: (lambda h: Ss_b[i_][:, h, :]))(i), B_Ss[i], True,
                             (lambda cs_: (lambda h: oTr[:, h, cs_]))(cs), B_oTr)
                    for h in range(8):
                        out_dma(ssms_out[b, h], Ss_f[i][:, h, :], [B_Ss[i][h]])
                gated_norm(N, hT, B_hT)
            if STAGE >= 8:
                ckt = [sb("ckt%d" % i, [128, 1024]) for i in range(2)]; B_ckt = [Buf("ckt%d" % i) for i in range(2)]
                kTs = [sb("kTs%d" % i, [128, 4, 128], BF16) for i in range(2)]; B_kTs = [Buf("kTs0"), Buf("kTs1")]
                vnew = [sb("vnew0", [4, 1536])] * 2; B_vnew = [Buf("vnew0")] * 2
                pTs = [sb("pTs%d" % i, [128, 16]) for i in range(4)]; B_pTs = [Buf("pTs%d" % i) for i in range(4)]
                scc = {"c": 0, "k": 0, "p": 0}
                ones4 = ones_f
                for b in range(NB_S):
                    vi = b % 2
                    em.dma("sp", vnew[vi][:, :], Vs_tm[4 * b:4 * b + 4, :], reads=[B_Vs], writes=[B_vnew[vi]])
                    pN, bpN = pget(hold=True)
                    pD, bpD = pget(hold=True)
                    first = True
                    tl = []
                    tl.append((0, 0, ck0[b, :, :]))
                    for r in range(4):
                        tl.append((1, 1 + r, ck1[b, sl(r, 128, 4), :]))
                    for r in range(4):
                        tl.append((2, 5 + r, ck2[b, sl(r, 128, 16), :]))
                    for (g, bi, src) in tl:
                        ci = scc["c"] % 2; scc["c"] += 1
                        em.dma("sp", ckt[ci][:, :], src, writes=[B_ckt[ci]])
                        ki = scc["k"] % 2; scc["k"] += 1
                        for h in range(4):
                            ptr, bptr = pget()
                            em.op("pe", lambda e: e.transpose(out=ptr[:, 0:128], in_=ckt[ci][:, h * 128:(h + 1) * 128], identity=ident_f),
                                  reads=[B_ckt[ci], B_cst], writes=[bptr])
                            copy_op(ev_eng(), kTs[ki][:, h, :], ptr[:, 0:128], [bptr], [B_kTs[ki]])
                        pS, bpS = pget()
                        for h in range(4):
                            em.op("pe", lambda e: e.matmul(pS[:, h * 4:(h + 1) * 4], lhsT=kTs[ki][:, h, :], rhs=QTs[:, g * 4 + h, 4 * b:4 * b + 4],
                                                           start=True, stop=True), reads=[B_kTs[ki], B_QT], writes=[bpS])
                        pi = scc["p"] % 4; scc["p"] += 1
                        em.op("dve", lambda e: e.tensor_tensor(out=pTs[pi][:, :], in0=pS[:, 0:16], in1=sbias[:, bi, :], op=ALU.add),
                              reads=[bpS, B_cst], writes=[B_pTs[pi]])
                        em.op("act", lambda e: e.activation(out=pTs[pi][:, :], in_=pTs[pi][:, :], func=AF.Exp),
                              reads=[B_pTs[pi]], writes=[B_pTs[pi]])
                        for h in range(4):
                            em.op("pe", lambda e: e.matmul(pN[:, h * 4:(h + 1) * 4], lhsT=ckt[ci][:, 512 + h * 128:512 + (h + 1) * 128],
                                                           rhs=pTs[pi][:, h * 4:(h + 1) * 4], start=(first and h == 0), stop=False),
                                  reads=[B_ckt[ci], B_pTs[pi]], writes=[bpN])
                        em.op("pe", lambda e: e.matmul(pD[:, 0:16], lhsT=ones_f, rhs=pTs[pi][:, :], start=first, stop=False),
                              reads=[B_cst, B_pTs[pi]], writes=[bpD])
                        first = False
                    for g in range(3):
                        pS, bpS = pget()
                        for h in range(4):
                            em.op("pe", lambda e: e.matmul(pS[0:4, h * 4:(h + 1) * 4], lhsT=KsT[:, g * 4 + h, 4 * b:4 * b + 4],
                                                           rhs=QTs[:, g * 4 + h, 4 * b:4 * b + 4], start=True, stop=True),
                                  reads=[B_KsT, B_QT], writes=[bpS])
                        pi = scc["p"] % 4; scc["p"] += 1
                        em.op("dve", lambda e: e.tensor_tensor(out=pTs[pi][0:4, :], in0=pS[0:4, 0:16], in1=sbiasn[:, g, :], op=ALU.add),
                              reads=[bpS, B_cst], writes=[B_pTs[pi]])
                        em.op("act", lambda e: e.activation(out=pTs[pi][0:4, :], in_=pTs[pi][0:4, :], func=AF.Exp),
                              reads=[B_pTs[pi]], writes=[B_pTs[pi]])
                        for h in range(4):
                            em.op("pe", lambda e: e.matmul(pN[:, h * 4:(h + 1) * 4], lhsT=vnew[vi][:, g * 512 + h * 128:g * 512 + (h + 1) * 128],
                                                           rhs=pTs[pi][0:4, h * 4:(h + 1) * 4], start=False, stop=(g == 2 and h == 3)),
                                  reads=[B_vnew[vi], B_pTs[pi]], writes=[bpN])
                        em.op("pe", lambda e: e.matmul(pD[:, 0:16], lhsT=ones_f[0:4, :], rhs=pTs[pi][0:4, :], start=False, stop=(g == 2)),
                              reads=[B_cst, B_pTs[pi]], writes=[bpD])
                    dsl = denT[:, :, 4 * b:4 * b + 4]
                    em.op("dve", lambda e: e.reciprocal(out=dsl, in_=pD[:, 0:16].rearrange("p (h s) -> p h s", s=4)),
                          reads=[bpD], writes=[B_nd])
                    em.op("dve", lambda e: e.tensor_tensor(out=oaT[:, :, 4 * b:4 * b + 4], in0=pN[:, 0:16].rearrange("p (h s) -> p h s", s=4),
                                                           in1=dsl, op=ALU.mult), reads=[bpN, B_nd], writes=[B_oaT])
                    prelease(bpN); prelease(bpD)
            if STAGE >= 9:
                post_mixer(stiles, NTS, hT, B_hT, actT[0], B_actT[0], actT[1], B_actT[1], lambda t: ys_out[:, :])

        allb = [B_scb] + B_xt + B_stf + B_stb
        if sample_branch and STAGE >= 7:
            allb += B_Ss[0] + B_Ss[1]
        if sample_branch:
            allb += [B_Vs]
        em.finish("sp", allb)
    print("[kernel] instructions emitted:", em.ninst, "dma sems:", em.n_dsem)
    return nc


def _slopes():
    return np.exp2(-8.0 * np.arange(1, 13, dtype=np.float64) / 12).astype(np.float64)


def _const_tables():
    sl_ = _slopes()
    cst = np.zeros((128, 5, 128), np.float32)
    i = np.arange(128)
    cst[:, 0, :] = np.eye(128)
    cst[:, 1, :] = 1.0
    cst[:, 2, :] = (i[:, None] <= i[None, :])
    cst[:, 3, :] = np.where(i[:, None] > i[None, :], 0.0, NEG)
    cst[:, 4, :] = np.where(i[None, :] >= i[:, None], 0.0, NEG)
    ab01 = np.zeros((128, 8, 2, 128), np.float32)
    k = i[:, None]; q = i[None, :]
    for h in range(8):
        dil = 1 if h < 4 else 4
        s = sl_[h]
        dprev = 128 + q - k
        ab01[:, h, 0, :] = np.where(dprev <= 128, -s * dil * dprev, NEG)
        dcur = q - k
        ab01[:, h, 1, :] = np.where(dcur >= 0, -s * dil * dcur, NEG)
    ab2a = np.zeros((128, 4, 32), np.float32)
    ab2b = np.zeros((32, 4, 32), np.float32)
    a = np.arange(128)[:, None]; qi = np.arange(32)[None, :]; c = np.arange(32)[:, None]
    for h in range(4):
        s = sl_[8 + h]
        d = 128 + qi - a
        ab2a[:, h, :] = np.where(d <= 128, -s * 16 * d, NEG)
        d2 = qi - c
        ab2b[:, h, :] = np.where(d2 >= 0, -s * 16 * d2, NEG)
    sbias = np.full((128, 9, 16), NEG, np.float32)
    p = np.arange(128)
    for h in range(4):
        for s_ in range(4):
            d = 128 + s_ - p
            sbias[:, 0, h * 4 + s_] = np.where(p >= s_, -sl_[h] * d, NEG)
            sbias[:, 1 + s_, h * 4 + s_] = -sl_[4 + h] * 4 * (128 - p)
            sbias[:, 5 + s_, h * 4 + s_] = -sl_[8 + h] * 16 * (128 - p)
    sbn = np.full((4, 3, 16), NEG, np.float32)
    for h in range(4):
        for s_ in range(4):
            for sp in range(4):
                if sp <= s_:
                    sbn[sp, 0, h * 4 + s_] = -sl_[h] * (s_ - sp)
                if sp == s_:
                    sbn[sp, 1, h * 4 + s_] = 0.0
                    sbn[sp, 2, h * 4 + s_] = 0.0
    return cst, ab01, ab2a, ab2b, sbias, sbn


_NC_CACHE = {}


def kernel(x_prompt, x_sample, cache_kv_w128, cache_kv_w512, cache_kv_w2048, state_conv, state_ssm,
           norm_ffn1, w_ffn1_gu, w_ffn1_down, norm_mix, w_in, conv_w, gdn_a_log, gdn_dt_bias, gdn_norm,
           w_proj_a, w_proj_b, w_out, norm_ffn2, w_ffn2_gu, w_ffn2_down, norm_out):
    f = lambda a: np.ascontiguousarray(np.asarray(a, dtype=np.float32))
    x_prompt = f(x_prompt); x_sample = f(x_sample)
    if "nc" not in _NC_CACHE:
        _NC_CACHE["nc"] = build_program()
    nc = _NC_CACHE["nc"]
    cst, ab01, ab2a, ab2b, sbias, sbn = _const_tables()
    normT = np.stack([f(norm_ffn1)[0], f(norm_mix)[0], f(norm_ffn2)[0]], 0).reshape(3, 8, 128).transpose(2, 0, 1)
    normout = np.broadcast_to(f(norm_out)[None, :], (128, D))
    convw = f(conv_w)[0].reshape(4, 24, 128).transpose(2, 1, 0)
    gnorm = f(gdn_norm)[0].reshape(128, 1)
    alog = np.broadcast_to(f(gdn_a_log)[0][None, :], (128, 8))
    dtb = np.broadcast_to(f(gdn_dt_bias)[0][None, :], (128, 8))
    shared = {
        "w_gu1": f(w_ffn1_gu)[0], "w_d1": f(w_ffn1_down)[0], "w_in": f(w_in)[0], "w_pa": f(w_proj_a)[0],
        "w_pb": f(w_proj_b)[0], "w_out": f(w_out)[0], "w_gu2": f(w_ffn2_gu)[0], "w_d2": f(w_ffn2_down)[0],
        "normT": f(normT), "normout": f(normout), "convw": f(convw), "gnorm": f(gnorm), "alog": f(alog), "dtb": f(dtb),
        "cst_f": cst, "ab01": ab01, "ab2a": ab2a, "ab2b": ab2b, "sbias": sbias, "sbiasn": sbn,
    }
    ck0 = f(cache_kv_w128)[0].reshape(128, 128, 1024)
    ck1 = f(cache_kv_w512)[0].reshape(128, 512, 1024)
    ck2 = f(cache_kv_w2048)[0].reshape(128, 2048, 1024)
    sconv = f(state_conv)[0]
    sssm = f(state_ssm)[0]
    in_maps = []
    for c in range(8):
        b, hf = c // 2, c % 2
        if hf == 0:
            xseq = np.concatenate([np.zeros((2048, D), np.float32), x_prompt[b, 0:2048]], 0)
        else:
            xseq = x_prompt[b]
        pm = np.zeros((128, 8), np.float32)
        if hf == 0:
            a = np.arange(128)
            pm[:, 0] = NEG
            pm[:, 1] = np.where(a < 96, NEG, 0.0)
            pm[:, 2] = np.where(a < 64, NEG, 0.0)
            pm[:, 3] = np.where(a < 32, NEG, 0.0)
        m = dict(shared)
        m.update({
            "xseq": f(xseq), "xs": f(x_sample[16 * c:16 * c + 16].reshape(64, D)),
            "ck0": f(ck0[16 * c:16 * c + 16]), "ck1": f(ck1[16 * c:16 * c + 16]), "ck2": f(ck2[16 * c:16 * c + 16, 0:(64 if os.environ.get('DBG_SMALL') else 2048)]),
            "sconv": f(sconv[16 * c:16 * c + 16].reshape(48, 3072)), "sssm": f(sssm[16 * c:16 * c + 16]),
            "pmask": pm,
        })
        in_maps.append(m)
    res = run_bass_kernel_spmd(nc, in_maps, core_ids=list(range(8)))
    R = res.results
    y_prompt = np.zeros((4, 4096, D), np.float32)
    for c in range(8):
        b, hf = c // 2, c % 2
        y_prompt[b, hf * 2048:(hf + 1) * 2048] = R[c]["y"]
    y_sample = np.concatenate([R[c]["ys"].reshape(16, 4, D) for c in range(8)], 0)
    kvp = []
    for nm, W in (("kv0", 128), ("kv1", 512), ("kv2", 2048)):
        kvp.append(np.stack([R[2 * b + 1][nm].reshape(W, 2, 4, 128) for b in range(4)], 0)[None])
    convp = np.stack([R[2 * b + 1]["convp"] for b in range(4)], 0)[None]
    ssmp = np.stack([R[2 * b + 1]["ssmp"] for b in range(4)], 0)[None]
    kvs = []
    for g in range(3):
        kvs.append(np.concatenate([R[c]["kvs%d" % g].reshape(16, 4, 2, 4, 128) for c in range(8)], 0)[None])
    convs = np.concatenate([R[c]["convs"] for c in range(8)], 0)[None]
    ssms = np.concatenate([R[c]["ssms"] for c in range(8)], 0)[None]
    outs = (y_prompt, y_sample, kvp[0], kvp[1], kvp[2], convp, ssmp, kvs[0], kvs[1], kvs[2], convs, ssms)
    return tuple(np.ascontiguousarray(o.astype(np.float32)) for o in outs)
```

```python
import math
import os
from contextlib import ExitStack
import numpy as np
import concourse.bass as bass
import concourse.mybir as mybir
from concourse.bass_utils import run_bass_kernel_spmd

F32 = mybir.dt.float32
BF16 = mybir.dt.bfloat16
AF = mybir.ActivationFunctionType
ALU = mybir.AluOpType

NEG = -30000.0
D = 1024
DFF = 2816
INW = 10768
C_QA, C_KA, C_VA, C_QKV, C_Z, C_B, C_G = 0, 1536, 3072, 4608, 7680, 8704, 8720
EPS = 1e-6
ST = 512
NPRE = 4
NMAIN = 4
SEQX = 4096
NB_S = 16
NTS = 64

STAGE = 9


def sl(start, n, step=1):
    return slice(start, start + (n - 1) * step + 1, step)


class Buf:
    __slots__ = ("name", "w", "r", "dsem", "dcnt", "excl")

    def __init__(self, name="", excl=False):
        self.excl = excl
        self.name = name
        self.w = None
        self.r = []
        self.dsem = None
        self.dcnt = 0


class Em:
    def __init__(self, nc):
        self.nc = nc
        self.engs = {"pe": nc.tensor, "act": nc.scalar, "dve": nc.vector,
                     "pool": nc.gpsimd, "sp": nc.sync}
        self.sem = {k: nc.alloc_semaphore("s_" + k) for k in self.engs}
        self.cnt = {k: 0 for k in self.engs}
        self.waited = {k: {} for k in self.engs}
        self.n_dsem = 0
        self.ninst = 0
        self.dbufs = []

    def _waits(self, e, reads, writes):
        deps = {}
        for b in reads:
            if b.w is not None:
                s, v = b.w
                if deps.get(id(s), (None, 0))[1] < v:
                    deps[id(s)] = (s, v)
        for b in writes:
            if b.w is not None:
                s, v = b.w
                if deps.get(id(s), (None, 0))[1] < v:
                    deps[id(s)] = (s, v)
            for (s, v) in b.r:
                if deps.get(id(s), (None, 0))[1] < v:
                    deps[id(s)] = (s, v)
        eng = self.engs[e]
        wd = self.waited[e]
        for key, (s, v) in deps.items():
            if e == "pe" and s is self.sem["pe"]:
                continue
            if wd.get(key, 0) >= v:
                continue
            eng.wait_ge(s, v)
            wd[key] = v
            self.ninst += 1

    def _post(self, ev, reads, writes):
        for b in reads:
            b.r.append(ev)
            if len(b.r) > 24:
                mx = {}
                for (s, v) in b.r:
                    if mx.get(id(s), (None, 0))[1] < v:
                        mx[id(s)] = (s, v)
                b.r = list(mx.values())
        for b in writes:
            b.w = ev
            b.r = []

    def op(self, e, fn, reads=(), writes=()):
        if any(b.excl for b in reads):
            writes = list(writes) + [b for b in reads if b.excl]
            reads = [b for b in reads if not b.excl]
        self._waits(e, reads, writes)
        ins = fn(self.engs[e])
        self.cnt[e] += 1
        ev = (self.sem[e], self.cnt[e])
        ins.then_inc(self.sem[e], 1)
        self.ninst += 1
        self._post(ev, reads, writes)
        return ins

    def dma(self, e, out, in_, reads=(), writes=(), owner=None, **kw):
        self._waits(e, reads, writes)
        if owner is None:
            owner = writes[0] if len(writes) else reads[0]
        if owner.dsem is None:
            owner.dsem = self.nc.alloc_semaphore("d%d" % self.n_dsem)
            self.n_dsem += 1
            self.dbufs.append(owner)
        ins = self.engs[e].dma_start(out=out, in_=in_, **kw)
        owner.dcnt += 16
        ins.then_inc(owner.dsem, 16)
        ev = (owner.dsem, owner.dcnt)
        self.ninst += 1
        self._post(ev, reads, writes)
        return ins

    def finish(self, e, bufs):
        self._waits(e, (), bufs)

    def barrier(self, dma_bufs=None):
        if dma_bufs is None:
            dma_bufs = list(self.dbufs)
        for e in ("pe", "act", "dve", "pool", "sp"):
            eng = self.engs[e]
            wd = self.waited[e]
            for o in ("pe", "act", "dve", "pool"):
                if o == e or self.cnt[o] == 0:
                    continue
                if wd.get(id(self.sem[o]), 0) < self.cnt[o]:
                    eng.wait_ge(self.sem[o], self.cnt[o])
                    wd[id(self.sem[o])] = self.cnt[o]
                    self.ninst += 1
            for b in dma_bufs:
                if b.dsem is not None and b.dcnt > 0 and wd.get(id(b.dsem), 0) < b.dcnt:
                    eng.wait_ge(b.dsem, b.dcnt)
                    wd[id(b.dsem)] = b.dcnt
                    self.ninst += 1


def build_program():
    nc = bass.Bass("TRN2", target_bir_lowering=False)
    em = Em(nc)

    def din(name, shape, dt=F32):
        return nc.dram_tensor(name, list(shape), dt, kind="ExternalInput").ap()

    def dout(name, shape, dt=F32):
        return nc.dram_tensor(name, list(shape), dt, kind="ExternalOutput").ap()

    def dscr(name, shape, dt=BF16):
        return nc.dram_tensor(name, list(shape), dt, kind="Internal").ap()

    xseq = din("xseq", [SEQX, D])
    xs_in = din("xs", [NTS, D])
    ck0 = din("ck0", [NB_S, 128, 1024])
    ck1 = din("ck1", [NB_S, 512, 1024])
    import os
    CK2R = 64 if os.environ.get('DBG_SMALL') else 2048
    ck2 = din("ck2", [NB_S, CK2R, 1024])
    sconv = din("sconv", [NB_S * 3, 3072])
    sssm = din("sssm", [NB_S, 8, 128, 128])
    w_gu1 = din("w_gu1", [D, 2 * DFF]); w_d1 = din("w_d1", [DFF, D])
    w_in = din("w_in", [D, INW])
    w_pa = din("w_pa", [512, D]); w_pb = din("w_pb", [D, D]); w_out = din("w_out", [D, D])
    w_gu2 = din("w_gu2", [D, 2 * DFF]); w_d2 = din("w_d2", [DFF, D])
    normT_in = din("normT", [128, 3, 8])
    normout_in = din("normout", [128, D])
    convw_in = din("convw", [128, 24, 4])
    gnorm_in = din("gnorm", [128, 1])
    alog_in = din("alog", [128, 8]); dtb_in = din("dtb", [128, 8])
    pmask_in = din("pmask", [128, 8])
    cst_f_in = din("cst_f", [128, 5, 128])
    ab01_in = din("ab01", [128, 8, 2, 128])
    ab2a_in = din("ab2a", [128, 4, 32])
    ab2b_in = din("ab2b", [32, 4, 32])
    sb_in = din("sbias", [128, 9, 16])
    sbn_in = din("sbiasn", [4, 3, 16])

    y_out = dout("y", [NMAIN * ST, D])
    ys_out = dout("ys", [NTS, D])
    kv0_out = dout("kv0", [128, 2, 512]); kv1_out = dout("kv1", [512, 2, 512]); kv2_out = dout("kv2", [2048, 2, 512])
    convp_out = dout("convp", [3, 3072])
    ssmp_out = dout("ssmp", [8, 128, 128])
    kvs_out = [dout("kvs%d" % g, [NTS, 2, 512]) for g in range(3)]
    convs_out = dout("convs", [NB_S, 3, 3072])
    ssms_out = dout("ssms", [NB_S, 8, 128, 128])

    gu_pan = [(fg * 128, min(4, 22 - fg) * 128) for fg in range(0, 22, 4)]
    gu_pan = gu_pan + [(DFF + c0, n) for (c0, n) in gu_pan]
    in_pan = ([(C_QA + g * 512, 512) for g in range(3)] + [(C_KA + g * 512, 512) for g in range(3)]
              + [(C_VA + g * 512, 512) for g in range(3)] + [(C_QKV + g * 512, 512) for g in range(6)]
              + [(C_Z + g * 512, 512) for g in range(2)] + [(C_B, 16)] + [(C_G + g * 512, 512) for g in range(4)])
    two = [(0, 512), (512, 512)]
    PAN = {"gu1": gu_pan, "gu2": gu_pan, "d1": two, "d2": two, "in": in_pan, "pa": two, "pb": two, "out": two}
    KCS = {"gu1": 8, "gu2": 8, "d1": 22, "d2": 22, "in": 8, "pa": 4, "pb": 8, "out": 8}
    POFF = {}
    scr1d = {}
    for nm_ in PAN:
        off = 0
        for (c0_, n_) in PAN[nm_]:
            POFF[(nm_, c0_, n_)] = off
            off += KCS[nm_] * 128 * n_
        scr1d[nm_] = dscr("s_" + nm_, [off])
    vscr = dscr("vscr", [SEQX, 1536])
    B_scr = {}

    def sb(name, shape, dt=F32):
        return nc.alloc_sbuf_tensor("sb_" + name, list(shape), dt)

    pstack = ExitStack()

    def sbp(name, shape, dt=F32):
        return pstack.enter_context(nc.sbuf_tensor("sb_" + name, list(shape), dt))

    cst_f = sb("cst_f", [128, 5, 128]); B_cst = Buf("cst")
    ident_f = cst_f[:, 0, :]; ones_f = cst_f[:, 1, :]; triU = cst_f[:, 2, :]
    nm_strict = cst_f[:, 3, :]; nm_inclT = cst_f[:, 4, :]
    cst_b = sb("cst_b", [128, 2, 128], BF16)
    ident_b = cst_b[:, 0, :]; ones_b = cst_b[:, 1, :]
    normT = sb("normT", [128, 3, 8]); normout = sb("normout", [128, D])
    convw = sb("convw", [128, 24, 4]); gnorm = sb("gnorm", [128, 1])
    alog = sb("alog", [128, 8]); dtb = sb("dtb", [128, 8]); negA = sb("negA", [128, 8])
    pmask = sb("pmask", [128, 8])
    epsc = sb("epsc", [128, 1])
    sbias = sb("sbias", [128, 9, 16]); sbiasn = sb("sbiasn", [4, 3, 16])

    with nc.Block() as block:
        for (dst, src) in [(cst_f, cst_f_in), (normT, normT_in), (normout, normout_in), (convw, convw_in),
                           (gnorm, gnorm_in), (alog, alog_in), (dtb, dtb_in), (pmask, pmask_in),
                           (sbias, sb_in), (sbiasn, sbn_in)]:
            em.dma("sp", dst[:], src, writes=[B_cst])
        em.op("pool", lambda e: e.memset(epsc[:], EPS), writes=[B_cst])
        em.op("dve", lambda e: e.tensor_copy(out=cst_b[:, 0, :], in_=ident_f), reads=[B_cst], writes=[B_cst])
        em.op("dve", lambda e: e.tensor_copy(out=cst_b[:, 1, :], in_=ones_f), reads=[B_cst], writes=[B_cst])
        em.op("act", lambda e: e.activation(out=negA[:], in_=alog[:], func=AF.Exp), reads=[B_cst], writes=[B_cst])
        em.op("dve", lambda e: e.tensor_scalar(out=negA[:], in0=negA[:], scalar1=-1.0, scalar2=None, op0=ALU.mult),
              reads=[B_cst], writes=[B_cst])

        def scr_view(nm, c0, n):
            o = POFF[(nm, c0, n)]
            kc = KCS[nm]
            return scr1d[nm][o:o + kc * 128 * n].rearrange("(p k c) -> p k c", p=128, k=kc, c=n)

        for (src, nm) in [(w_gu1, "gu1"), (w_d1, "d1"), (w_in, "in"), (w_pa, "pa"),
                          (w_pb, "pb"), (w_out, "out"), (w_gu2, "gu2"), (w_d2, "d2")]:
            bb = Buf("scr_" + nm)
            B_scr[nm] = bb
            for (c0_, n_) in PAN[nm]:
                em.dma("pool", scr_view(nm, c0_, n_), src[:, c0_:c0_ + n_].rearrange("(k p) c -> p k c", p=128),
                       writes=(), owner=bb, max_dma_last_dim=4096)
            bb.w = (bb.dsem, bb.dcnt)

        NPS = 6
        psf = [nc.alloc_psum_tensor("psf%d" % i, [128, 512], F32) for i in range(NPS)]
        B_psf = [Buf("psf%d" % i, excl=True) for i in range(NPS)]
        psb = [nc.alloc_psum_tensor("psb%d" % i, [128, 1024], BF16) for i in range(2)]
        B_psb = [Buf("psb%d" % i, excl=True) for i in range(2)]
        rr = {"f": 0, "b": 0, "ev": 0}

        held = set()

        def pget(hold=False):
            while True:
                i = rr["f"] % NPS
                rr["f"] += 1
                if i not in held:
                    break
            if hold:
                held.add(i)
            return psf[i], B_psf[i]

        def prelease(bp):
            held.discard(B_psf.index(bp))

        def pgetb():
            i = rr["b"] % 2
            rr["b"] += 1
            return psb[i], B_psb[i]

        def ev_eng():
            rr["ev"] += 1
            return "act" if rr["ev"] % 2 else "dve"

        def copy_op(eng, out, in_, reads, writes, scale=None):
            if eng == "act":
                if scale is None:
                    em.op("act", lambda e: e.activation(out=out, in_=in_, func=AF.Copy), reads=reads, writes=writes)
                else:
                    em.op("act", lambda e: e.activation(out=out, in_=in_, func=AF.Copy, scale=float(scale)),
                          reads=reads, writes=writes)
            else:
                if scale is None:
                    em.op(eng, lambda e: e.tensor_copy(out=out, in_=in_), reads=reads, writes=writes)
                else:
                    em.op(eng, lambda e: e.tensor_scalar(out=out, in0=in_, scalar1=float(scale), scalar2=None,
                                                         op0=ALU.mult), reads=reads, writes=writes)

        NSLOT = 3
        wslots = [sb("wslot%d" % i, [128, 8, 512], BF16) for i in range(NSLOT)]
        B_ws = [Buf("ws%d" % i) for i in range(NSLOT)]
        wst = {"i": 0, "pref": {}}

        def wload(nm, k0, nk, c0, ncols):
            key = (nm, k0, nk, c0, ncols)
            if key in wst["pref"]:
                return wst["pref"].pop(key)
            i = wst["i"] % NSLOT
            wst["i"] += 1
            src = scr_view(nm, c0, ncols)[:, k0:k0 + nk, :]
            em.dma("sp", wslots[i][:, 0:nk, 0:ncols], src, reads=[B_scr[nm]], writes=[B_ws[i]])
            return wslots[i], B_ws[i]

        def wprefetch(nm, k0, nk, c0, ncols):
            key = (nm, k0, nk, c0, ncols)
            if key not in wst["pref"]:
                wst["pref"][key] = wload(nm, k0, nk, c0, ncols)

        xt = sb("xt", [128, 4, D]); B_xt = [Buf("xt%d" % i) for i in range(4)]
        actT = [sb("actT%d" % i, [128, 8, ST], BF16) for i in range(2)]
        B_actT = [Buf("actT%d" % i) for i in range(2)]
        big = sb("big", [128, 24, ST], BF16); B_big = Buf("big")
        hidT = big[:, 0:22, :]; B_hid = B_big
        xn_tmp = sb("xn_tmp", [128, D], BF16); B_xn = Buf("xn")
        sm = sb("sm", [128, 8]); B_sm = Buf("sm")
        sp_y = sb("sp_y", [128, 4, 8]); sp_p = sb("sp_p", [128, 4, 8]); sp_m = sb("sp_m", [128, 4, 8]); B_spt = Buf("sp_tmp")
        junk = xn_tmp; B_junk = B_xn

        def rms_rstd(x_ap, xb, npart, col):
            em.op("act", lambda e: e.activation(out=junk[0:npart, :], in_=x_ap, func=AF.Square,
                                                accum_out=sm[0:npart, col:col + 1]),
                  reads=[xb], writes=[B_junk, B_sm])
            em.op("act", lambda e: e.activation(out=sm[0:npart, col:col + 1], in_=sm[0:npart, col:col + 1], func=AF.Sqrt,
                                                bias=epsc[0:npart, 0:1], scale=1.0 / D), reads=[B_sm, B_cst], writes=[B_sm])
            em.op("dve", lambda e: e.reciprocal(out=sm[0:npart, col:col + 1], in_=sm[0:npart, col:col + 1]),
                  reads=[B_sm], writes=[B_sm])

        def norm_T(tiles, npart, nidx, dstT, B_dst):
            for t, (x_ap, xb) in enumerate(tiles):
                rms_rstd(x_ap, xb, npart, 0)
                em.op("dve", lambda e: e.tensor_scalar(out=xn_tmp[0:npart, :], in0=x_ap, scalar1=sm[0:npart, 0:1],
                                                       scalar2=None, op0=ALU.mult),
                      reads=[xb, B_sm], writes=[B_xn])
                pt, bpt = pgetb()
                for kc in range(8):
                    em.op("pe", lambda e: e.transpose(out=pt[:, kc * 128:kc * 128 + npart],
                                                      in_=xn_tmp[0:npart, kc * 128:(kc + 1) * 128],
                                                      identity=ident_b[0:npart, 0:npart]),
                          reads=[B_xn, B_cst], writes=[bpt])
                for kc in range(8):
                    eng = ev_eng()
                    o = dstT[:, kc, t * npart:(t + 1) * npart]
                    i_ = pt[:, kc * 128:kc * 128 + npart]
                    if eng == "act":
                        em.op("act", lambda e: e.activation(out=o, in_=i_, func=AF.Copy,
                                                            scale=normT[:, nidx, kc:kc + 1]),
                              reads=[bpt, B_cst], writes=[B_dst])
                    else:
                        em.op("dve", lambda e: e.tensor_scalar(out=o, in0=i_, scalar1=normT[:, nidx, kc:kc + 1],
                                                               scalar2=None, op0=ALU.mult),
                              reads=[bpt, B_cst], writes=[B_dst])

        def mm_fm(srcT, B_src, wsl, B_w, nk, col0, ncol, N, ps, bps, first=True, last=True, k_off=0):
            for kc in range(nk):
                em.op("pe", lambda e: e.matmul(ps[0:ncol, 0:N], lhsT=wsl[:, kc, col0:col0 + ncol],
                                               rhs=srcT[:, k_off + kc, 0:N],
                                               start=(first and kc == 0), stop=(last and kc == nk - 1)),
                      reads=[B_src, B_w], writes=[bps])

        def mm_tm(srcT, B_src, tok0, ntok, wsl, B_w, nk, col0, ncol, ps, bps, first=True, last=True, k_off=0):
            for kc in range(nk):
                em.op("pe", lambda e: e.matmul(ps[0:ntok, 0:ncol], lhsT=srcT[:, k_off + kc, tok0:tok0 + ntok],
                                               rhs=wsl[:, kc, col0:col0 + ncol],
                                               start=(first and kc == 0), stop=(last and kc == nk - 1)),
                      reads=[B_src, B_w], writes=[bps])

        def ffn(tiles, npart, nidx, gu, dn, srcT, B_srcT):
            N = npart * len(tiles)
            norm_T(tiles, npart, nidx, srcT, B_srcT)
            for fg in range(0, 22, 4):
                nf = min(4, 22 - fg)
                wg, bwg = wload(gu, 0, 8, fg * 128, nf * 128)
                wu, bwu = wload(gu, 0, 8, DFF + fg * 128, nf * 128)
                for j in range(nf):
                    fh = fg + j
                    pg, bpg = pget()
                    mm_fm(srcT, B_srcT, wg, bwg, 8, j * 128, 128, N, pg, bpg)
                    pu, bpu = pget()
                    mm_fm(srcT, B_srcT, wu, bwu, 8, j * 128, 128, N, pu, bpu)
                    sgi = fh % 2
                    em.op("act", lambda e: e.activation(out=sg_tmp[sgi][:, 0:N], in_=pg[:, 0:N], func=AF.Silu),
                          reads=[bpg], writes=[B_sg[sgi]])
                    em.op("dve", lambda e: e.tensor_tensor(out=hidT[:, fh, 0:N], in0=sg_tmp[sgi][:, 0:N],
                                                           in1=pu[:, 0:N], op=ALU.mult),
                          reads=[B_sg[sgi], bpu], writes=[B_hid])
            for half in range(2):
                wds = [wload(dn, kg * 8, min(8, 22 - kg * 8), half * 512, 512) for kg in range(3)]
                for t, (x_ap, xb) in enumerate(tiles):
                    po, bpo = pget()
                    for kg in range(3):
                        nk = min(8, 22 - kg * 8)
                        mm_tm(hidT, B_hid, t * npart, npart, wds[kg][0], wds[kg][1], nk, 0, 512, po, bpo,
                              first=(kg == 0), last=(kg == 2), k_off=kg * 8)
                    xo = x_ap[:, half * 512:(half + 1) * 512]
                    em.op("dve", lambda e: e.scalar_tensor_tensor(out=xo, in0=po[0:npart, :], scalar=0.5, in1=xo,
                                                                  op0=ALU.mult, op1=ALU.add),
                          reads=[bpo, xb], writes=[xb])

        QT = sb("QT", [128, 4, ST], BF16); B_QT = Buf("QT")
        denT = big[:, 8:16, :].bitcast(F32).rearrange("p (a b) n -> p a (b n)", b=2); B_nd = B_big
        numT = big[:, 16:24, :].bitcast(F32).rearrange("p (a b) n -> p a (b n)", b=2)
        oaT = sb("oaT", [128, 4, ST], BF16); B_oaT = Buf("oaT")
        qbT = big[:, 0:8, :]; kbT = big[:, 8:16, :]; vbT = big[:, 16:24, :]
        B_qb = B_big; B_kb = B_big; B_vb = B_big
        ztmp = sb("ztmp", [128, ST], BF16); B_zt = Buf("ztmp")
        oTr = actT[0]; B_oTr = B_actT[0]
        obT = actT[0]; B_obT = B_actT[0]
        cb = [sb("cb0", [128, ST + 3])] * 2; B_cb = [Buf("cb0")] * 2
        cy = [sb("cy0", [128, ST])] * 2; B_cy = [Buf("cy0")] * 2
        sg_tmp = cy; B_sg = B_cy
        csq = [sb("csq0", [128, ST], BF16)] * 2; B_csq = [Buf("csq0")] * 2
        crn = [sb("crn0", [128, ST])] * 2; B_crn = [Buf("crn0")] * 2
        ba_all = sb("ba_all", [128, 4, 16]); B_ba = Buf("ba")
        beta_all = sb("beta_all", [128, 4, 8]); g_all = sb("g_all", [128, 4, 8]); B_bg = Buf("betag")
        stage_f = crn; B_stf = B_crn
        stage_b = [sb("stage_b0", [128, 512], BF16)] * 2; B_stb = [Buf("stb0")] * 2
        gd = {}
        for nm_, shp, dt_ in [("gc", [128, 8], F32), ("ngc", [128, 8], F32), ("ekd", [128, 8], F32), ("bgc", [128, 8], F32),
                              ("GC", [128, 8, 128], F32), ("EGC", [128, 8, 128], F32)]:
            gd[nm_] = sb("gd_" + nm_, shp, dt_)
        gd["R"] = gd["EGC"]
        B_gdt = Buf("gd_tile")
        B_GC = Buf("gd_GC"); B_R = B_GC
        hb = []

        def make_hb(par):
            d_ = {}
            for nm_, shp, dt_ in [("arg", [128, 128], F32), ("decT", [128, 128], F32),
                                  ("A_f", [128, 128], F32), ("Y_f", [128, 128], F32),
                                  ("AT_b", [128, 128], F32),
                                  ("P0", [128, 128], F32), ("PT0", [128, 128], F32),
                                  ("P1", [128, 128], F32), ("PT1", [128, 128], F32),
                                  ("vn", [128, 128], F32), ("QKT", [128, 128], F32),
                                  ("QgT", [128, 128], F32)]:
                d_[nm_] = sb("gh%d_%s" % (par, nm_), shp, dt_)
                d_["B_" + nm_] = Buf("gh%d_%s" % (par, nm_))
            d_["decs"] = d_["arg"]; d_["B_decs"] = d_["B_arg"]
            for a_, b_ in (("Kbg", "P0"), ("Kd", "PT0"), ("Vb", "P1"), ("WTn", "PT1")):
                d_[a_] = d_[b_]; d_["B_" + a_] = d_["B_" + b_]
            d_["A_b"] = d_["A_f"]; d_["B_A_b"] = d_["B_A_f"]
            d_["Y_b"] = d_["Y_f"]; d_["B_Y_b"] = d_["B_Y_f"]
            hb.append(d_)

        for par_ in range(4):
            make_hb(par_)
        ab01 = sbp("ab01", [128, 8, 2, 128]); ab2a = sbp("ab2a", [128, 4, 32]); ab2b = sbp("ab2b", [32, 4, 32])
        for (dst, src) in [(ab01, ab01_in), (ab2a, ab2a_in), (ab2b, ab2b_in)]:
            em.dma("sp", dst[:], src, writes=[B_cst])
        KT2 = sbp("KT2", [128, 4, SEQX], BF16); B_KT2 = Buf("KT2")
        KT01 = sbp("KT01", [128, 8, 2 * ST], BF16); B_KT01 = [Buf("KT01a"), Buf("KT01b")]
        S_f = sbp("S_f", [128, 8, 128]); S_b = S_f
        B_S = [Buf("S%d" % h) for h in range(8)]
        halo = sbp("halo", [128, 24, 3]); B_halo = Buf("halo")
        vt = [sbp("vt%d" % i, [128, 512], BF16) for i in range(2)]; B_vt = [Buf("vt%d" % i) for i in range(2)]
        scs = [sbp("scs0", [128, 512])] * 2; B_scs = [Buf("scs0")] * 2
        pts = [sbp("pts%d" % i, [128, 512], BF16) for i in range(2)]; B_pts = [Buf("pts%d" % i) for i in range(2)]
        cnt = {"st": 0, "sb": 0, "cb": 0}

        em.op("pool", lambda e: e.memset(S_f[:], 0.0), writes=B_S)
        em.op("pool", lambda e: e.memset(halo[:], 0.0), writes=[B_halo])

        def out_dma(dst, src, reads):
            if os.environ.get('DBG_NOOUT'):
                return
            em.dma("pool", dst, src, reads=reads, writes=(), owner=reads[0])

        gcount = {"h": 0}

        def gdn_tile(C, qT, kT, vT, B_q, B_k, B_v, beta, g, B_bgin, Sf, Sb, B_Sl, need_o, oT_dst, B_oT):
            nsq = min(4, max(1, int(math.ceil(math.log2(C))) - 1))
            p0, bp0 = pget()
            em.op("pe", lambda e: e.matmul(p0[0:C, 0:8], lhsT=triU[0:C, 0:C], rhs=g, start=True, stop=True),
                  reads=[B_cst, B_bgin], writes=[bp0])
            em.op("act", lambda e: e.activation(out=gd["gc"][0:C, :], in_=p0[0:C, 0:8], func=AF.Copy),
                  reads=[bp0], writes=[B_gdt])
            em.op("dve", lambda e: e.tensor_scalar(out=gd["ngc"][0:C, :], in0=p0[0:C, 0:8], scalar1=-1.0, scalar2=None,
                                                   op0=ALU.mult), reads=[bp0], writes=[B_gdt])
            for h in range(8):
                em.op("dve", lambda e: e.tensor_scalar(out=gd["R"][0:C, h, 0:C], in0=triU[0:C, 0:C],
                                                       scalar1=g[:, h:h + 1], scalar2=None, op0=ALU.mult),
                      reads=[B_cst, B_bgin], writes=[B_R])
            hpb = max(1, 512 // C)
            for h0 in range(0, 8, min(8, hpb)):
                nh = min(8, hpb)
                pg_, bpg_ = pget()
                src = pg_[:, 0:nh * C].rearrange("p (h c) -> p h c", c=C)
                em.op("pe", lambda e: e.matmul(src, lhsT=ones_f[0:C, :], rhs=gd["R"][0:C, h0:h0 + nh, 0:C],
                                               start=True, stop=True), reads=[B_cst, B_R], writes=[bpg_])
                em.op("dve", lambda e: e.tensor_copy(out=gd["GC"][:, h0:h0 + nh, 0:C], in_=src),
                      reads=[bpg_], writes=[B_GC])
                em.op("act", lambda e: e.activation(out=gd["EGC"][:, h0:h0 + nh, 0:C], in_=src, func=AF.Exp),
                      reads=[bpg_], writes=[B_GC])
            em.op("dve", lambda e: e.tensor_tensor(out=gd["ekd"][0:C, :], in0=gd["GC"][0:C, :, C - 1],
                                                   in1=gd["gc"][0:C, :], op=ALU.subtract),
                  reads=[B_GC, B_gdt], writes=[B_gdt])
            em.op("act", lambda e: e.activation(out=gd["ekd"][0:C, :], in_=gd["ekd"][0:C, :], func=AF.Exp),
                  reads=[B_gdt], writes=[B_gdt])
            em.op("act", lambda e: e.activation(out=gd["bgc"][0:C, :], in_=gd["gc"][0:C, :], func=AF.Exp),
                  reads=[B_gdt], writes=[B_gdt])
            em.op("dve", lambda e: e.tensor_tensor(out=gd["bgc"][0:C, :], in0=gd["bgc"][0:C, :], in1=beta, op=ALU.mult),
                  reads=[B_gdt, B_bgin], writes=[B_gdt])
            def head_gen(h, T):
                kTh = kT(h)
                pkk, bpkk = pget()
                em.op("pe", lambda e: e.matmul(pkk[0:C, 0:C], lhsT=kTh, rhs=kTh, start=True, stop=True),
                      reads=[B_k], writes=[bpkk])
                em.op("dve", lambda e: e.scalar_tensor_tensor(out=T["arg"][0:C, 0:C], in0=gd["GC"][0:C, h, 0:C],
                                                              scalar=-1.0, in1=nm_strict[0:C, 0:C],
                                                              op0=ALU.mult, op1=ALU.add),
                      reads=[B_GC, B_cst], writes=[T["B_arg"]])
                em.op("act", lambda e: e.activation(out=T["decs"][0:C, 0:C], in_=T["arg"][0:C, 0:C], func=AF.Exp,
                                                    bias=gd["gc"][0:C, h:h + 1]),
                      reads=[T["B_arg"], B_gdt], writes=[T["B_decs"]])
                em.op("dve", lambda e: e.scalar_tensor_tensor(out=T["A_f"][0:C, 0:C], in0=pkk[0:C, 0:C],
                                                              scalar=beta[:, h:h + 1], in1=T["decs"][0:C, 0:C],
                                                              op0=ALU.mult, op1=ALU.mult),
                      reads=[bpkk, B_bgin, T["B_decs"]], writes=[T["B_A_f"]])
                if need_o:
                    em.op("dve", lambda e: e.tensor_tensor(out=T["arg"][0:C, 0:C], in0=gd["GC"][0:C, h, 0:C],
                                                           in1=nm_inclT[0:C, 0:C], op=ALU.add),
                          reads=[B_GC, B_cst], writes=[T["B_arg"]])
                    em.op("act", lambda e: e.activation(out=T["decT"][0:C, 0:C], in_=T["arg"][0:C, 0:C], func=AF.Exp,
                                                        bias=gd["ngc"][0:C, h:h + 1]),
                          reads=[T["B_arg"], B_gdt], writes=[T["B_decT"]])
                yield
                pat, bpat = pget()
                em.op("pe", lambda e: e.transpose(out=pat[0:C, 0:C], in_=T["A_f"][0:C, 0:C], identity=ident_f[0:C, 0:C]),
                      reads=[T["B_A_f"], B_cst], writes=[bpat])
                em.op("dve", lambda e: e.scalar_tensor_tensor(out=T["Y_f"][0:C, 0:C], in0=pat[0:C, 0:C], scalar=-1.0,
                                                              in1=ident_f[0:C, 0:C], op0=ALU.mult, op1=ALU.add),
                      reads=[bpat, B_cst], writes=[T["B_Y_f"]])
                em.op("act", lambda e: e.activation(out=T["AT_b"][0:C, 0:C], in_=pat[0:C, 0:C], func=AF.Copy),
                      reads=[bpat], writes=[T["B_AT_b"]])
                P, PT_, BP, BPT = T["A_b"], T["AT_b"], T["B_A_b"], T["B_AT_b"]
                for s in range(nsq):
                    nP, nPT = T["P%d" % (s % 2)], T["PT%d" % (s % 2)]
                    BnP, BnPT = T["B_P%d" % (s % 2)], T["B_PT%d" % (s % 2)]
                    yield
                    pp, bpp = pget()
                    em.op("pe", lambda e: e.matmul(pp[0:C, 0:C], lhsT=PT_[0:C, 0:C], rhs=P[0:C, 0:C], start=True, stop=True),
                          reads=[BP, BPT], writes=[bpp])
                    copy_op("act", nP[0:C, 0:C], pp[0:C, 0:C], [bpp], [BnP])
                    if s < nsq - 1:
                        yield
                        pq, bpq = pget()
                        em.op("pe", lambda e: e.matmul(pq[0:C, 0:C], lhsT=P[0:C, 0:C], rhs=PT_[0:C, 0:C], start=True, stop=True),
                              reads=[BP, BPT], writes=[bpq])
                        copy_op("dve", nPT[0:C, 0:C], pq[0:C, 0:C], [bpq], [BnPT])
                    yield
                    py, bpy = pget()
                    em.op("pe", lambda e: e.matmul(py[0:C, 0:C], lhsT=nP[0:C, 0:C], rhs=T["Y_b"][0:C, 0:C], start=True, stop=True),
                          reads=[BnP, T["B_Y_b"]], writes=[bpy])
                    em.op("dve", lambda e: e.tensor_tensor(out=T["Y_f"][0:C, 0:C], in0=py[0:C, 0:C], in1=T["Y_f"][0:C, 0:C],
                                                           op=ALU.add), reads=[bpy, T["B_Y_f"]], writes=[T["B_Y_f"]])
                    P, PT_, BP, BPT = nP, nPT, BnP, BnPT
                yield
                ptk, bptk = pgetb()
                em.op("pe", lambda e: e.transpose(out=ptk[0:C, 0:128], in_=kTh, identity=ident_b[:, :]),
                      reads=[B_k, B_cst], writes=[bptk])
                em.op("pe", lambda e: e.transpose(out=ptk[0:C, 128:256], in_=vT(h), identity=ident_b[:, :]),
                      reads=[B_v, B_cst], writes=[bptk])
                em.op("dve", lambda e: e.tensor_scalar(out=T["Kbg"][0:C, :], in0=ptk[0:C, 0:128], scalar1=gd["bgc"][0:C, h:h + 1],
                                                       scalar2=None, op0=ALU.mult), reads=[bptk, B_gdt], writes=[T["B_Kbg"]])
                em.op("act", lambda e: e.activation(out=T["Kd"][0:C, :], in_=ptk[0:C, 0:128], func=AF.Copy,
                                                    scale=gd["ekd"][0:C, h:h + 1]), reads=[bptk, B_gdt], writes=[T["B_Kd"]])
                em.op("dve", lambda e: e.tensor_scalar(out=T["Vb"][0:C, :], in0=ptk[0:C, 128:256], scalar1=beta[:, h:h + 1],
                                                       scalar2=None, op0=ALU.mult), reads=[bptk, B_bgin], writes=[T["B_Vb"]])
                yield
                pw, bpw = pget()
                em.op("pe", lambda e: e.matmul(pw[:, 0:C], lhsT=T["Kbg"][0:C, :], rhs=T["Y_f"][0:C, 0:C], start=True, stop=True),
                      reads=[T["B_Kbg"], T["B_Y_f"]], writes=[bpw])
                copy_op("act", T["WTn"][:, 0:C], pw[:, 0:C], [bpw], [T["B_WTn"]], scale=-1.0)
                yield
                pv, bpv = pget()
                em.op("pe", lambda e: e.matmul(pv[0:C, 0:128], lhsT=T["Y_f"][0:C, 0:C], rhs=T["Vb"][0:C, :], start=True, stop=False),
                      reads=[T["B_Y_f"], T["B_Vb"]], writes=[bpv])
                em.op("pe", lambda e: e.matmul(pv[0:C, 0:128], lhsT=T["WTn"][:, 0:C], rhs=Sf(h), start=False, stop=True),
                      reads=[T["B_WTn"], B_Sl[h]], writes=[bpv])
                copy_op("dve", T["vn"][0:C, :], pv[0:C, 0:128], [bpv], [T["B_vn"]])
                if need_o:
                    yield
                    pqk, bpqk = pget()
                    em.op("pe", lambda e: e.matmul(pqk[0:C, 0:C], lhsT=kTh, rhs=qT(h), start=True, stop=True),
                          reads=[B_k, B_q], writes=[bpqk])
                    em.op("dve", lambda e: e.tensor_tensor(out=T["QKT"][0:C, 0:C], in0=pqk[0:C, 0:C], in1=T["decT"][0:C, 0:C],
                                                           op=ALU.mult), reads=[bpqk, T["B_decT"]], writes=[T["B_QKT"]])
                    em.op("pool", lambda e: e.tensor_tensor(out=T["QgT"][:, 0:C], in0=qT(h), in1=gd["EGC"][:, h, 0:C],
                                                            op=ALU.mult), reads=[B_q, B_GC], writes=[T["B_QgT"]])
                    yield
                    po_, bpo_ = pget()
                    em.op("pe", lambda e: e.matmul(po_[:, 0:C], lhsT=Sf(h), rhs=T["QgT"][:, 0:C], start=True, stop=False),
                          reads=[B_Sl[h], T["B_QgT"]], writes=[bpo_])
                    em.op("pe", lambda e: e.matmul(po_[:, 0:C], lhsT=T["vn"][0:C, :], rhs=T["QKT"][0:C, 0:C], start=False, stop=True),
                          reads=[T["B_vn"], T["B_QKT"]], writes=[bpo_])
                    copy_op("act", oT_dst(h), po_[:, 0:C], [bpo_], [B_oT])
                yield
                ps_, bps_ = pget()
                em.op("pe", lambda e: e.matmul(ps_[:, 0:128], lhsT=T["Kd"][0:C, :], rhs=T["vn"][0:C, :], start=True, stop=True),
                      reads=[T["B_Kd"], T["B_vn"]], writes=[bps_])
                em.op("dve", lambda e: e.scalar_tensor_tensor(out=Sf(h), in0=Sf(h), scalar=gd["EGC"][:, h, C - 1:C],
                                                              in1=ps_[:, 0:128], op0=ALU.mult, op1=ALU.add),
                      reads=[bps_, B_GC, B_Sl[h]], writes=[B_Sl[h]])
                yield

            NIL = len(hb)
            for h0 in range(0, 8, NIL):
                alive = [head_gen(h0 + j, hb[j]) for j in range(NIL) if h0 + j < 8]
                while alive:
                    for g_ in list(alive):
                        try:
                            next(g_)
                        except StopIteration:
                            alive.remove(g_)

        def beta_g(ba_ap, beta_o, g_o, npart, reads):
            n = ba_ap.shape[1]
            y = sp_y[0:npart, 0:n, :]; p = sp_p[0:npart, 0:n, :]; m = sp_m[0:npart, 0:n, :]
            em.op("act", lambda e: e.activation(out=beta_o, in_=ba_ap[:, :, 0:8], func=AF.Sigmoid), reads=reads, writes=[B_bg])
            for j in range(n):
                em.op("dve", lambda e: e.tensor_tensor(out=y[:, j, :], in0=ba_ap[:, j, 8:16], in1=dtb[0:npart, :], op=ALU.add),
                      reads=reads + [B_cst], writes=[B_spt])
            em.op("act", lambda e: e.activation(out=y, in_=y, func=AF.Exp), reads=[B_spt], writes=[B_spt])
            em.op("act", lambda e: e.activation(out=g_o, in_=y, func=AF.Ln, bias=1.0), reads=[B_spt], writes=[B_bg])
            em.op("dve", lambda e: e.tensor_scalar(out=p, in0=y, scalar1=1.0 / 7, scalar2=None, op0=ALU.mult),
                  reads=[B_spt], writes=[B_spt])
            for ck in (-1.0 / 6, 1.0 / 5, -1.0 / 4, 1.0 / 3, -1.0 / 2, 1.0):
                em.op("dve", lambda e: e.scalar_tensor_tensor(out=p, in0=p, scalar=float(ck), in1=y, op0=ALU.add, op1=ALU.mult),
                      reads=[B_spt], writes=[B_spt])
            em.op("dve", lambda e: e.tensor_single_scalar(out=m, in_=y, scalar=0.3, op=ALU.is_lt), reads=[B_spt], writes=[B_spt])
            em.op("dve", lambda e: e.tensor_tensor(out=p, in0=p, in1=g_o, op=ALU.subtract), reads=[B_spt, B_bg], writes=[B_spt])
            em.op("dve", lambda e: e.tensor_tensor(out=p, in0=p, in1=m, op=ALU.mult), reads=[B_spt], writes=[B_spt])
            em.op("dve", lambda e: e.tensor_tensor(out=g_o, in0=g_o, in1=p, op=ALU.add), reads=[B_spt, B_bg], writes=[B_bg])
            for j in range(n):
                em.op("dve", lambda e: e.tensor_tensor(out=g_o[:, j, :], in0=g_o[:, j, :], in1=negA[0:npart, :], op=ALU.mult),
                      reads=[B_bg, B_cst], writes=[B_bg])

        def conv_chunk(c, ps, bps, N, sample=False):
            i = cnt["cb"] % 2
            cnt["cb"] += 1
            if not sample:
                em.op("pool", lambda e: e.tensor_copy(out=cb[i][:, 0:3], in_=halo[:, c, :]), reads=[B_halo], writes=[B_cb[i]])
                em.op("act", lambda e: e.activation(out=cb[i][:, 3:3 + N], in_=ps[:, 0:N], func=AF.Copy), reads=[bps], writes=[B_cb[i]])
                em.op("pool", lambda e: e.tensor_copy(out=halo[:, c, :], in_=cb[i][:, N:N + 3]), reads=[B_cb[i]], writes=[B_halo])
                win = lambda j: cb[i][:, j:j + N]
                yv = cy[i][:, 0:N]
                rbuf = [B_cb[i]]
            else:
                cbs = scb[:, c, :, :]
                em.op("act", lambda e: e.activation(out=cbs[:, :, 3:7], in_=ps[:, 0:N].rearrange("p (b s) -> p b s", s=4),
                                                    func=AF.Copy), reads=[bps], writes=[B_scb])
                win = lambda j: cbs[:, :, j:j + 4]
                yv = cy[i][:, 0:N].rearrange("p (b s) -> p b s", s=4)
                rbuf = [B_scb]
            em.op("dve", lambda e: e.tensor_scalar(out=yv, in0=win(0), scalar1=convw[:, c, 0:1], scalar2=None, op0=ALU.mult),
                  reads=rbuf + [B_cst], writes=[B_cy[i]])
            for j in range(1, 4):
                em.op("dve", lambda e: e.scalar_tensor_tensor(out=yv, in0=win(j), scalar=convw[:, c, j:j + 1], in1=yv,
                                                              op0=ALU.mult, op1=ALU.add),
                      reads=rbuf + [B_cst, B_cy[i]], writes=[B_cy[i]])
            y2 = cy[i][:, 0:N]
            if c >= 16:
                em.op("act", lambda e: e.activation(out=vbT[:, c - 16, 0:N], in_=y2, func=AF.Silu), reads=[B_cy[i]], writes=[B_vb])
                return
            em.op("act", lambda e: e.activation(out=y2, in_=y2, func=AF.Silu), reads=[B_cy[i]], writes=[B_cy[i]])
            em.op("pool", lambda e: e.tensor_tensor(out=csq[i][:, 0:N], in0=y2, in1=y2, op=ALU.mult), reads=[B_cy[i]], writes=[B_csq[i]])
            pss, bpss = pget()
            em.op("pe", lambda e: e.matmul(pss[:, 0:N], lhsT=ones_b, rhs=csq[i][:, 0:N], start=True, stop=True),
                  reads=[B_cst, B_csq[i]], writes=[bpss])
            em.op("act", lambda e: e.activation(out=crn[i][:, 0:N], in_=pss[:, 0:N], func=AF.Sqrt, bias=epsc[:, 0:1]),
                  reads=[bpss, B_cst], writes=[B_crn[i]])
            em.op("dve", lambda e: e.reciprocal(out=crn[i][:, 0:N], in_=crn[i][:, 0:N]), reads=[B_crn[i]], writes=[B_crn[i]])
            if c < 8:
                dst, bd, scl = qbT[:, c, 0:N], B_qb, 128.0 ** -0.5
            else:
                dst, bd, scl = kbT[:, c - 8, 0:N], B_kb, 1.0
            em.op("dve", lambda e: e.scalar_tensor_tensor(out=dst, in0=y2, scalar=scl, in1=crn[i][:, 0:N],
                                                          op0=ALU.mult, op1=ALU.mult),
                  reads=[B_cy[i], B_crn[i]], writes=[bd])

        def gated_norm(N, hT, B_hT):
            for h in range(8):
                if h % 4 == 0:
                    wz, bwz = wload("in", 0, 8, C_Z + (h // 4) * 512, 512)
                pz_, bpz_ = pget()
                mm_fm(hT, B_hT, wz, bwz, 8, (h % 4) * 128, 128, N, pz_, bpz_)
                em.op("act", lambda e: e.activation(out=ztmp[:, 0:N], in_=pz_[:, 0:N], func=AF.Silu),
                      reads=[bpz_], writes=[B_zt])
                i = 0
                em.op("pool", lambda e: e.tensor_tensor(out=csq[i][:, 0:N], in0=oTr[:, h, 0:N], in1=oTr[:, h, 0:N], op=ALU.mult),
                      reads=[B_oTr], writes=[B_csq[i]])
                pss, bpss = pget()
                em.op("pe", lambda e: e.matmul(pss[:, 0:N], lhsT=ones_b, rhs=csq[i][:, 0:N], start=True, stop=True),
                      reads=[B_cst, B_csq[i]], writes=[bpss])
                em.op("act", lambda e: e.activation(out=crn[i][:, 0:N], in_=pss[:, 0:N], func=AF.Sqrt, bias=epsc[:, 0:1],
                                                    scale=1.0 / 128), reads=[bpss, B_cst], writes=[B_crn[i]])
                em.op("dve", lambda e: e.reciprocal(out=crn[i][:, 0:N], in_=crn[i][:, 0:N]), reads=[B_crn[i]], writes=[B_crn[i]])
                em.op("dve", lambda e: e.tensor_tensor(out=crn[i][:, 0:N], in0=crn[i][:, 0:N], in1=oTr[:, h, 0:N], op=ALU.mult),
                      reads=[B_crn[i], B_oTr], writes=[B_crn[i]])
                em.op("dve", lambda e: e.scalar_tensor_tensor(out=obT[:, h, 0:N], in0=crn[i][:, 0:N], scalar=gnorm[:, 0:1],
                                                              in1=ztmp[:, 0:N], op0=ALU.mult, op1=ALU.mult),
                      reads=[B_crn[i], B_cst, B_zt], writes=[B_obT])

        acnt = {"v": 0, "s": 0, "p": 0}

        def attn_block(g, nq, qcols, tiles, first_group, scat):
            ptl = []
            for (nk, kTf, bk, vrows, bias, mcol) in tiles:
                vi = acnt["v"] % 2; acnt["v"] += 1
                em.dma("sp", vt[vi][0:nk, :], vrows, reads=(), writes=[B_vt[vi]])
                pS, bpS = pget()
                for h in range(4):
                    em.op("pe", lambda e: e.matmul(pS[0:nk, h * nq:(h + 1) * nq], lhsT=kTf(h), rhs=qcols(h),
                                                   start=True, stop=True), reads=[bk, B_QT], writes=[bpS])
                si = acnt["s"] % 2; acnt["s"] += 1
                em.op("dve", lambda e: e.tensor_tensor(out=scs[si][0:nk, 0:4 * nq].rearrange("p (h q) -> p h q", q=nq),
                                                       in0=pS[0:nk, 0:4 * nq].rearrange("p (h q) -> p h q", q=nq),
                                                       in1=bias, op=ALU.add),
                      reads=[bpS, B_cst], writes=[B_scs[si]])
                pi = acnt["p"] % 2; acnt["p"] += 1
                em.op("act", lambda e: e.activation(out=pts[pi][0:nk, 0:4 * nq], in_=scs[si][0:nk, 0:4 * nq], func=AF.Exp,
                                                    bias=mcol), reads=[B_scs[si], B_cst], writes=[B_pts[pi]])
                ptl.append((nk, vi, pi))
            pN, bpN = pget()
            pD, bpD = pget()
            first = True
            for ti, (nk, vi, pi) in enumerate(ptl):
                lastt = ti == len(ptl) - 1
                for h in range(4):
                    em.op("pe", lambda e: e.matmul(pN[:, h * nq:(h + 1) * nq], lhsT=vt[vi][0:nk, h * 128:(h + 1) * 128],
                                                   rhs=pts[pi][0:nk, h * nq:(h + 1) * nq], start=(first and h == 0),
                                                   stop=(lastt and h == 3)), reads=[B_vt[vi], B_pts[pi]], writes=[bpN])
                em.op("pe", lambda e: e.matmul(pD[:, 0:4 * nq], lhsT=ones_b[0:nk, :], rhs=pts[pi][0:nk, 0:4 * nq],
                                               start=first, stop=lastt), reads=[B_cst, B_pts[pi]], writes=[bpD])
                first = False
            srcN = pN[:, 0:4 * nq].rearrange("p (h q) -> p h q", q=nq)
            srcD = pD[:, 0:4 * nq].rearrange("p (h q) -> p h q", q=nq)
            if first_group:
                em.op("act", lambda e: e.activation(out=scat(numT), in_=srcN, func=AF.Copy), reads=[bpN], writes=[B_nd])
                em.op("dve", lambda e: e.tensor_copy(out=scat(denT), in_=srcD), reads=[bpD], writes=[B_nd])
            else:
                em.op("dve", lambda e: e.tensor_tensor(out=scat(numT), in0=srcN, in1=scat(numT), op=ALU.add),
                      reads=[bpN, B_nd], writes=[B_nd])
                em.op("dve", lambda e: e.tensor_tensor(out=scat(denT), in0=srcD, in1=scat(denT), op=ALU.add),
                      reads=[bpD, B_nd], writes=[B_nd])

        def q_group(g, hT, B_hT, N, dst, B_dst):
            wq, bwq = wload("in", 0, 8, C_QA + g * 512, 512)
            for j in range(4):
                pq_, bpq_ = pget()
                mm_fm(hT, B_hT, wq, bwq, 8, j * 128, 128, N, pq_, bpq_)
                copy_op(ev_eng(), dst(j), pq_[:, 0:N], [bpq_], [B_dst], scale=128.0 ** -0.5)

        def attention_st(s, hT, B_hT):
            T0 = (NPRE + s) * ST
            cur = (NPRE + s) % 2
            prv = 1 - cur
            zc = pmask[:, 4:5]
            em.finish("sp", B_stb)
            q_group(0, hT, B_hT, ST, lambda j: QT[:, j, :], B_QT)
            for qb in range(4):
                if qb == 0:
                    kprev = lambda h: KT01[:, h, prv * ST + 384: prv * ST + 512]; bkp = B_KT01[prv]
                    mc = pmask[:, 0:1] if s == 0 else zc
                else:
                    kprev = (lambda qb_: (lambda h: KT01[:, h, cur * ST + (qb_ - 1) * 128: cur * ST + qb_ * 128]))(qb); bkp = B_KT01[cur]
                    mc = zc
                kcur = (lambda qb_: (lambda h: KT01[:, h, cur * ST + qb_ * 128: cur * ST + (qb_ + 1) * 128]))(qb)
                tl = [(128, kprev, bkp, vscr[T0 + qb * 128 - 128:T0 + qb * 128, 0:512], ab01[:, 0:4, 0, :], mc),
                      (128, kcur, B_KT01[cur], vscr[T0 + qb * 128:T0 + qb * 128 + 128, 0:512], ab01[:, 0:4, 1, :], zc)]
                attn_block(0, 128, (lambda qb_: (lambda h: QT[:, h, qb_ * 128:(qb_ + 1) * 128]))(qb), tl, True,
                           (lambda qb_: (lambda tns: tns[:, :, qb_ * 128:(qb_ + 1) * 128]))(qb))
            q_group(1, hT, B_hT, ST, lambda j: QT[:, j, :], B_QT)
            for r in range(4):
                kprev = (lambda r_: (lambda h: KT01[:, 4 + h, sl(prv * ST + r_, 128, 4)]))(r)
                kcur = (lambda r_: (lambda h: KT01[:, 4 + h, sl(cur * ST + r_, 128, 4)]))(r)
                mc = pmask[:, 0:1] if s == 0 else zc
                tl = [(128, kprev, B_KT01[prv], vscr[sl(T0 - ST + r, 128, 4), 512:1024], ab01[:, 4:8, 0, :], mc),
                      (128, kcur, B_KT01[cur], vscr[sl(T0 + r, 128, 4), 512:1024], ab01[:, 4:8, 1, :], zc)]
                attn_block(1, 128, (lambda r_: (lambda h: QT[:, h, sl(r_, 128, 4)]))(r), tl, False,
                           (lambda r_: (lambda tns: tns[:, :, sl(r_, 128, 4)]))(r))
            q_group(2, hT, B_hT, ST, lambda j: QT[:, j, :], B_QT)
            for r in range(16):
                ka = (lambda r_: (lambda h: KT2[:, h, sl(T0 - 2048 + r_, 128, 16)]))(r)
                kb_ = (lambda r_: (lambda h: KT2[:, h, sl(T0 + r_, 32, 16)]))(r)
                mc = pmask[:, s:s + 1]
                tl = [(128, ka, B_KT2, vscr[sl(T0 - 2048 + r, 128, 16), 1024:1536], ab2a[:, :, :], mc),
                      (32, kb_, B_KT2, vscr[sl(T0 + r, 32, 16), 1024:1536], ab2b[:, :, :], pmask[0:32, 4:5])]
                attn_block(2, 32, (lambda r_: (lambda h: QT[:, h, sl(r_, 32, 16)]))(r), tl, False,
                           (lambda r_: (lambda tns: tns[:, :, sl(r_, 32, 16)]))(r))
            em.op("dve", lambda e: e.reciprocal(out=denT[:], in_=denT[:]), reads=[B_nd], writes=[B_nd])
            em.op("dve", lambda e: e.tensor_tensor(out=oaT[:], in0=numT[:], in1=denT[:], op=ALU.mult),
                  reads=[B_nd], writes=[B_oaT])

        sgm = big[:, 8:12, :]; B_sgm = B_big
        mrg = big[:, 0:8, :].rearrange("p k n -> p (k n)").rearrange("p (t d) -> p t d", d=D); B_mrg = B_big

        def post_mixer(tiles, npart, hT, B_hT, mT, B_mT, fT, B_fT, ydst):
            N = npart * len(tiles)
            nt = len(tiles)
            for half in range(2):
                for br in range(2):
                    wg, bwg = wload("in", 0, 8, C_G + br * D + half * 512, 512)
                    for t in range(nt):
                        pg_, bpg_ = pget()
                        mm_tm(hT, B_hT, t * npart, npart, wg, bwg, 8, 0, 512, pg_, bpg_)
                        em.op("act", lambda e: e.activation(out=sgm[0:npart, t, :], in_=pg_[0:npart, :], func=AF.Sigmoid),
                              reads=[bpg_], writes=[B_sgm])
                    if br == 0:
                        wp, bwp = wload("pa", 0, 4, half * 512, 512)
                        src, bsrc, nk = oaT, B_oaT, 4
                    else:
                        wp, bwp = wload("pb", 0, 8, half * 512, 512)
                        src, bsrc, nk = obT, B_obT, 8
                    for t in range(nt):
                        pp_, bpp_ = pget()
                        mm_tm(src, bsrc, t * npart, npart, wp, bwp, nk, 0, 512, pp_, bpp_)
                        mo = mrg[0:npart, t, half * 512:(half + 1) * 512]
                        if br == 0:
                            em.op("dve", lambda e: e.tensor_tensor(out=mo, in0=pp_[0:npart, :], in1=sgm[0:npart, t, :], op=ALU.mult),
                                  reads=[bpp_, B_sgm], writes=[B_mrg])
                        else:
                            em.op("dve", lambda e: e.tensor_tensor(out=sgm[0:npart, t, :], in0=pp_[0:npart, :], in1=sgm[0:npart, t, :],
                                                                   op=ALU.mult), reads=[bpp_, B_sgm], writes=[B_sgm])
                            em.op("pool", lambda e: e.tensor_tensor(out=mo, in0=mo, in1=sgm[0:npart, t, :], op=ALU.add),
                                  reads=[B_sgm, B_mrg], writes=[B_mrg])
            for t in range(nt):
                em.op("act", lambda e: e.activation(out=xn_tmp[0:npart, :], in_=mrg[0:npart, t, :], func=AF.Copy),
                      reads=[B_mrg], writes=[B_xn])
                pt, bpt = pgetb()
                for kc in range(8):
                    em.op("pe", lambda e: e.transpose(out=pt[:, kc * 128:kc * 128 + npart], in_=xn_tmp[0:npart, kc * 128:(kc + 1) * 128],
                                                      identity=ident_b[0:npart, 0:npart]), reads=[B_xn, B_cst], writes=[bpt])
                copy_op(ev_eng(), mT[:, :, t * npart:(t + 1) * npart],
                        pt[:, :].rearrange("p (k c) -> p k c", c=128)[:, :, 0:npart], [bpt], [B_mT])
            for half in range(2):
                wo, bwo = wload("out", 0, 8, half * 512, 512)
                for t, (x_ap, xb) in enumerate(tiles):
                    po, bpo = pget()
                    mm_tm(mT, B_mT, t * npart, npart, wo, bwo, 8, 0, 512, po, bpo)
                    xo = x_ap[:, half * 512:(half + 1) * 512]
                    em.op("dve", lambda e: e.tensor_tensor(out=xo, in0=po[0:npart, :], in1=xo, op=ALU.add),
                          reads=[bpo, xb], writes=[xb])
            ffn(tiles, npart, 2, "gu2", "d2", fT, B_fT)
            for t, (x_ap, xb) in enumerate(tiles):
                rms_rstd(x_ap, xb, npart, 1)
                em.op("dve", lambda e: e.scalar_tensor_tensor(out=x_ap, in0=x_ap, scalar=sm[0:npart, 1:2], in1=normout[0:npart, :],
                                                              op0=ALU.mult, op1=ALU.mult), reads=[xb, B_sm, B_cst], writes=[xb])
                out_dma(ydst(t), x_ap, [xb])

        n_st = NPRE + NMAIN if STAGE >= 2 else 1
        import os
        if os.environ.get('DBG_NST'):
            n_st = int(os.environ['DBG_NST'])
        for st in range(n_st):
            main = st >= NPRE and not os.environ.get('DBG_NOMAIN')
            T0 = st * ST
            tiles = [(xt[:, t, :], B_xt[t]) for t in range(4)]
            for t in range(4):
                em.dma("sp", xt[:, t, :], xseq[T0 + t * 128:T0 + (t + 1) * 128, :], writes=[B_xt[t]])
            ffn(tiles, 128, 0, "gu1", "d1", actT[0], B_actT[0])
            hT, B_hT = actT[1], B_actT[1]
            norm_T(tiles, 128, 1, hT, B_hT)
            cur = st % 2
            for hg in range(3):
                wk, bwk = wload("in", 0, 8, C_KA + hg * 512, 512)
                for j in range(4):
                    pk_, bpk_ = pget()
                    mm_fm(hT, B_hT, wk, bwk, 8, j * 128, 128, ST, pk_, bpk_)
                    hd = hg * 4 + j
                    if hd < 8:
                        copy_op(ev_eng(), KT01[:, hd, cur * ST:(cur + 1) * ST], pk_[:, :], [bpk_], [B_KT01[cur]])
                    else:
                        copy_op(ev_eng(), KT2[:, hd - 8, T0:T0 + ST], pk_[:, :], [bpk_], [B_KT2])
            for g in range(3):
                need_k = main and ((g == 2) or (g == 1 and st == n_st - 1) or (g == 0 and st == n_st - 1))
                if need_k:
                    wk, bwk = wload("in", 0, 8, C_KA + g * 512, 512)
                wv, bwv = wload("in", 0, 8, C_VA + g * 512, 512)
                for t in range(4):
                    mrow = (st - NPRE) * ST + t * 128
                    if g == 2:
                        dk_, ok = kv2_out, main
                        r0 = mrow
                    elif g == 1:
                        dk_, ok = kv1_out, main and st == n_st - 1
                        r0 = t * 128
                    else:
                        dk_, ok = kv0_out, main and st == n_st - 1 and t == 3
                        r0 = 0
                    if need_k and ok:
                        pk_, bpk_ = pget()
                        mm_tm(hT, B_hT, t * 128, 128, wk, bwk, 8, 0, 512, pk_, bpk_)
                        i = cnt["st"] % 2; cnt["st"] += 1
                        copy_op("act", stage_f[i][:, :], pk_[:, :], [bpk_], [B_stf[i]])
                        out_dma(dk_[r0:r0 + 128, 0, :], stage_f[i][:, :], [B_stf[i]])
                    pv_, bpv_ = pget()
                    mm_tm(hT, B_hT, t * 128, 128, wv, bwv, 8, 0, 512, pv_, bpv_)
                    i = cnt["sb"] % 2; cnt["sb"] += 1
                    copy_op("dve", stage_b[i][:, :], pv_[:, :], [bpv_], [B_stb[i]])
                    em.dma("pool", vscr[T0 + t * 128:T0 + (t + 1) * 128, g * 512:(g + 1) * 512], stage_b[i][:, :],
                           reads=[B_stb[i]], writes=(), owner=B_stb[i])
                    if ok:
                        i = cnt["st"] % 2; cnt["st"] += 1
                        copy_op("act", stage_f[i][:, :], pv_[:, :], [bpv_], [B_stf[i]])
                        out_dma(dk_[r0:r0 + 128, 1, :], stage_f[i][:, :], [B_stf[i]])
            wb_, bwb_ = wload("in", 0, 8, C_B, 16)
            for t in range(4):
                pb_, bpb_ = pget()
                mm_tm(hT, B_hT, t * 128, 128, wb_, bwb_, 8, 0, 16, pb_, bpb_)
                copy_op("act", ba_all[:, t, :], pb_[:, 0:16], [bpb_], [B_ba])
            beta_g(ba_all[:, :, :], beta_all[:, :, :], g_all[:, :, :], 128, [B_ba])
            for cg in range(6):
                wc, bwc = wload("in", 0, 8, C_QKV + cg * 512, 512)
                for j in range(4):
                    c = cg * 4 + j
                    if st < NPRE - 1 and c < 8:
                        continue
                    pc_, bpc_ = pget()
                    mm_fm(hT, B_hT, wc, bwc, 8, j * 128, 128, ST, pc_, bpc_)
                    conv_chunk(c, pc_, bpc_, ST)
            if STAGE >= 3:
                for t in range(4):
                    cs = slice(t * 128, (t + 1) * 128)
                    gdn_tile(128,
                             (lambda cs_: (lambda h: qbT[:, h, cs_]))(cs), (lambda cs_: (lambda h: kbT[:, h, cs_]))(cs),
                             (lambda cs_: (lambda h: vbT[:, h, cs_]))(cs), B_qb, B_kb, B_vb,
                             beta_all[:, t, :], g_all[:, t, :], B_bg,
                             lambda h: S_f[:, h, :], lambda h: S_b[:, h, :], B_S, main,
                             (lambda cs_: (lambda h: oTr[:, h, cs_]))(cs), B_oTr)
            if main and STAGE >= 4:
                gated_norm(ST, hT, B_hT)
                attention_st(st - NPRE, hT, B_hT)
            if main and STAGE >= 5:
                post_mixer(tiles, 128, hT, B_hT, actT[0], B_actT[0], actT[1], B_actT[1],
                           lambda t: y_out[(st - NPRE) * ST + t * 128:(st - NPRE) * ST + (t + 1) * 128, :])
        if STAGE >= 3:
            for h in range(8):
                out_dma(ssmp_out[h], S_f[:, h, :], [B_S[h]])
        for c in range(24):
            em.dma("sp", convp_out[:, c * 128:(c + 1) * 128].rearrange("t p -> p t"), halo[:, c, :], reads=[B_halo], writes=(),
                   owner=B_halo, allow_slow_non_contiguous=True)

        em.barrier()
        pstack.close()
        scb = sb("scb", [128, 24, NB_S, 7]); B_scb = Buf("scb")
        sample_branch = STAGE >= 6
        if sample_branch:
            xs_t = xt[0:NTS, 0, :]; B_xs = B_xt[0]
            stiles = [(xs_t, B_xs)]
            em.dma("sp", xs_t, xs_in, writes=[B_xs])
            sc_tm = sb("sc_tm", [48, 3072]); B_sct = Buf("sc_tm")
            em.dma("sp", sc_tm[:, :], sconv, writes=[B_sct])
            for c in range(24):
                pt_, bpt_ = pget()
                em.op("pe", lambda e: e.transpose(out=pt_[:, 0:48], in_=sc_tm[:, c * 128:(c + 1) * 128], identity=ident_f[0:48, 0:48]),
                      reads=[B_sct, B_cst], writes=[bpt_])
                copy_op(ev_eng(), scb[:, c, :, 0:3], pt_[:, 0:48].rearrange("p (b s) -> p b s", s=3), [bpt_], [B_scb])
            ffn(stiles, NTS, 0, "gu1", "d1", actT[0], B_actT[0])
            hT, B_hT = actT[1], B_actT[1]
            norm_T(stiles, NTS, 1, hT, B_hT)
            N = NTS
            QTs = sb("QsT", [128, 12, NTS], BF16)
            KsT = sb("KsT", [128, 12, NTS], BF16); B_KsT = Buf("KsT")
            for hg in range(3):
                wq, bwq = wload("in", 0, 8, C_QA + hg * 512, 512)
                for j in range(4):
                    pq_, bpq_ = pget()
                    mm_fm(hT, B_hT, wq, bwq, 8, j * 128, 128, N, pq_, bpq_)
                    copy_op(ev_eng(), QTs[:, hg * 4 + j, 0:N], pq_[:, 0:N], [bpq_], [B_QT], scale=128.0 ** -0.5)
            for hg in range(3):
                wk, bwk = wload("in", 0, 8, C_KA + hg * 512, 512)
                for j in range(4):
                    pk_, bpk_ = pget()
                    mm_fm(hT, B_hT, wk, bwk, 8, j * 128, 128, N, pk_, bpk_)
                    copy_op(ev_eng(), KsT[:, hg * 4 + j, 0:N], pk_[:, 0:N], [bpk_], [B_KsT])
            Vs_tm = sb("Vs_tm", [NTS, 1536]); B_Vs = Buf("Vs_tm")
            for g in range(3):
                wk, bwk = wload("in", 0, 8, C_KA + g * 512, 512)
                pk_, bpk_ = pget()
                mm_tm(hT, B_hT, 0, N, wk, bwk, 8, 0, 512, pk_, bpk_)
                i = cnt["st"] % 2; cnt["st"] += 1
                copy_op("act", stage_f[i][0:N, :], pk_[0:N, :], [bpk_], [B_stf[i]])
                out_dma(kvs_out[g][:, 0, :], stage_f[i][0:N, :], [B_stf[i]])
                wv, bwv = wload("in", 0, 8, C_VA + g * 512, 512)
                pv_, bpv_ = pget()
                mm_tm(hT, B_hT, 0, N, wv, bwv, 8, 0, 512, pv_, bpv_)
                copy_op("dve", Vs_tm[:, g * 512:(g + 1) * 512], pv_[0:N, :], [bpv_], [B_Vs])
                out_dma(kvs_out[g][:, 1, :], Vs_tm[:, g * 512:(g + 1) * 512], [B_Vs])
            for cg in range(6):
                wc, bwc = wload("in", 0, 8, C_QKV + cg * 512, 512)
                for j in range(4):
                    c = cg * 4 + j
                    pc_, bpc_ = pget()
                    mm_fm(hT, B_hT, wc, bwc, 8, j * 128, 128, N, pc_, bpc_)
                    conv_chunk(c, pc_, bpc_, N, sample=True)
            for c in range(24):
                for s_ in range(3):
                    em.dma("sp", convs_out[:, s_, c * 128:(c + 1) * 128].rearrange("b p -> p b"), scb[:, c, :, 4 + s_],
                           reads=[B_scb], writes=(), owner=B_scb, allow_slow_non_contiguous=True)
            if STAGE >= 7:
                wb_, bwb_ = wload("in", 0, 8, C_B, 16)
                Ss_f = [sb("Ss_f%d" % i, [128, 8, 128]) for i in range(2)]
                Ss_b = Ss_f
                B_Ss = [[Buf("Ss%d_%d" % (i, h)) for h in range(8)] for i in range(2)]
                ba_s = [sb("ba_s%d" % i, [4, 1, 16]) for i in range(2)]; B_bas = [Buf("bas0"), Buf("bas1")]
                bt_s = [sb("bt_s%d" % i, [4, 1, 8]) for i in range(2)]
                g_s = [sb("g_s%d" % i, [4, 1, 8]) for i in range(2)]
                make_hb(4)
                for b in range(NB_S):
                    i = b % 2
                    for h in range(8):
                        em.dma("sp", Ss_f[i][:, h, :], sssm[b, h], writes=[B_Ss[i][h]])
                    pb_, bpb_ = pget()
                    mm_tm(hT, B_hT, 4 * b, 4, wb_, bwb_, 8, 0, 16, pb_, bpb_)
                    copy_op("act", ba_s[i][:, 0, :], pb_[0:4, 0:16], [bpb_], [B_bas[i]])
                    beta_g(ba_s[i][:, :, :], bt_s[i][:, :, :], g_s[i][:, :, :], 4, [B_bas[i]])
                    cs = slice(4 * b, 4 * b + 4)
                    gdn_tile(4,
                             (lambda cs_: (lambda h: qbT[:, h, cs_]))(cs), (lambda cs_: (lambda h: kbT[:, h, cs_]))(cs),
                             (lambda cs_: (lambda h: vbT[:, h, cs_]))(cs), B_qb, B_kb, B_vb,
                             bt_s[i][:, 0, :], g_s[i][:, 0, :], B_bg,
                             (lambda i_: (lambda h: Ss_f[i_][:, h, :]))(i), (lambda i_: (lambda h: Ss_b[i_][:, h, :]))(i), B_Ss[i], True,
                             (lambda cs_: (lambda h: oTr[:, h, cs_]))(cs), B_oTr)
                    for h in range(8):
                        out_dma(ssms_out[b, h], Ss_f[i][:, h, :], [B_Ss[i][h]])
                gated_norm(N, hT, B_hT)
            if STAGE >= 8:
                ckt = [sb("ckt%d" % i, [128, 1024]) for i in range(2)]; B_ckt = [Buf("ckt%d" % i) for i in range(2)]
                kTs = [sb("kTs%d" % i, [128, 4, 128], BF16) for i in range(2)]; B_kTs = [Buf("kTs0"), Buf("kTs1")]
                vnew = [sb("vnew0", [4, 1536])] * 2; B_vnew = [Buf("vnew0")] * 2
                pTs = [sb("pTs%d" % i, [128, 16]) for i in range(4)]; B_pTs = [Buf("pTs%d" % i) for i in range(4)]
                scc = {"c": 0, "k": 0, "p": 0}
                ones4 = ones_f
                for b in range(NB_S):
                    vi = b % 2
                    em.dma("sp", vnew[vi][:, :], Vs_tm[4 * b:4 * b + 4, :], reads=[B_Vs], writes=[B_vnew[vi]])
                    pN, bpN = pget(hold=True)
                    pD, bpD = pget(hold=True)
                    first = True
                    tl = []
                    tl.append((0, 0, ck0[b, :, :]))
                    for r in range(4):
                        tl.append((1, 1 + r, ck1[b, sl(r, 128, 4), :]))
                    for r in range(4):
                        tl.append((2, 5 + r, ck2[b, sl(r, 128, 16), :]))
                    for (g, bi, src) in tl:
                        ci = scc["c"] % 2; scc["c"] += 1
                        em.dma("sp", ckt[ci][:, :], src, writes=[B_ckt[ci]])
                        ki = scc["k"] % 2; scc["k"] += 1
                        for h in range(4):
                            ptr, bptr = pget()
                            em.op("pe", lambda e: e.transpose(out=ptr[:, 0:128], in_=ckt[ci][:, h * 128:(h + 1) * 128], identity=ident_f),
                                  reads=[B_ckt[ci], B_cst], writes=[bptr])
                            copy_op(ev_eng(), kTs[ki][:, h, :], ptr[:, 0:128], [bptr], [B_kTs[ki]])
                        pS, bpS = pget()
                        for h in range(4):
                            em.op("pe", lambda e: e.matmul(pS[:, h * 4:(h + 1) * 4], lhsT=kTs[ki][:, h, :], rhs=QTs[:, g * 4 + h, 4 * b:4 * b + 4],
                                                           start=True, stop=True), reads=[B_kTs[ki], B_QT], writes=[bpS])
                        pi = scc["p"] % 4; scc["p"] += 1
                        em.op("dve", lambda e: e.tensor_tensor(out=pTs[pi][:, :], in0=pS[:, 0:16], in1=sbias[:, bi, :], op=ALU.add),
                              reads=[bpS, B_cst], writes=[B_pTs[pi]])
                        em.op("act", lambda e: e.activation(out=pTs[pi][:, :], in_=pTs[pi][:, :], func=AF.Exp),
                              reads=[B_pTs[pi]], writes=[B_pTs[pi]])
                        for h in range(4):
                            em.op("pe", lambda e: e.matmul(pN[:, h * 4:(h + 1) * 4], lhsT=ckt[ci][:, 512 + h * 128:512 + (h + 1) * 128],
                                                           rhs=pTs[pi][:, h * 4:(h + 1) * 4], start=(first and h == 0), stop=False),
                                  reads=[B_ckt[ci], B_pTs[pi]], writes=[bpN])
                        em.op("pe", lambda e: e.matmul(pD[:, 0:16], lhsT=ones_f, rhs=pTs[pi][:, :], start=first, stop=False),
                              reads=[B_cst, B_pTs[pi]], writes=[bpD])
                        first = False
                    for g in range(3):
                        pS, bpS = pget()
                        for h in range(4):
                            em.op("pe", lambda e: e.matmul(pS[0:4, h * 4:(h + 1) * 4], lhsT=KsT[:, g * 4 + h, 4 * b:4 * b + 4],
                                                           rhs=QTs[:, g * 4 + h, 4 * b:4 * b + 4], start=True, stop=True),
                                  reads=[B_KsT, B_QT], writes=[bpS])
                        pi = scc["p"] % 4; scc["p"] += 1
                        em.op("dve", lambda e: e.tensor_tensor(out=pTs[pi][0:4, :], in0=pS[0:4, 0:16], in1=sbiasn[:, g, :], op=ALU.add),
                              reads=[bpS, B_cst], writes=[B_pTs[pi]])
                        em.op("act", lambda e: e.activation(out=pTs[pi][0:4, :], in_=pTs[pi][0:4, :], func=AF.Exp),
                              reads=[B_pTs[pi]], writes=[B_pTs[pi]])
                        for h in range(4):
                            em.op("pe", lambda e: e.matmul(pN[:, h * 4:(h + 1) * 4], lhsT=vnew[vi][:, g * 512 + h * 128:g * 512 + (h + 1) * 128],
                                                           rhs=pTs[pi][0:4, h * 4:(h + 1) * 4], start=False, stop=(g == 2 and h == 3)),
                                  reads=[B_vnew[vi], B_pTs[pi]], writes=[bpN])
                        em.op("pe", lambda e: e.matmul(pD[:, 0:16], lhsT=ones_f[0:4, :], rhs=pTs[pi][0:4, :], start=False, stop=(g == 2)),
                              reads=[B_cst, B_pTs[pi]], writes=[bpD])
                    dsl = denT[:, :, 4 * b:4 * b + 4]
                    em.op("dve", lambda e: e.reciprocal(out=dsl, in_=pD[:, 0:16].rearrange("p (h s) -> p h s", s=4)),
                          reads=[bpD], writes=[B_nd])
                    em.op("dve", lambda e: e.tensor_tensor(out=oaT[:, :, 4 * b:4 * b + 4], in0=pN[:, 0:16].rearrange("p (h s) -> p h s", s=4),
                                                           in1=dsl, op=ALU.mult), reads=[bpN, B_nd], writes=[B_oaT])
                    prelease(bpN); prelease(bpD)
            if STAGE >= 9:
                post_mixer(stiles, NTS, hT, B_hT, actT[0], B_actT[0], actT[1], B_actT[1], lambda t: ys_out[:, :])

        allb = [B_scb] + B_xt + B_stf + B_stb
        if sample_branch and STAGE >= 7:
            allb += B_Ss[0] + B_Ss[1]
        if sample_branch:
            allb += [B_Vs]
        em.finish("sp", allb)
    print("[kernel] instructions emitted:", em.ninst, "dma sems:", em.n_dsem)
    return nc


def _slopes():
    return np.exp2(-8.0 * np.arange(1, 13, dtype=np.float64) / 12).astype(np.float64)


def _const_tables():
    sl_ = _slopes()
    cst = np.zeros((128, 5, 128), np.float32)
    i = np.arange(128)
    cst[:, 0, :] = np.eye(128)
    cst[:, 1, :] = 1.0
    cst[:, 2, :] = (i[:, None] <= i[None, :])
    cst[:, 3, :] = np.where(i[:, None] > i[None, :], 0.0, NEG)
    cst[:, 4, :] = np.where(i[None, :] >= i[:, None], 0.0, NEG)
    ab01 = np.zeros((128, 8, 2, 128), np.float32)
    k = i[:, None]; q = i[None, :]
    for h in range(8):
        dil = 1 if h < 4 else 4
        s = sl_[h]
        dprev = 128 + q - k
        ab01[:, h, 0, :] = np.where(dprev <= 128, -s * dil * dprev, NEG)
        dcur = q - k
        ab01[:, h, 1, :] = np.where(dcur >= 0, -s * dil * dcur, NEG)
    ab2a = np.zeros((128, 4, 32), np.float32)
    ab2b = np.zeros((32, 4, 32), np.float32)
    a = np.arange(128)[:, None]; qi = np.arange(32)[None, :]; c = np.arange(32)[:, None]
    for h in range(4):
        s = sl_[8 + h]
        d = 128 + qi - a
        ab2a[:, h, :] = np.where(d <= 128, -s * 16 * d, NEG)
        d2 = qi - c
        ab2b[:, h, :] = np.where(d2 >= 0, -s * 16 * d2, NEG)
    sbias = np.full((128, 9, 16), NEG, np.float32)
    p = np.arange(128)
    for h in range(4):
        for s_ in range(4):
            d = 128 + s_ - p
            sbias[:, 0, h * 4 + s_] = np.where(p >= s_, -sl_[h] * d, NEG)
            sbias[:, 1 + s_, h * 4 + s_] = -sl_[4 + h] * 4 * (128 - p)
            sbias[:, 5 + s_, h * 4 + s_] = -sl_[8 + h] * 16 * (128 - p)
    sbn = np.full((4, 3, 16), NEG, np.float32)
    for h in range(4):
        for s_ in range(4):
            for sp in range(4):
                if sp <= s_:
                    sbn[sp, 0, h * 4 + s_] = -sl_[h] * (s_ - sp)
                if sp == s_:
                    sbn[sp, 1, h * 4 + s_] = 0.0
                    sbn[sp, 2, h * 4 + s_] = 0.0
    return cst, ab01, ab2a, ab2b, sbias, sbn


_NC_CACHE = {}


def kernel(x_prompt, x_sample, cache_kv_w128, cache_kv_w512, cache_kv_w2048, state_conv, state_ssm,
           norm_ffn1, w_ffn1_gu, w_ffn1_down, norm_mix, w_in, conv_w, gdn_a_log, gdn_dt_bias, gdn_norm,
           w_proj_a, w_proj_b, w_out, norm_ffn2, w_ffn2_gu, w_ffn2_down, norm_out):
    f = lambda a: np.ascontiguousarray(np.asarray(a, dtype=np.float32))
    x_prompt = f(x_prompt); x_sample = f(x_sample)
    if "nc" not in _NC_CACHE:
        _NC_CACHE["nc"] = build_program()
    nc = _NC_CACHE["nc"]
    cst, ab01, ab2a, ab2b, sbias, sbn = _const_tables()
    normT = np.stack([f(norm_ffn1)[0], f(norm_mix)[0], f(norm_ffn2)[0]], 0).reshape(3, 8, 128).transpose(2, 0, 1)
    normout = np.broadcast_to(f(norm_out)[None, :], (128, D))
    convw = f(conv_w)[0].reshape(4, 24, 128).transpose(2, 1, 0)
    gnorm = f(gdn_norm)[0].reshape(128, 1)
    alog = np.broadcast_to(f(gdn_a_log)[0][None, :], (128, 8))
    dtb = np.broadcast_to(f(gdn_dt_bias)[0][None, :], (128, 8))
    shared = {
        "w_gu1": f(w_ffn1_gu)[0], "w_d1": f(w_ffn1_down)[0], "w_in": f(w_in)[0], "w_pa": f(w_proj_a)[0],
        "w_pb": f(w_proj_b)[0], "w_out": f(w_out)[0], "w_gu2": f(w_ffn2_gu)[0], "w_d2": f(w_ffn2_down)[0],
        "normT": f(normT), "normout": f(normout), "convw": f(convw), "gnorm": f(gnorm), "alog": f(alog), "dtb": f(dtb),
        "cst_f": cst, "ab01": ab01, "ab2a": ab2a, "ab2b": ab2b, "sbias": sbias, "sbiasn": sbn,
    }
    ck0 = f(cache_kv_w128)[0].reshape(128, 128, 1024)
    ck1 = f(cache_kv_w512)[0].reshape(128, 512, 1024)
    ck2 = f(cache_kv_w2048)[0].reshape(128, 2048, 1024)
    sconv = f(state_conv)[0]
    sssm = f(state_ssm)[0]
    in_maps = []
    for c in range(8):
        b, hf = c // 2, c % 2
        if hf == 0:
            xseq = np.concatenate([np.zeros((2048, D), np.float32), x_prompt[b, 0:2048]], 0)
        else:
            xseq = x_prompt[b]
        pm = np.zeros((128, 8), np.float32)
        if hf == 0:
            a = np.arange(128)
            pm[:, 0] = NEG
            pm[:, 1] = np.where(a < 96, NEG, 0.0)
            pm[:, 2] = np.where(a < 64, NEG, 0.0)
            pm[:, 3] = np.where(a < 32, NEG, 0.0)
        m = dict(shared)
        m.update({
            "xseq": f(xseq), "xs": f(x_sample[16 * c:16 * c + 16].reshape(64, D)),
            "ck0": f(ck0[16 * c:16 * c + 16]), "ck1": f(ck1[16 * c:16 * c + 16]), "ck2": f(ck2[16 * c:16 * c + 16, 0:(64 if os.environ.get('DBG_SMALL') else 2048)]),
            "sconv": f(sconv[16 * c:16 * c + 16].reshape(48, 3072)), "sssm": f(sssm[16 * c:16 * c + 16]),
            "pmask": pm,
        })
        in_maps.append(m)
    res = run_bass_kernel_spmd(nc, in_maps, core_ids=list(range(8)))
    R = res.results
    y_prompt = np.zeros((4, 4096, D), np.float32)
    for c in range(8):
        b, hf = c // 2, c % 2
        y_prompt[b, hf * 2048:(hf + 1) * 2048] = R[c]["y"]
    y_sample = np.concatenate([R[c]["ys"].reshape(16, 4, D) for c in range(8)], 0)
    kvp = []
    for nm, W in (("kv0", 128), ("kv1", 512), ("kv2", 2048)):
        kvp.append(np.stack([R[2 * b + 1][nm].reshape(W, 2, 4, 128) for b in range(4)], 0)[None])
    convp = np.stack([R[2 * b + 1]["convp"] for b in range(4)], 0)[None]
    ssmp = np.stack([R[2 * b + 1]["ssmp"] for b in range(4)], 0)[None]
    kvs = []
    for g in range(3):
        kvs.append(np.concatenate([R[c]["kvs%d" % g].reshape(16, 4, 2, 4, 128) for c in range(8)], 0)[None])
    convs = np.concatenate([R[c]["convs"] for c in range(8)], 0)[None]
    ssms = np.concatenate([R[c]["ssms"] for c in range(8)], 0)[None]
    outs = (y_prompt, y_sample, kvp[0], kvp[1], kvp[2], convp, ssmp, kvs[0], kvs[1], kvs[2], convs, ssms)
    return tuple(np.ascontiguousarray(o.astype(np.float32)) for o in outs)
```

```python
import math
import os
from contextlib import ExitStack
import numpy as np
import concourse.bass as bass
import concourse.mybir as mybir
from concourse.bass_utils import run_bass_kernel_spmd

F32 = mybir.dt.float32
BF16 = mybir.dt.bfloat16
AF = mybir.ActivationFunctionType
ALU = mybir.AluOpType

NEG = -30000.0
D = 1024
DFF = 2816
INW = 10768
C_QA, C_KA, C_VA, C_QKV, C_Z, C_B, C_G = 0, 1536, 3072, 4608, 7680, 8704, 8720
EPS = 1e-6
ST = 512
NPRE = 4
NMAIN = 4
SEQX = 4096
NB_S = 16
NTS = 64

STAGE = 9


def sl(start, n, step=1):
    return slice(start, start + (n - 1) * step + 1, step)


class Buf:
    __slots__ = ("name", "w", "r", "dsem", "dcnt", "excl")

    def __init__(self, name="", excl=False):
        self.excl = excl
        self.name = name
        self.w = None
        self.r = []
        self.dsem = None
        self.dcnt = 0


class Em:
    def __init__(self, nc):
        self.nc = nc
        self.engs = {"pe": nc.tensor, "act": nc.scalar, "dve": nc.vector,
                     "pool": nc.gpsimd, "sp": nc.sync}
        self.sem = {k: nc.alloc_semaphore("s_" + k) for k in self.engs}
        self.cnt = {k: 0 for k in self.engs}
        self.waited = {k: {} for k in self.engs}
        self.n_dsem = 0
        self.ninst = 0
        self.dbufs = []

    def _waits(self, e, reads, writes):
        deps = {}
        for b in reads:
            if b.w is not None:
                s, v = b.w
                if deps.get(id(s), (None, 0))[1] < v:
                    deps[id(s)] = (s, v)
        for b in writes:
            if b.w is not None:
                s, v = b.w
                if deps.get(id(s), (None, 0))[1] < v:
                    deps[id(s)] = (s, v)
            for (s, v) in b.r:
                if deps.get(id(s), (None, 0))[1] < v:
                    deps[id(s)] = (s, v)
        eng = self.engs[e]
        wd = self.waited[e]
        for key, (s, v) in deps.items():
            if e == "pe" and s is self.sem["pe"]:
                continue
            if wd.get(key, 0) >= v:
                continue
            eng.wait_ge(s, v)
            wd[key] = v
            self.ninst += 1

    def _post(self, ev, reads, writes):
        for b in reads:
            b.r.append(ev)
            if len(b.r) > 24:
                mx = {}
                for (s, v) in b.r:
                    if mx.get(id(s), (None, 0))[1] < v:
                        mx[id(s)] = (s, v)
                b.r = list(mx.values())
        for b in writes:
            b.w = ev
            b.r = []

    def op(self, e, fn, reads=(), writes=(), inc=True):
        if any(b.excl for b in reads):
            writes = list(writes) + [b for b in reads if b.excl]
            reads = [b for b in reads if not b.excl]
        self._waits(e, reads, writes)
        ins = fn(self.engs[e])
        if inc:
            self.cnt[e] += 1
            ev = (self.sem[e], self.cnt[e])
            ins.then_inc(self.sem[e], 1)
        else:
            ev = (self.sem[e], self.cnt[e] + 1)
        self.ninst += 1
        self._post(ev, reads, writes)
        return ins

    def dma(self, e, out, in_, reads=(), writes=(), owner=None, **kw):
        self._waits(e, reads, writes)
        if owner is None:
            owner = writes[0] if len(writes) else reads[0]
        if owner.dsem is None:
            owner.dsem = self.nc.alloc_semaphore("d%d" % self.n_dsem)
            self.n_dsem += 1
            self.dbufs.append(owner)
        ins = self.engs[e].dma_start(out=out, in_=in_, **kw)
        owner.dcnt += 16
        ins.then_inc(owner.dsem, 16)
        ev = (owner.dsem, owner.dcnt)
        self.ninst += 1
        self._post(ev, reads, writes)
        return ins

    def finish(self, e, bufs):
        self._waits(e, (), bufs)

    def barrier(self, dma_bufs=None):
        if dma_bufs is None:
            dma_bufs = list(self.dbufs)
        for e in ("pe", "act", "dve", "pool", "sp"):
            eng = self.engs[e]
            wd = self.waited[e]
            for o in ("pe", "act", "dve", "pool"):
                if o == e or self.cnt[o] == 0:
                    continue
                if wd.get(id(self.sem[o]), 0) < self.cnt[o]:
                    eng.wait_ge(self.sem[o], self.cnt[o])
                    wd[id(self.sem[o])] = self.cnt[o]
                    self.ninst += 1
            for b in dma_bufs:
                if b.dsem is not None and b.dcnt > 0 and wd.get(id(b.dsem), 0) < b.dcnt:
                    eng.wait_ge(b.dsem, b.dcnt)
                    wd[id(b.dsem)] = b.dcnt
                    self.ninst += 1


def build_program():
    nc = bass.Bass("TRN2", target_bir_lowering=False)
    em = Em(nc)

    def din(name, shape, dt=F32):
        return nc.dram_tensor(name, list(shape), dt, kind="ExternalInput").ap()

    def dout(name, shape, dt=F32):
        return nc.dram_tensor(name, list(shape), dt, kind="ExternalOutput").ap()

    def dscr(name, shape, dt=BF16):
        return nc.dram_tensor(name, list(shape), dt, kind="Internal").ap()

    xseq = din("xseq", [SEQX, D])
    xs_in = din("xs", [NTS, D])
    ck0 = din("ck0", [NB_S, 128, 1024])
    ck1 = din("ck1", [NB_S, 512, 1024])
    import os
    CK2R = 64 if os.environ.get('DBG_SMALL') else 2048
    ck2 = din("ck2", [NB_S, CK2R, 1024])
    sconv = din("sconv", [NB_S * 3, 3072])
    sssm = din("sssm", [NB_S, 8, 128, 128])
    w_gu1 = din("w_gu1", [D, 2 * DFF]); w_d1 = din("w_d1", [DFF, D])
    w_in = din("w_in", [D, INW])
    w_pa = din("w_pa", [512, D]); w_pb = din("w_pb", [D, D]); w_out = din("w_out", [D, D])
    w_gu2 = din("w_gu2", [D, 2 * DFF]); w_d2 = din("w_d2", [DFF, D])
    normT_in = din("normT", [128, 3, 8])
    normout_in = din("normout", [128, D])
    convw_in = din("convw", [128, 24, 4])
    gnorm_in = din("gnorm", [128, 1])
    alog_in = din("alog", [128, 8]); dtb_in = din("dtb", [128, 8])
    pmask_in = din("pmask", [128, 8])
    cst_f_in = din("cst_f", [128, 5, 128])
    ab01_in = din("ab01", [128, 8, 2, 128])
    ab2a_in = din("ab2a", [128, 4, 32])
    ab2b_in = din("ab2b", [32, 4, 32])
    sb_in = din("sbias", [128, 9, 16])
    sbn_in = din("sbiasn", [4, 3, 16])

    y_out = dout("y", [NMAIN * ST, D])
    ys_out = dout("ys", [NTS, D])
    kv0_out = dout("kv0", [128, 2, 512]); kv1_out = dout("kv1", [512, 2, 512]); kv2_out = dout("kv2", [2048, 2, 512])
    convp_out = dout("convp", [3, 3072])
    ssmp_out = dout("ssmp", [8, 128, 128])
    kvs_out = [dout("kvs%d" % g, [NTS, 2, 512]) for g in range(3)]
    convs_out = dout("convs", [NB_S, 3, 3072])
    ssms_out = dout("ssms", [NB_S, 8, 128, 128])

    gu_pan = [(fg * 128, min(4, 22 - fg) * 128) for fg in range(0, 22, 4)]
    gu_pan = gu_pan + [(DFF + c0, n) for (c0, n) in gu_pan]
    in_pan = ([(C_QA + g * 512, 512) for g in range(3)] + [(C_KA + g * 512, 512) for g in range(3)]
              + [(C_VA + g * 512, 512) for g in range(3)] + [(C_QKV + g * 512, 512) for g in range(6)]
              + [(C_Z + g * 512, 512) for g in range(2)] + [(C_B, 16)] + [(C_G + g * 512, 512) for g in range(4)])
    two = [(0, 512), (512, 512)]
    PAN = {"gu1": gu_pan, "gu2": gu_pan, "d1": two, "d2": two, "in": in_pan, "pa": two, "pb": two, "out": two}
    KCS = {"gu1": 8, "gu2": 8, "d1": 22, "d2": 22, "in": 8, "pa": 4, "pb": 8, "out": 8}
    POFF = {}
    scr1d = {}
    for nm_ in PAN:
        off = 0
        for (c0_, n_) in PAN[nm_]:
            POFF[(nm_, c0_, n_)] = off
            off += KCS[nm_] * 128 * n_
        scr1d[nm_] = dscr("s_" + nm_, [off])
    vscr = dscr("vscr", [SEQX, 1536])
    B_scr = {}

    def sb(name, shape, dt=F32):
        return nc.alloc_sbuf_tensor("sb_" + name, list(shape), dt)

    pstack = ExitStack()

    def sbp(name, shape, dt=F32):
        return pstack.enter_context(nc.sbuf_tensor("sb_" + name, list(shape), dt))

    cst_f = sb("cst_f", [128, 5, 128]); B_cst = Buf("cst")
    ident_f = cst_f[:, 0, :]; ones_f = cst_f[:, 1, :]; triU = cst_f[:, 2, :]
    nm_strict = cst_f[:, 3, :]; nm_inclT = cst_f[:, 4, :]
    cst_b = sb("cst_b", [128, 2, 128], BF16)
    ident_b = cst_b[:, 0, :]; ones_b = cst_b[:, 1, :]
    normT = sb("normT", [128, 3, 8]); normout = sb("normout", [128, D])
    convw = sb("convw", [128, 24, 4]); gnorm = sb("gnorm", [128, 1])
    alog = sb("alog", [128, 8]); dtb = sb("dtb", [128, 8]); negA = sb("negA", [128, 8])
    pmask = sb("pmask", [128, 8])
    epsc = sb("epsc", [128, 1])
    sbias = sb("sbias", [128, 9, 16]); sbiasn = sb("sbiasn", [4, 3, 16])

    with nc.Block() as block:
        for (dst, src) in [(cst_f, cst_f_in), (normT, normT_in), (normout, normout_in), (convw, convw_in),
                           (gnorm, gnorm_in), (alog, alog_in), (dtb, dtb_in), (pmask, pmask_in),
                           (sbias, sb_in), (sbiasn, sbn_in)]:
            em.dma("sp", dst[:], src, writes=[B_cst])
        em.op("pool", lambda e: e.memset(epsc[:], EPS), writes=[B_cst])
        em.op("dve", lambda e: e.tensor_copy(out=cst_b[:, 0, :], in_=ident_f), reads=[B_cst], writes=[B_cst])
        em.op("dve", lambda e: e.tensor_copy(out=cst_b[:, 1, :], in_=ones_f), reads=[B_cst], writes=[B_cst])
        em.op("act", lambda e: e.activation(out=negA[:], in_=alog[:], func=AF.Exp), reads=[B_cst], writes=[B_cst])
        em.op("dve", lambda e: e.tensor_scalar(out=negA[:], in0=negA[:], scalar1=-1.0, scalar2=None, op0=ALU.mult),
              reads=[B_cst], writes=[B_cst])

        def scr_view(nm, c0, n):
            o = POFF[(nm, c0, n)]
            kc = KCS[nm]
            return scr1d[nm][o:o + kc * 128 * n].rearrange("(p k c) -> p k c", p=128, k=kc, c=n)

        for (src, nm) in [(w_gu1, "gu1"), (w_d1, "d1"), (w_in, "in"), (w_pa, "pa"),
                          (w_pb, "pb"), (w_out, "out"), (w_gu2, "gu2"), (w_d2, "d2")]:
            bb = Buf("scr_" + nm)
            B_scr[nm] = bb
            for (c0_, n_) in PAN[nm]:
                em.dma("pool", scr_view(nm, c0_, n_), src[:, c0_:c0_ + n_].rearrange("(k p) c -> p k c", p=128),
                       writes=(), owner=bb, max_dma_last_dim=4096)
            bb.w = (bb.dsem, bb.dcnt)

        NPS = 6
        psf = [nc.alloc_psum_tensor("psf%d" % i, [128, 512], F32) for i in range(NPS)]
        B_psf = [Buf("psf%d" % i, excl=True) for i in range(NPS)]
        psb = [nc.alloc_psum_tensor("psb%d" % i, [128, 1024], BF16) for i in range(2)]
        B_psb = [Buf("psb%d" % i, excl=True) for i in range(2)]
        rr = {"f": 0, "b": 0, "ev": 0}

        held = set()

        def pget(hold=False):
            while True:
                i = rr["f"] % NPS
                rr["f"] += 1
                if i not in held:
                    break
            if hold:
                held.add(i)
            return psf[i], B_psf[i]

        def prelease(bp):
            held.discard(B_psf.index(bp))

        def pgetb():
            i = rr["b"] % 2
            rr["b"] += 1
            return psb[i], B_psb[i]

        def ev_eng():
            rr["ev"] += 1
            return "act" if rr["ev"] % 2 else "dve"

        def copy_op(eng, out, in_, reads, writes, scale=None):
            if eng == "act":
                if scale is None:
                    em.op("act", lambda e: e.activation(out=out, in_=in_, func=AF.Copy), reads=reads, writes=writes)
                else:
                    em.op("act", lambda e: e.activation(out=out, in_=in_, func=AF.Copy, scale=float(scale)),
                          reads=reads, writes=writes)
            else:
                if scale is None:
                    em.op(eng, lambda e: e.tensor_copy(out=out, in_=in_), reads=reads, writes=writes)
                else:
                    em.op(eng, lambda e: e.tensor_scalar(out=out, in0=in_, scalar1=float(scale), scalar2=None,
                                                         op0=ALU.mult), reads=reads, writes=writes)

        NSLOT = 3
        wslots = [sb("wslot%d" % i, [128, 8, 512], BF16) for i in range(NSLOT)]
        B_ws = [Buf("ws%d" % i) for i in range(NSLOT)]
        wst = {"i": 0, "pref": {}}

        def wload(nm, k0, nk, c0, ncols):
            key = (nm, k0, nk, c0, ncols)
            if key in wst["pref"]:
                return wst["pref"].pop(key)
            i = wst["i"] % NSLOT
            wst["i"] += 1
            src = scr_view(nm, c0, ncols)[:, k0:k0 + nk, :]
            em.dma("sp", wslots[i][:, 0:nk, 0:ncols], src, reads=[B_scr[nm]], writes=[B_ws[i]])
            return wslots[i], B_ws[i]

        def wprefetch(nm, k0, nk, c0, ncols):
            key = (nm, k0, nk, c0, ncols)
            if key not in wst["pref"]:
                wst["pref"][key] = wload(nm, k0, nk, c0, ncols)

        xt = sb("xt", [128, 4, D]); B_xt = [Buf("xt%d" % i) for i in range(4)]
        actT = [sb("actT%d" % i, [128, 8, ST], BF16) for i in range(2)]
        B_actT = [Buf("actT%d" % i) for i in range(2)]
        big = sb("big", [128, 24, ST], BF16); B_big = Buf("big")
        hidT = big[:, 0:22, :]; B_hid = B_big
        xn_tmp = sb("xn_tmp", [128, D], BF16); B_xn = Buf("xn")
        sm = sb("sm", [128, 8]); B_sm = Buf("sm")
        sp_y = sb("sp_y", [128, 4, 8]); sp_p = sb("sp_p", [128, 4, 8]); sp_m = sb("sp_m", [128, 4, 8]); B_spt = Buf("sp_tmp")
        junk = xn_tmp; B_junk = B_xn

        def rms_rstd(x_ap, xb, npart, col):
            em.op("act", lambda e: e.activation(out=junk[0:npart, :], in_=x_ap, func=AF.Square,
                                                accum_out=sm[0:npart, col:col + 1]),
                  reads=[xb], writes=[B_junk, B_sm])
            em.op("act", lambda e: e.activation(out=sm[0:npart, col:col + 1], in_=sm[0:npart, col:col + 1], func=AF.Sqrt,
                                                bias=epsc[0:npart, 0:1], scale=1.0 / D), reads=[B_sm, B_cst], writes=[B_sm])
            em.op("dve", lambda e: e.reciprocal(out=sm[0:npart, col:col + 1], in_=sm[0:npart, col:col + 1]),
                  reads=[B_sm], writes=[B_sm])

        def norm_T(tiles, npart, nidx, dstT, B_dst):
            for t, (x_ap, xb) in enumerate(tiles):
                rms_rstd(x_ap, xb, npart, 0)
                em.op("dve", lambda e: e.tensor_scalar(out=xn_tmp[0:npart, :], in0=x_ap, scalar1=sm[0:npart, 0:1],
                                                       scalar2=None, op0=ALU.mult),
                      reads=[xb, B_sm], writes=[B_xn])
                pt, bpt = pgetb()
                for kc in range(8):
                    em.op("pe", lambda e: e.transpose(out=pt[:, kc * 128:kc * 128 + npart],
                                                      in_=xn_tmp[0:npart, kc * 128:(kc + 1) * 128],
                                                      identity=ident_b[0:npart, 0:npart]),
                          reads=[B_xn, B_cst], writes=[bpt])
                for kc in range(8):
                    eng = ev_eng()
                    o = dstT[:, kc, t * npart:(t + 1) * npart]
                    i_ = pt[:, kc * 128:kc * 128 + npart]
                    if eng == "act":
                        em.op("act", lambda e: e.activation(out=o, in_=i_, func=AF.Copy,
                                                            scale=normT[:, nidx, kc:kc + 1]),
                              reads=[bpt, B_cst], writes=[B_dst])
                    else:
                        em.op("dve", lambda e: e.tensor_scalar(out=o, in0=i_, scalar1=normT[:, nidx, kc:kc + 1],
                                                               scalar2=None, op0=ALU.mult),
                              reads=[bpt, B_cst], writes=[B_dst])

        def mm_fm(srcT, B_src, wsl, B_w, nk, col0, ncol, N, ps, bps, first=True, last=True, k_off=0):
            for kc in range(nk):
                em.op("pe", lambda e: e.matmul(ps[0:ncol, 0:N], lhsT=wsl[:, kc, col0:col0 + ncol],
                                               rhs=srcT[:, k_off + kc, 0:N],
                                               start=(first and kc == 0), stop=(last and kc == nk - 1)),
                      reads=[B_src, B_w], writes=[bps], inc=(kc == nk - 1))

        def mm_tm(srcT, B_src, tok0, ntok, wsl, B_w, nk, col0, ncol, ps, bps, first=True, last=True, k_off=0):
            for kc in range(nk):
                em.op("pe", lambda e: e.matmul(ps[0:ntok, 0:ncol], lhsT=srcT[:, k_off + kc, tok0:tok0 + ntok],
                                               rhs=wsl[:, kc, col0:col0 + ncol],
                                               start=(first and kc == 0), stop=(last and kc == nk - 1)),
                      reads=[B_src, B_w], writes=[bps], inc=(kc == nk - 1))

        def ffn(tiles, npart, nidx, gu, dn, srcT, B_srcT):
            N = npart * len(tiles)
            norm_T(tiles, npart, nidx, srcT, B_srcT)
            for fg in range(0, 22, 4):
                nf = min(4, 22 - fg)
                wg, bwg = wload(gu, 0, 8, fg * 128, nf * 128)
                wu, bwu = wload(gu, 0, 8, DFF + fg * 128, nf * 128)
                for j in range(nf):
                    fh = fg + j
                    pg, bpg = pget()
                    mm_fm(srcT, B_srcT, wg, bwg, 8, j * 128, 128, N, pg, bpg)
                    pu, bpu = pget()
                    mm_fm(srcT, B_srcT, wu, bwu, 8, j * 128, 128, N, pu, bpu)
                    sgi = fh % 2
                    em.op("act", lambda e: e.activation(out=sg_tmp[sgi][:, 0:N], in_=pg[:, 0:N], func=AF.Silu),
                          reads=[bpg], writes=[B_sg[sgi]])
                    em.op("dve", lambda e: e.tensor_tensor(out=hidT[:, fh, 0:N], in0=sg_tmp[sgi][:, 0:N],
                                                           in1=pu[:, 0:N], op=ALU.mult),
                          reads=[B_sg[sgi], bpu], writes=[B_hid])
            for half in range(2):
                wds = [wload(dn, kg * 8, min(8, 22 - kg * 8), half * 512, 512) for kg in range(3)]
                for t, (x_ap, xb) in enumerate(tiles):
                    po, bpo = pget()
                    for kg in range(3):
                        nk = min(8, 22 - kg * 8)
                        mm_tm(hidT, B_hid, t * npart, npart, wds[kg][0], wds[kg][1], nk, 0, 512, po, bpo,
                              first=(kg == 0), last=(kg == 2), k_off=kg * 8)
                    xo = x_ap[:, half * 512:(half + 1) * 512]
                    em.op("dve", lambda e: e.scalar_tensor_tensor(out=xo, in0=po[0:npart, :], scalar=0.5, in1=xo,
                                                                  op0=ALU.mult, op1=ALU.add),
                          reads=[bpo, xb], writes=[xb])

        QT = sb("QT", [128, 4, ST], BF16); B_QT = Buf("QT")
        denT = big[:, 8:16, :].bitcast(F32).rearrange("p (a b) n -> p a (b n)", b=2); B_nd = B_big
        numT = big[:, 16:24, :].bitcast(F32).rearrange("p (a b) n -> p a (b n)", b=2)
        oaT = sb("oaT", [128, 4, ST], BF16); B_oaT = Buf("oaT")
        qbT = big[:, 0:8, :]; kbT = big[:, 8:16, :]; vbT = big[:, 16:24, :]
        B_qb = B_big; B_kb = B_big; B_vb = B_big
        ztmp = sb("ztmp", [128, ST], BF16); B_zt = Buf("ztmp")
        oTr = actT[0]; B_oTr = B_actT[0]
        obT = actT[0]; B_obT = B_actT[0]
        cb = [sb("cb0", [128, ST + 3])] * 2; B_cb = [Buf("cb0")] * 2
        cy = [sb("cy0", [128, ST])] * 2; B_cy = [Buf("cy0")] * 2
        sg_tmp = cy; B_sg = B_cy
        csq = [sb("csq0", [128, ST], BF16)] * 2; B_csq = [Buf("csq0")] * 2
        crn = [sb("crn0", [128, ST])] * 2; B_crn = [Buf("crn0")] * 2
        ba_all = sb("ba_all", [128, 4, 16]); B_ba = Buf("ba")
        beta_all = sb("beta_all", [128, 4, 8]); g_all = sb("g_all", [128, 4, 8]); B_bg = Buf("betag")
        stage_f = crn; B_stf = B_crn
        stage_b = [sb("stage_b0", [128, 512], BF16)] * 2; B_stb = [Buf("stb0")] * 2
        gd = {}
        for nm_, shp, dt_ in [("gc", [128, 8], F32), ("ngc", [128, 8], F32), ("ekd", [128, 8], F32), ("bgc", [128, 8], F32),
                              ("GC", [128, 8, 128], F32), ("EGC", [128, 8, 128], F32)]:
            gd[nm_] = sb("gd_" + nm_, shp, dt_)
        gd["R"] = gd["EGC"]
        B_gdt = Buf("gd_tile")
        B_GC = Buf("gd_GC"); B_R = B_GC
        hb = []

        def make_hb(par):
            d_ = {}
            for nm_, shp, dt_ in [("arg", [128, 128], F32), ("decT", [128, 128], F32),
                                  ("A_f", [128, 128], F32), ("Y_f", [128, 128], F32),
                                  ("AT_b", [128, 128], F32),
                                  ("P0", [128, 128], F32), ("PT0", [128, 128], F32),
                                  ("P1", [128, 128], F32), ("PT1", [128, 128], F32),
                                  ("vn", [128, 128], F32), ("QKT", [128, 128], F32),
                                  ("QgT", [128, 128], F32)]:
                d_[nm_] = sb("gh%d_%s" % (par, nm_), shp, dt_)
                d_["B_" + nm_] = Buf("gh%d_%s" % (par, nm_))
            d_["decs"] = d_["arg"]; d_["B_decs"] = d_["B_arg"]
            for a_, b_ in (("Kbg", "P0"), ("Kd", "PT0"), ("Vb", "P1"), ("WTn", "PT1")):
                d_[a_] = d_[b_]; d_["B_" + a_] = d_["B_" + b_]
            d_["A_b"] = d_["A_f"]; d_["B_A_b"] = d_["B_A_f"]
            d_["Y_b"] = d_["Y_f"]; d_["B_Y_b"] = d_["B_Y_f"]
            hb.append(d_)

        for par_ in range(4):
            make_hb(par_)
        ab01 = sbp("ab01", [128, 8, 2, 128]); ab2a = sbp("ab2a", [128, 4, 32]); ab2b = sbp("ab2b", [32, 4, 32])
        for (dst, src) in [(ab01, ab01_in), (ab2a, ab2a_in), (ab2b, ab2b_in)]:
            em.dma("sp", dst[:], src, writes=[B_cst])
        KT2 = sbp("KT2", [128, 4, SEQX], BF16); B_KT2 = Buf("KT2")
        KT01 = sbp("KT01", [128, 8, 2 * ST], BF16); B_KT01 = [Buf("KT01a"), Buf("KT01b")]
        S_f = sbp("S_f", [128, 8, 128]); S_b = S_f
        B_S = [Buf("S%d" % h) for h in range(8)]
        halo = sbp("halo", [128, 24, 3]); B_halo = Buf("halo")
        vt = [sbp("vt%d" % i, [128, 512], BF16) for i in range(2)]; B_vt = [Buf("vt%d" % i) for i in range(2)]
        scs = [sbp("scs0", [128, 512])] * 2; B_scs = [Buf("scs0")] * 2
        pts = [sbp("pts%d" % i, [128, 512], BF16) for i in range(2)]; B_pts = [Buf("pts%d" % i) for i in range(2)]
        cnt = {"st": 0, "sb": 0, "cb": 0}

        em.op("pool", lambda e: e.memset(S_f[:], 0.0), writes=B_S)
        em.op("pool", lambda e: e.memset(halo[:], 0.0), writes=[B_halo])

        def out_dma(dst, src, reads):
            if os.environ.get('DBG_NOOUT'):
                return
            em.dma("pool", dst, src, reads=reads, writes=(), owner=reads[0])

        gcount = {"h": 0}

        def gdn_tile(C, qT, kT, vT, B_q, B_k, B_v, beta, g, B_bgin, Sf, Sb, B_Sl, need_o, oT_dst, B_oT):
            nsq = min(4, max(1, int(math.ceil(math.log2(C))) - 1))
            p0, bp0 = pget()
            em.op("pe", lambda e: e.matmul(p0[0:C, 0:8], lhsT=triU[0:C, 0:C], rhs=g, start=True, stop=True),
                  reads=[B_cst, B_bgin], writes=[bp0])
            em.op("act", lambda e: e.activation(out=gd["gc"][0:C, :], in_=p0[0:C, 0:8], func=AF.Copy),
                  reads=[bp0], writes=[B_gdt])
            em.op("dve", lambda e: e.tensor_scalar(out=gd["ngc"][0:C, :], in0=p0[0:C, 0:8], scalar1=-1.0, scalar2=None,
                                                   op0=ALU.mult), reads=[bp0], writes=[B_gdt])
            for h in range(8):
                em.op("dve", lambda e: e.tensor_scalar(out=gd["R"][0:C, h, 0:C], in0=triU[0:C, 0:C],
                                                       scalar1=g[:, h:h + 1], scalar2=None, op0=ALU.mult),
                      reads=[B_cst, B_bgin], writes=[B_R])
            hpb = max(1, 512 // C)
            for h0 in range(0, 8, min(8, hpb)):
                nh = min(8, hpb)
                pg_, bpg_ = pget()
                src = pg_[:, 0:nh * C].rearrange("p (h c) -> p h c", c=C)
                em.op("pe", lambda e: e.matmul(src, lhsT=ones_f[0:C, :], rhs=gd["R"][0:C, h0:h0 + nh, 0:C],
                                               start=True, stop=True), reads=[B_cst, B_R], writes=[bpg_])
                em.op("dve", lambda e: e.tensor_copy(out=gd["GC"][:, h0:h0 + nh, 0:C], in_=src),
                      reads=[bpg_], writes=[B_GC])
                em.op("act", lambda e: e.activation(out=gd["EGC"][:, h0:h0 + nh, 0:C], in_=src, func=AF.Exp),
                      reads=[bpg_], writes=[B_GC])
            em.op("dve", lambda e: e.tensor_tensor(out=gd["ekd"][0:C, :], in0=gd["GC"][0:C, :, C - 1],
                                                   in1=gd["gc"][0:C, :], op=ALU.subtract),
                  reads=[B_GC, B_gdt], writes=[B_gdt])
            em.op("act", lambda e: e.activation(out=gd["ekd"][0:C, :], in_=gd["ekd"][0:C, :], func=AF.Exp),
                  reads=[B_gdt], writes=[B_gdt])
            em.op("act", lambda e: e.activation(out=gd["bgc"][0:C, :], in_=gd["gc"][0:C, :], func=AF.Exp),
                  reads=[B_gdt], writes=[B_gdt])
            em.op("dve", lambda e: e.tensor_tensor(out=gd["bgc"][0:C, :], in0=gd["bgc"][0:C, :], in1=beta, op=ALU.mult),
                  reads=[B_gdt, B_bgin], writes=[B_gdt])
            def head_gen(h, T):
                kTh = kT(h)
                pkk, bpkk = pget()
                em.op("pe", lambda e: e.matmul(pkk[0:C, 0:C], lhsT=kTh, rhs=kTh, start=True, stop=True),
                      reads=[B_k], writes=[bpkk])
                em.op("dve", lambda e: e.scalar_tensor_tensor(out=T["arg"][0:C, 0:C], in0=gd["GC"][0:C, h, 0:C],
                                                              scalar=-1.0, in1=nm_strict[0:C, 0:C],
                                                              op0=ALU.mult, op1=ALU.add),
                      reads=[B_GC, B_cst], writes=[T["B_arg"]])
                em.op("act", lambda e: e.activation(out=T["decs"][0:C, 0:C], in_=T["arg"][0:C, 0:C], func=AF.Exp,
                                                    bias=gd["gc"][0:C, h:h + 1]),
                      reads=[T["B_arg"], B_gdt], writes=[T["B_decs"]])
                em.op("dve", lambda e: e.scalar_tensor_tensor(out=T["A_f"][0:C, 0:C], in0=pkk[0:C, 0:C],
                                                              scalar=beta[:, h:h + 1], in1=T["decs"][0:C, 0:C],
                                                              op0=ALU.mult, op1=ALU.mult),
                      reads=[bpkk, B_bgin, T["B_decs"]], writes=[T["B_A_f"]])
                if need_o:
                    em.op("dve", lambda e: e.tensor_tensor(out=T["arg"][0:C, 0:C], in0=gd["GC"][0:C, h, 0:C],
                                                           in1=nm_inclT[0:C, 0:C], op=ALU.add),
                          reads=[B_GC, B_cst], writes=[T["B_arg"]])
                    em.op("act", lambda e: e.activation(out=T["decT"][0:C, 0:C], in_=T["arg"][0:C, 0:C], func=AF.Exp,
                                                        bias=gd["ngc"][0:C, h:h + 1]),
                          reads=[T["B_arg"], B_gdt], writes=[T["B_decT"]])
                yield
                pat, bpat = pget()
                em.op("pe", lambda e: e.transpose(out=pat[0:C, 0:C], in_=T["A_f"][0:C, 0:C], identity=ident_f[0:C, 0:C]),
                      reads=[T["B_A_f"], B_cst], writes=[bpat])
                em.op("dve", lambda e: e.scalar_tensor_tensor(out=T["Y_f"][0:C, 0:C], in0=pat[0:C, 0:C], scalar=-1.0,
                                                              in1=ident_f[0:C, 0:C], op0=ALU.mult, op1=ALU.add),
                      reads=[bpat, B_cst], writes=[T["B_Y_f"]])
                em.op("act", lambda e: e.activation(out=T["AT_b"][0:C, 0:C], in_=pat[0:C, 0:C], func=AF.Copy),
                      reads=[bpat], writes=[T["B_AT_b"]])
                P, PT_, BP, BPT = T["A_b"], T["AT_b"], T["B_A_b"], T["B_AT_b"]
                for s in range(nsq):
                    nP, nPT = T["P%d" % (s % 2)], T["PT%d" % (s % 2)]
                    BnP, BnPT = T["B_P%d" % (s % 2)], T["B_PT%d" % (s % 2)]
                    yield
                    pp, bpp = pget()
                    em.op("pe", lambda e: e.matmul(pp[0:C, 0:C], lhsT=PT_[0:C, 0:C], rhs=P[0:C, 0:C], start=True, stop=True),
                          reads=[BP, BPT], writes=[bpp])
                    copy_op("act", nP[0:C, 0:C], pp[0:C, 0:C], [bpp], [BnP])
                    if s < nsq - 1:
                        yield
                        pq, bpq = pget()
                        em.op("pe", lambda e: e.matmul(pq[0:C, 0:C], lhsT=P[0:C, 0:C], rhs=PT_[0:C, 0:C], start=True, stop=True),
                              reads=[BP, BPT], writes=[bpq])
                        copy_op("dve", nPT[0:C, 0:C], pq[0:C, 0:C], [bpq], [BnPT])
                    yield
                    py, bpy = pget()
                    em.op("pe", lambda e: e.matmul(py[0:C, 0:C], lhsT=nP[0:C, 0:C], rhs=T["Y_b"][0:C, 0:C], start=True, stop=True),
                          reads=[BnP, T["B_Y_b"]], writes=[bpy])
                    em.op("dve", lambda e: e.tensor_tensor(out=T["Y_f"][0:C, 0:C], in0=py[0:C, 0:C], in1=T["Y_f"][0:C, 0:C],
                                                           op=ALU.add), reads=[bpy, T["B_Y_f"]], writes=[T["B_Y_f"]])
                    P, PT_, BP, BPT = nP, nPT, BnP, BnPT
                yield
                ptk, bptk = pgetb()
                em.op("pe", lambda e: e.transpose(out=ptk[0:C, 0:128], in_=kTh, identity=ident_b[:, :]),
                      reads=[B_k, B_cst], writes=[bptk])
                em.op("pe", lambda e: e.transpose(out=ptk[0:C, 128:256], in_=vT(h), identity=ident_b[:, :]),
                      reads=[B_v, B_cst], writes=[bptk])
                em.op("dve", lambda e: e.tensor_scalar(out=T["Kbg"][0:C, :], in0=ptk[0:C, 0:128], scalar1=gd["bgc"][0:C, h:h + 1],
                                                       scalar2=None, op0=ALU.mult), reads=[bptk, B_gdt], writes=[T["B_Kbg"]])
                em.op("act", lambda e: e.activation(out=T["Kd"][0:C, :], in_=ptk[0:C, 0:128], func=AF.Copy,
                                                    scale=gd["ekd"][0:C, h:h + 1]), reads=[bptk, B_gdt], writes=[T["B_Kd"]])
                em.op("dve", lambda e: e.tensor_scalar(out=T["Vb"][0:C, :], in0=ptk[0:C, 128:256], scalar1=beta[:, h:h + 1],
                                                       scalar2=None, op0=ALU.mult), reads=[bptk, B_bgin], writes=[T["B_Vb"]])
                yield
                pw, bpw = pget()
                em.op("pe", lambda e: e.matmul(pw[:, 0:C], lhsT=T["Kbg"][0:C, :], rhs=T["Y_f"][0:C, 0:C], start=True, stop=True),
                      reads=[T["B_Kbg"], T["B_Y_f"]], writes=[bpw])
                copy_op("act", T["WTn"][:, 0:C], pw[:, 0:C], [bpw], [T["B_WTn"]], scale=-1.0)
                yield
                pv, bpv = pget()
                em.op("pe", lambda e: e.matmul(pv[0:C, 0:128], lhsT=T["Y_f"][0:C, 0:C], rhs=T["Vb"][0:C, :], start=True, stop=False),
                      reads=[T["B_Y_f"], T["B_Vb"]], writes=[bpv])
                em.op("pe", lambda e: e.matmul(pv[0:C, 0:128], lhsT=T["WTn"][:, 0:C], rhs=Sf(h), start=False, stop=True),
                      reads=[T["B_WTn"], B_Sl[h]], writes=[bpv])
                copy_op("dve", T["vn"][0:C, :], pv[0:C, 0:128], [bpv], [T["B_vn"]])
                if need_o:
                    yield
                    pqk, bpqk = pget()
                    em.op("pe", lambda e: e.matmul(pqk[0:C, 0:C], lhsT=kTh, rhs=qT(h), start=True, stop=True),
                          reads=[B_k, B_q], writes=[bpqk])
                    em.op("dve", lambda e: e.tensor_tensor(out=T["QKT"][0:C, 0:C], in0=pqk[0:C, 0:C], in1=T["decT"][0:C, 0:C],
                                                           op=ALU.mult), reads=[bpqk, T["B_decT"]], writes=[T["B_QKT"]])
                    em.op("pool", lambda e: e.tensor_tensor(out=T["QgT"][:, 0:C], in0=qT(h), in1=gd["EGC"][:, h, 0:C],
                                                            op=ALU.mult), reads=[B_q, B_GC], writes=[T["B_QgT"]])
                    yield
                    po_, bpo_ = pget()
                    em.op("pe", lambda e: e.matmul(po_[:, 0:C], lhsT=Sf(h), rhs=T["QgT"][:, 0:C], start=True, stop=False),
                          reads=[B_Sl[h], T["B_QgT"]], writes=[bpo_])
                    em.op("pe", lambda e: e.matmul(po_[:, 0:C], lhsT=T["vn"][0:C, :], rhs=T["QKT"][0:C, 0:C], start=False, stop=True),
                          reads=[T["B_vn"], T["B_QKT"]], writes=[bpo_])
                    copy_op("act", oT_dst(h), po_[:, 0:C], [bpo_], [B_oT])
                yield
                ps_, bps_ = pget()
                em.op("pe", lambda e: e.matmul(ps_[:, 0:128], lhsT=T["Kd"][0:C, :], rhs=T["vn"][0:C, :], start=True, stop=True),
                      reads=[T["B_Kd"], T["B_vn"]], writes=[bps_])
                em.op("dve", lambda e: e.scalar_tensor_tensor(out=Sf(h), in0=Sf(h), scalar=gd["EGC"][:, h, C - 1:C],
                                                              in1=ps_[:, 0:128], op0=ALU.mult, op1=ALU.add),
                      reads=[bps_, B_GC, B_Sl[h]], writes=[B_Sl[h]])
                yield

            NIL = len(hb)
            for h0 in range(0, 8, NIL):
                alive = [head_gen(h0 + j, hb[j]) for j in range(NIL) if h0 + j < 8]
                while alive:
                    for g_ in list(alive):
                        try:
                            next(g_)
                        except StopIteration:
                            alive.remove(g_)

        def beta_g(ba_ap, beta_o, g_o, npart, reads):
            n = ba_ap.shape[1]
            y = sp_y[0:npart, 0:n, :]; p = sp_p[0:npart, 0:n, :]; m = sp_m[0:npart, 0:n, :]
            em.op("act", lambda e: e.activation(out=beta_o, in_=ba_ap[:, :, 0:8], func=AF.Sigmoid), reads=reads, writes=[B_bg])
            for j in range(n):
                em.op("dve", lambda e: e.tensor_tensor(out=y[:, j, :], in0=ba_ap[:, j, 8:16], in1=dtb[0:npart, :], op=ALU.add),
                      reads=reads + [B_cst], writes=[B_spt])
            em.op("act", lambda e: e.activation(out=y, in_=y, func=AF.Exp), reads=[B_spt], writes=[B_spt])
            em.op("act", lambda e: e.activation(out=g_o, in_=y, func=AF.Ln, bias=1.0), reads=[B_spt], writes=[B_bg])
            em.op("dve", lambda e: e.tensor_scalar(out=p, in0=y, scalar1=1.0 / 7, scalar2=None, op0=ALU.mult),
                  reads=[B_spt], writes=[B_spt])
            for ck in (-1.0 / 6, 1.0 / 5, -1.0 / 4, 1.0 / 3, -1.0 / 2, 1.0):
                em.op("dve", lambda e: e.scalar_tensor_tensor(out=p, in0=p, scalar=float(ck), in1=y, op0=ALU.add, op1=ALU.mult),
                      reads=[B_spt], writes=[B_spt])
            em.op("dve", lambda e: e.tensor_single_scalar(out=m, in_=y, scalar=0.3, op=ALU.is_lt), reads=[B_spt], writes=[B_spt])
            em.op("dve", lambda e: e.tensor_tensor(out=p, in0=p, in1=g_o, op=ALU.subtract), reads=[B_spt, B_bg], writes=[B_spt])
            em.op("dve", lambda e: e.tensor_tensor(out=p, in0=p, in1=m, op=ALU.mult), reads=[B_spt], writes=[B_spt])
            em.op("dve", lambda e: e.tensor_tensor(out=g_o, in0=g_o, in1=p, op=ALU.add), reads=[B_spt, B_bg], writes=[B_bg])
            for j in range(n):
                em.op("dve", lambda e: e.tensor_tensor(out=g_o[:, j, :], in0=g_o[:, j, :], in1=negA[0:npart, :], op=ALU.mult),
                      reads=[B_bg, B_cst], writes=[B_bg])

        def conv_chunk(c, ps, bps, N, sample=False):
            i = cnt["cb"] % 2
            cnt["cb"] += 1
            if not sample:
                em.op("pool", lambda e: e.tensor_copy(out=cb[i][:, 0:3], in_=halo[:, c, :]), reads=[B_halo], writes=[B_cb[i]])
                em.op("act", lambda e: e.activation(out=cb[i][:, 3:3 + N], in_=ps[:, 0:N], func=AF.Copy), reads=[bps], writes=[B_cb[i]])
                em.op("pool", lambda e: e.tensor_copy(out=halo[:, c, :], in_=cb[i][:, N:N + 3]), reads=[B_cb[i]], writes=[B_halo])
                win = lambda j: cb[i][:, j:j + N]
                yv = cy[i][:, 0:N]
                rbuf = [B_cb[i]]
            else:
                cbs = scb[:, c, :, :]
                em.op("act", lambda e: e.activation(out=cbs[:, :, 3:7], in_=ps[:, 0:N].rearrange("p (b s) -> p b s", s=4),
                                                    func=AF.Copy), reads=[bps], writes=[B_scb])
                win = lambda j: cbs[:, :, j:j + 4]
                yv = cy[i][:, 0:N].rearrange("p (b s) -> p b s", s=4)
                rbuf = [B_scb]
            em.op("dve", lambda e: e.tensor_scalar(out=yv, in0=win(0), scalar1=convw[:, c, 0:1], scalar2=None, op0=ALU.mult),
                  reads=rbuf + [B_cst], writes=[B_cy[i]])
            for j in range(1, 4):
                em.op("dve", lambda e: e.scalar_tensor_tensor(out=yv, in0=win(j), scalar=convw[:, c, j:j + 1], in1=yv,
                                                              op0=ALU.mult, op1=ALU.add),
                      reads=rbuf + [B_cst, B_cy[i]], writes=[B_cy[i]])
            y2 = cy[i][:, 0:N]
            if c >= 16:
                em.op("act", lambda e: e.activation(out=vbT[:, c - 16, 0:N], in_=y2, func=AF.Silu), reads=[B_cy[i]], writes=[B_vb])
                return
            em.op("act", lambda e: e.activation(out=y2, in_=y2, func=AF.Silu), reads=[B_cy[i]], writes=[B_cy[i]])
            em.op("act", lambda e: e.activation(out=csq[i][:, 0:N], in_=y2, func=AF.Square), reads=[B_cy[i]], writes=[B_csq[i]])
            pss, bpss = pget()
            em.op("pe", lambda e: e.matmul(pss[:, 0:N], lhsT=ones_b, rhs=csq[i][:, 0:N], start=True, stop=True),
                  reads=[B_cst, B_csq[i]], writes=[bpss])
            em.op("act", lambda e: e.activation(out=crn[i][:, 0:N], in_=pss[:, 0:N], func=AF.Sqrt, bias=epsc[:, 0:1]),
                  reads=[bpss, B_cst], writes=[B_crn[i]])
            em.op("dve", lambda e: e.reciprocal(out=crn[i][:, 0:N], in_=crn[i][:, 0:N]), reads=[B_crn[i]], writes=[B_crn[i]])
            if c < 8:
                dst, bd, scl = qbT[:, c, 0:N], B_qb, 128.0 ** -0.5
            else:
                dst, bd, scl = kbT[:, c - 8, 0:N], B_kb, 1.0
            em.op("dve", lambda e: e.scalar_tensor_tensor(out=dst, in0=y2, scalar=scl, in1=crn[i][:, 0:N],
                                                          op0=ALU.mult, op1=ALU.mult),
                  reads=[B_cy[i], B_crn[i]], writes=[bd])

        def gated_norm(N, hT, B_hT):
            for h in range(8):
                if h % 4 == 0:
                    wz, bwz = wload("in", 0, 8, C_Z + (h // 4) * 512, 512)
                pz_, bpz_ = pget()
                mm_fm(hT, B_hT, wz, bwz, 8, (h % 4) * 128, 128, N, pz_, bpz_)
                em.op("act", lambda e: e.activation(out=ztmp[:, 0:N], in_=pz_[:, 0:N], func=AF.Silu),
                      reads=[bpz_], writes=[B_zt])
                i = 0
                em.op("act", lambda e: e.activation(out=csq[i][:, 0:N], in_=oTr[:, h, 0:N], func=AF.Square),
                      reads=[B_oTr], writes=[B_csq[i]])
                pss, bpss = pget()
                em.op("pe", lambda e: e.matmul(pss[:, 0:N], lhsT=ones_b, rhs=csq[i][:, 0:N], start=True, stop=True),
                      reads=[B_cst, B_csq[i]], writes=[bpss])
                em.op("act", lambda e: e.activation(out=crn[i][:, 0:N], in_=pss[:, 0:N], func=AF.Sqrt, bias=epsc[:, 0:1],
                                                    scale=1.0 / 128), reads=[bpss, B_cst], writes=[B_crn[i]])
                em.op("dve", lambda e: e.reciprocal(out=crn[i][:, 0:N], in_=crn[i][:, 0:N]), reads=[B_crn[i]], writes=[B_crn[i]])
                em.op("dve", lambda e: e.tensor_tensor(out=crn[i][:, 0:N], in0=crn[i][:, 0:N], in1=oTr[:, h, 0:N], op=ALU.mult),
                      reads=[B_crn[i], B_oTr], writes=[B_crn[i]])
                em.op("dve", lambda e: e.scalar_tensor_tensor(out=obT[:, h, 0:N], in0=crn[i][:, 0:N], scalar=gnorm[:, 0:1],
                                                              in1=ztmp[:, 0:N], op0=ALU.mult, op1=ALU.mult),
                      reads=[B_crn[i], B_cst, B_zt], writes=[B_obT])

        acnt = {"v": 0, "s": 0, "p": 0}

        def attn_block(g, nq, qcols, tiles, first_group, scat):
            ptl = []
            for (nk, kTf, bk, vrows, bias, mcol) in tiles:
                vi = acnt["v"] % 2; acnt["v"] += 1
                em.dma("sp", vt[vi][0:nk, :], vrows, reads=(), writes=[B_vt[vi]])
                pS, bpS = pget()
                for h in range(4):
                    em.op("pe", lambda e: e.matmul(pS[0:nk, h * nq:(h + 1) * nq], lhsT=kTf(h), rhs=qcols(h),
                                                   start=True, stop=True), reads=[bk, B_QT], writes=[bpS])
                si = acnt["s"] % 2; acnt["s"] += 1
                em.op("dve", lambda e: e.tensor_tensor(out=scs[si][0:nk, 0:4 * nq].rearrange("p (h q) -> p h q", q=nq),
                                                       in0=pS[0:nk, 0:4 * nq].rearrange("p (h q) -> p h q", q=nq),
                                                       in1=bias, op=ALU.add),
                      reads=[bpS, B_cst], writes=[B_scs[si]])
                pi = acnt["p"] % 2; acnt["p"] += 1
                em.op("act", lambda e: e.activation(out=pts[pi][0:nk, 0:4 * nq], in_=scs[si][0:nk, 0:4 * nq], func=AF.Exp,
                                                    bias=mcol), reads=[B_scs[si], B_cst], writes=[B_pts[pi]])
                ptl.append((nk, vi, pi))
            pN, bpN = pget()
            pD, bpD = pget()
            first = True
            for ti, (nk, vi, pi) in enumerate(ptl):
                lastt = ti == len(ptl) - 1
                for h in range(4):
                    em.op("pe", lambda e: e.matmul(pN[:, h * nq:(h + 1) * nq], lhsT=vt[vi][0:nk, h * 128:(h + 1) * 128],
                                                   rhs=pts[pi][0:nk, h * nq:(h + 1) * nq], start=(first and h == 0),
                                                   stop=(lastt and h == 3)), reads=[B_vt[vi], B_pts[pi]], writes=[bpN])
                em.op("pe", lambda e: e.matmul(pD[:, 0:4 * nq], lhsT=ones_b[0:nk, :], rhs=pts[pi][0:nk, 0:4 * nq],
                                               start=first, stop=lastt), reads=[B_cst, B_pts[pi]], writes=[bpD])
                first = False
            srcN = pN[:, 0:4 * nq].rearrange("p (h q) -> p h q", q=nq)
            srcD = pD[:, 0:4 * nq].rearrange("p (h q) -> p h q", q=nq)
            if first_group:
                em.op("act", lambda e: e.activation(out=scat(numT), in_=srcN, func=AF.Copy), reads=[bpN], writes=[B_nd])
                em.op("dve", lambda e: e.tensor_copy(out=scat(denT), in_=srcD), reads=[bpD], writes=[B_nd])
            else:
                em.op("dve", lambda e: e.tensor_tensor(out=scat(numT), in0=srcN, in1=scat(numT), op=ALU.add),
                      reads=[bpN, B_nd], writes=[B_nd])
                em.op("dve", lambda e: e.tensor_tensor(out=scat(denT), in0=srcD, in1=scat(denT), op=ALU.add),
                      reads=[bpD, B_nd], writes=[B_nd])

        def q_group(g, hT, B_hT, N, dst, B_dst):
            wq, bwq = wload("in", 0, 8, C_QA + g * 512, 512)
            for j in range(4):
                pq_, bpq_ = pget()
                mm_fm(hT, B_hT, wq, bwq, 8, j * 128, 128, N, pq_, bpq_)
                copy_op(ev_eng(), dst(j), pq_[:, 0:N], [bpq_], [B_dst], scale=128.0 ** -0.5)

        def attention_st(s, hT, B_hT):
            T0 = (NPRE + s) * ST
            cur = (NPRE + s) % 2
            prv = 1 - cur
            zc = pmask[:, 4:5]
            em.finish("sp", B_stb)
            q_group(0, hT, B_hT, ST, lambda j: QT[:, j, :], B_QT)
            for qb in range(4):
                if qb == 0:
                    kprev = lambda h: KT01[:, h, prv * ST + 384: prv * ST + 512]; bkp = B_KT01[prv]
                    mc = pmask[:, 0:1] if s == 0 else zc
                else:
                    kprev = (lambda qb_: (lambda h: KT01[:, h, cur * ST + (qb_ - 1) * 128: cur * ST + qb_ * 128]))(qb); bkp = B_KT01[cur]
                    mc = zc
                kcur = (lambda qb_: (lambda h: KT01[:, h, cur * ST + qb_ * 128: cur * ST + (qb_ + 1) * 128]))(qb)
                tl = [(128, kprev, bkp, vscr[T0 + qb * 128 - 128:T0 + qb * 128, 0:512], ab01[:, 0:4, 0, :], mc),
                      (128, kcur, B_KT01[cur], vscr[T0 + qb * 128:T0 + qb * 128 + 128, 0:512], ab01[:, 0:4, 1, :], zc)]
                attn_block(0, 128, (lambda qb_: (lambda h: QT[:, h, qb_ * 128:(qb_ + 1) * 128]))(qb), tl, True,
                           (lambda qb_: (lambda tns: tns[:, :, qb_ * 128:(qb_ + 1) * 128]))(qb))
            q_group(1, hT, B_hT, ST, lambda j: QT[:, j, :], B_QT)
            for r in range(4):
                kprev = (lambda r_: (lambda h: KT01[:, 4 + h, sl(prv * ST + r_, 128, 4)]))(r)
                kcur = (lambda r_: (lambda h: KT01[:, 4 + h, sl(cur * ST + r_, 128, 4)]))(r)
                mc = pmask[:, 0:1] if s == 0 else zc
                tl = [(128, kprev, B_KT01[prv], vscr[sl(T0 - ST + r, 128, 4), 512:1024], ab01[:, 4:8, 0, :], mc),
                      (128, kcur, B_KT01[cur], vscr[sl(T0 + r, 128, 4), 512:1024], ab01[:, 4:8, 1, :], zc)]
                attn_block(1, 128, (lambda r_: (lambda h: QT[:, h, sl(r_, 128, 4)]))(r), tl, False,
                           (lambda r_: (lambda tns: tns[:, :, sl(r_, 128, 4)]))(r))
            q_group(2, hT, B_hT, ST, lambda j: QT[:, j, :], B_QT)
            for r in range(16):
                ka = (lambda r_: (lambda h: KT2[:, h, sl(T0 - 2048 + r_, 128, 16)]))(r)
                kb_ = (lambda r_: (lambda h: KT2[:, h, sl(T0 + r_, 32, 16)]))(r)
                mc = pmask[:, s:s + 1]
                tl = [(128, ka, B_KT2, vscr[sl(T0 - 2048 + r, 128, 16), 1024:1536], ab2a[:, :, :], mc),
                      (32, kb_, B_KT2, vscr[sl(T0 + r, 32, 16), 1024:1536], ab2b[:, :, :], pmask[0:32, 4:5])]
                attn_block(2, 32, (lambda r_: (lambda h: QT[:, h, sl(r_, 32, 16)]))(r), tl, False,
                           (lambda r_: (lambda tns: tns[:, :, sl(r_, 32, 16)]))(r))
            em.op("dve", lambda e: e.reciprocal(out=denT[:], in_=denT[:]), reads=[B_nd], writes=[B_nd])
            em.op("dve", lambda e: e.tensor_tensor(out=oaT[:], in0=numT[:], in1=denT[:], op=ALU.mult),
                  reads=[B_nd], writes=[B_oaT])

        sgm = big[:, 8:12, :]; B_sgm = B_big
        mrg = big[:, 0:8, :].rearrange("p k n -> p (k n)").rearrange("p (t d) -> p t d", d=D); B_mrg = B_big

        def post_mixer(tiles, npart, hT, B_hT, mT, B_mT, fT, B_fT, ydst):
            N = npart * len(tiles)
            nt = len(tiles)
            for half in range(2):
                for br in range(2):
                    wg, bwg = wload("in", 0, 8, C_G + br * D + half * 512, 512)
                    for t in range(nt):
                        pg_, bpg_ = pget()
                        mm_tm(hT, B_hT, t * npart, npart, wg, bwg, 8, 0, 512, pg_, bpg_)
                        em.op("act", lambda e: e.activation(out=sgm[0:npart, t, :], in_=pg_[0:npart, :], func=AF.Sigmoid),
                              reads=[bpg_], writes=[B_sgm])
                    if br == 0:
                        wp, bwp = wload("pa", 0, 4, half * 512, 512)
                        src, bsrc, nk = oaT, B_oaT, 4
                    else:
                        wp, bwp = wload("pb", 0, 8, half * 512, 512)
                        src, bsrc, nk = obT, B_obT, 8
                    for t in range(nt):
                        pp_, bpp_ = pget()
                        mm_tm(src, bsrc, t * npart, npart, wp, bwp, nk, 0, 512, pp_, bpp_)
                        mo = mrg[0:npart, t, half * 512:(half + 1) * 512]
                        if br == 0:
                            em.op("dve", lambda e: e.tensor_tensor(out=mo, in0=pp_[0:npart, :], in1=sgm[0:npart, t, :], op=ALU.mult),
                                  reads=[bpp_, B_sgm], writes=[B_mrg])
                        else:
                            em.op("dve", lambda e: e.tensor_tensor(out=sgm[0:npart, t, :], in0=pp_[0:npart, :], in1=sgm[0:npart, t, :],
                                                                   op=ALU.mult), reads=[bpp_, B_sgm], writes=[B_sgm])
                            em.op("pool", lambda e: e.tensor_tensor(out=mo, in0=mo, in1=sgm[0:npart, t, :], op=ALU.add),
                                  reads=[B_sgm, B_mrg], writes=[B_mrg])
            for t in range(nt):
                em.op("act", lambda e: e.activation(out=xn_tmp[0:npart, :], in_=mrg[0:npart, t, :], func=AF.Copy),
                      reads=[B_mrg], writes=[B_xn])
                pt, bpt = pgetb()
                for kc in range(8):
                    em.op("pe", lambda e: e.transpose(out=pt[:, kc * 128:kc * 128 + npart], in_=xn_tmp[0:npart, kc * 128:(kc + 1) * 128],
                                                      identity=ident_b[0:npart, 0:npart]), reads=[B_xn, B_cst], writes=[bpt])
                copy_op(ev_eng(), mT[:, :, t * npart:(t + 1) * npart],
                        pt[:, :].rearrange("p (k c) -> p k c", c=128)[:, :, 0:npart], [bpt], [B_mT])
            for half in range(2):
                wo, bwo = wload("out", 0, 8, half * 512, 512)
                for t, (x_ap, xb) in enumerate(tiles):
                    po, bpo = pget()
                    mm_tm(mT, B_mT, t * npart, npart, wo, bwo, 8, 0, 512, po, bpo)
                    xo = x_ap[:, half * 512:(half + 1) * 512]
                    em.op("dve", lambda e: e.tensor_tensor(out=xo, in0=po[0:npart, :], in1=xo, op=ALU.add),
                          reads=[bpo, xb], writes=[xb])
            ffn(tiles, npart, 2, "gu2", "d2", fT, B_fT)
            for t, (x_ap, xb) in enumerate(tiles):
                rms_rstd(x_ap, xb, npart, 1)
                em.op("dve", lambda e: e.scalar_tensor_tensor(out=x_ap, in0=x_ap, scalar=sm[0:npart, 1:2], in1=normout[0:npart, :],
                                                              op0=ALU.mult, op1=ALU.mult), reads=[xb, B_sm, B_cst], writes=[xb])
                out_dma(ydst(t), x_ap, [xb])

        n_st = NPRE + NMAIN if STAGE >= 2 else 1
        import os
        if os.environ.get('DBG_NST'):
            n_st = int(os.environ['DBG_NST'])
        for st in range(n_st):
            main = st >= NPRE and not os.environ.get('DBG_NOMAIN')
            T0 = st * ST
            tiles = [(xt[:, t, :], B_xt[t]) for t in range(4)]
            for t in range(4):
                em.dma("sp", xt[:, t, :], xseq[T0 + t * 128:T0 + (t + 1) * 128, :], writes=[B_xt[t]])
            ffn(tiles, 128, 0, "gu1", "d1", actT[0], B_actT[0])
            hT, B_hT = actT[1], B_actT[1]
            norm_T(tiles, 128, 1, hT, B_hT)
            cur = st % 2
            for hg in range(3):
                wk, bwk = wload("in", 0, 8, C_KA + hg * 512, 512)
                for j in range(4):
                    pk_, bpk_ = pget()
                    mm_fm(hT, B_hT, wk, bwk, 8, j * 128, 128, ST, pk_, bpk_)
                    hd = hg * 4 + j
                    if hd < 8:
                        copy_op(ev_eng(), KT01[:, hd, cur * ST:(cur + 1) * ST], pk_[:, :], [bpk_], [B_KT01[cur]])
                    else:
                        copy_op(ev_eng(), KT2[:, hd - 8, T0:T0 + ST], pk_[:, :], [bpk_], [B_KT2])
            for g in range(3):
                need_k = main and ((g == 2) or (g == 1 and st == n_st - 1) or (g == 0 and st == n_st - 1))
                if need_k:
                    wk, bwk = wload("in", 0, 8, C_KA + g * 512, 512)
                wv, bwv = wload("in", 0, 8, C_VA + g * 512, 512)
                for t in range(4):
                    mrow = (st - NPRE) * ST + t * 128
                    if g == 2:
                        dk_, ok = kv2_out, main
                        r0 = mrow
                    elif g == 1:
                        dk_, ok = kv1_out, main and st == n_st - 1
                        r0 = t * 128
                    else:
                        dk_, ok = kv0_out, main and st == n_st - 1 and t == 3
                        r0 = 0
                    if need_k and ok:
                        pk_, bpk_ = pget()
                        mm_tm(hT, B_hT, t * 128, 128, wk, bwk, 8, 0, 512, pk_, bpk_)
                        i = cnt["st"] % 2; cnt["st"] += 1
                        copy_op("act", stage_f[i][:, :], pk_[:, :], [bpk_], [B_stf[i]])
                        out_dma(dk_[r0:r0 + 128, 0, :], stage_f[i][:, :], [B_stf[i]])
                    pv_, bpv_ = pget()
                    mm_tm(hT, B_hT, t * 128, 128, wv, bwv, 8, 0, 512, pv_, bpv_)
                    i = cnt["sb"] % 2; cnt["sb"] += 1
                    copy_op("dve", stage_b[i][:, :], pv_[:, :], [bpv_], [B_stb[i]])
                    em.dma("pool", vscr[T0 + t * 128:T0 + (t + 1) * 128, g * 512:(g + 1) * 512], stage_b[i][:, :],
                           reads=[B_stb[i]], writes=(), owner=B_stb[i])
                    if ok:
                        i = cnt["st"] % 2; cnt["st"] += 1
                        copy_op("act", stage_f[i][:, :], pv_[:, :], [bpv_], [B_stf[i]])
                        out_dma(dk_[r0:r0 + 128, 1, :], stage_f[i][:, :], [B_stf[i]])
            wb_, bwb_ = wload("in", 0, 8, C_B, 16)
            for t in range(4):
                pb_, bpb_ = pget()
                mm_tm(hT, B_hT, t * 128, 128, wb_, bwb_, 8, 0, 16, pb_, bpb_)
                copy_op("act", ba_all[:, t, :], pb_[:, 0:16], [bpb_], [B_ba])
            beta_g(ba_all[:, :, :], beta_all[:, :, :], g_all[:, :, :], 128, [B_ba])
            for cg in range(6):
                wc, bwc = wload("in", 0, 8, C_QKV + cg * 512, 512)
                for j in range(4):
                    c = cg * 4 + j
                    if st < NPRE - 1 and c < 8:
                        continue
                    pc_, bpc_ = pget()
                    mm_fm(hT, B_hT, wc, bwc, 8, j * 128, 128, ST, pc_, bpc_)
                    conv_chunk(c, pc_, bpc_, ST)
            if STAGE >= 3:
                for t in range(4):
                    cs = slice(t * 128, (t + 1) * 128)
                    gdn_tile(128,
                             (lambda cs_: (lambda h: qbT[:, h, cs_]))(cs), (lambda cs_: (lambda h: kbT[:, h, cs_]))(cs),
                             (lambda cs_: (lambda h: vbT[:, h, cs_]))(cs), B_qb, B_kb, B_vb,
                             beta_all[:, t, :], g_all[:, t, :], B_bg,
                             lambda h: S_f[:, h, :], lambda h: S_b[:, h, :], B_S, main,
                             (lambda cs_: (lambda h: oTr[:, h, cs_]))(cs), B_oTr)
            if main and STAGE >= 4:
                gated_norm(ST, hT, B_hT)
                attention_st(st - NPRE, hT, B_hT)
            if main and STAGE >= 5:
                post_mixer(tiles, 128, hT, B_hT, actT[0], B_actT[0], actT[1], B_actT[1],
                           lambda t: y_out[(st - NPRE) * ST + t * 128:(st - NPRE) * ST + (t + 1) * 128, :])
        if STAGE >= 3:
            for h in range(8):
                out_dma(ssmp_out[h], S_f[:, h, :], [B_S[h]])
        for c in range(24):
            em.dma("sp", convp_out[:, c * 128:(c + 1) * 128].rearrange("t p -> p t"), halo[:, c, :], reads=[B_halo], writes=(),
                   owner=B_halo, allow_slow_non_contiguous=True)

        em.barrier()
        pstack.close()
        scb = sb("scb", [128, 24, NB_S, 7]); B_scb = Buf("scb")
        sample_branch = STAGE >= 6
        if sample_branch:
            xs_t = xt[0:NTS, 0, :]; B_xs = B_xt[0]
            stiles = [(xs_t, B_xs)]
            em.dma("sp", xs_t, xs_in, writes=[B_xs])
            sc_tm = sb("sc_tm", [48, 3072]); B_sct = Buf("sc_tm")
            em.dma("sp", sc_tm[:, :], sconv, writes=[B_sct])
            for c in range(24):
                pt_, bpt_ = pget()
                em.op("pe", lambda e: e.transpose(out=pt_[:, 0:48], in_=sc_tm[:, c * 128:(c + 1) * 128], identity=ident_f[0:48, 0:48]),
                      reads=[B_sct, B_cst], writes=[bpt_])
                copy_op(ev_eng(), scb[:, c, :, 0:3], pt_[:, 0:48].rearrange("p (b s) -> p b s", s=3), [bpt_], [B_scb])
            ffn(stiles, NTS, 0, "gu1", "d1", actT[0], B_actT[0])
            hT, B_hT = actT[1], B_actT[1]
            norm_T(stiles, NTS, 1, hT, B_hT)
            N = NTS
            QTs = sb("QsT", [128, 12, NTS], BF16)
            KsT = sb("KsT", [128, 12, NTS], BF16); B_KsT = Buf("KsT")
            for hg in range(3):
                wq, bwq = wload("in", 0, 8, C_QA + hg * 512, 512)
                for j in range(4):
                    pq_, bpq_ = pget()
                    mm_fm(hT, B_hT, wq, bwq, 8, j * 128, 128, N, pq_, bpq_)
                    copy_op(ev_eng(), QTs[:, hg * 4 + j, 0:N], pq_[:, 0:N], [bpq_], [B_QT], scale=128.0 ** -0.5)
            for hg in range(3):
                wk, bwk = wload("in", 0, 8, C_KA + hg * 512, 512)
                for j in range(4):
                    pk_, bpk_ = pget()
                    mm_fm(hT, B_hT, wk, bwk, 8, j * 128, 128, N, pk_, bpk_)
                    copy_op(ev_eng(), KsT[:, hg * 4 + j, 0:N], pk_[:, 0:N], [bpk_], [B_KsT])
            Vs_tm = sb("Vs_tm", [NTS, 1536]); B_Vs = Buf("Vs_tm")
            for g in range(3):
                wk, bwk = wload("in", 0, 8, C_KA + g * 512, 512)
                pk_, bpk_ = pget()
                mm_tm(hT, B_hT, 0, N, wk, bwk, 8, 0, 512, pk_, bpk_)
                i = cnt["st"] % 2; cnt["st"] += 1
                copy_op("act", stage_f[i][0:N, :], pk_[0:N, :], [bpk_], [B_stf[i]])
                out_dma(kvs_out[g][:, 0, :], stage_f[i][0:N, :], [B_stf[i]])
                wv, bwv = wload("in", 0, 8, C_VA + g * 512, 512)
                pv_, bpv_ = pget()
                mm_tm(hT, B_hT, 0, N, wv, bwv, 8, 0, 512, pv_, bpv_)
                copy_op("dve", Vs_tm[:, g * 512:(g + 1) * 512], pv_[0:N, :], [bpv_], [B_Vs])
                out_dma(kvs_out[g][:, 1, :], Vs_tm[:, g * 512:(g + 1) * 512], [B_Vs])
            for cg in range(6):
                wc, bwc = wload("in", 0, 8, C_QKV + cg * 512, 512)
                for j in range(4):
                    c = cg * 4 + j
                    pc_, bpc_ = pget()
                    mm_fm(hT, B_hT, wc, bwc, 8, j * 128, 128, N, pc_, bpc_)
                    conv_chunk(c, pc_, bpc_, N, sample=True)
            for c in range(24):
                for s_ in range(3):
                    em.dma("sp", convs_out[:, s_, c * 128:(c + 1) * 128].rearrange("b p -> p b"), scb[:, c, :, 4 + s_],
                           reads=[B_scb], writes=(), owner=B_scb, allow_slow_non_contiguous=True)
            if STAGE >= 7:
                wb_, bwb_ = wload("in", 0, 8, C_B, 16)
                Ss_f = [sb("Ss_f%d" % i, [128, 8, 128]) for i in range(2)]
                Ss_b = Ss_f
                B_Ss = [[Buf("Ss%d_%d" % (i, h)) for h in range(8)] for i in range(2)]
                ba_s = [sb("ba_s%d" % i, [4, 1, 16]) for i in range(2)]; B_bas = [Buf("bas0"), Buf("bas1")]
                bt_s = [sb("bt_s%d" % i, [4, 1, 8]) for i in range(2)]
                g_s = [sb("g_s%d" % i, [4, 1, 8]) for i in range(2)]
                make_hb(4)
                for b in range(NB_S):
                    i = b % 2
                    for h in range(8):
                        em.dma("sp", Ss_f[i][:, h, :], sssm[b, h], writes=[B_Ss[i][h]])
                    pb_, bpb_ = pget()
                    mm_tm(hT, B_hT, 4 * b, 4, wb_, bwb_, 8, 0, 16, pb_, bpb_)
                    copy_op("act", ba_s[i][:, 0, :], pb_[0:4, 0:16], [bpb_], [B_bas[i]])
                    beta_g(ba_s[i][:, :, :], bt_s[i][:, :, :], g_s[i][:, :, :], 4, [B_bas[i]])
                    cs = slice(4 * b, 4 * b + 4)
                    gdn_tile(4,
                             (lambda cs_: (lambda h: qbT[:, h, cs_]))(cs), (lambda cs_: (lambda h: kbT[:, h, cs_]))(cs),
                             (lambda cs_: (lambda h: vbT[:, h, cs_]))(cs), B_qb, B_kb, B_vb,
                             bt_s[i][:, 0, :], g_s[i][:, 0, :], B_bg,
                             (lambda i_: (lambda h: Ss_f[i_][:, h, :]))(i), (lambda i_: (lambda h: Ss_b[i_][:, h, :]))(i), B_Ss[i], True,
                             (lambda cs_: (lambda h: oTr[:, h, cs_]))(cs), B_oTr)
                    for h in range(8):
                        out_dma(ssms_out[b, h], Ss_f[i][:, h, :], [B_Ss[i][h]])
                gated_norm(N, hT, B_hT)
            if STAGE >= 8:
                ckt = [sb("ckt%d" % i, [128, 1024]) for i in range(2)]; B_ckt = [Buf("ckt%d" % i) for i in range(2)]
                kTs = [sb("kTs%d" % i, [128, 4, 128], BF16) for i in range(2)]; B_kTs = [Buf("kTs0"), Buf("kTs1")]
                vnew = [sb("vnew0", [4, 1536])] * 2; B_vnew = [Buf("vnew0")] * 2
                pTs = [sb("pTs%d" % i, [128, 16]) for i in range(4)]; B_pTs = [Buf("pTs%d" % i) for i in range(4)]
                scc = {"c": 0, "k": 0, "p": 0}
                ones4 = ones_f
                for b in range(NB_S):
                    vi = b % 2
                    em.dma("sp", vnew[vi][:, :], Vs_tm[4 * b:4 * b + 4, :], reads=[B_Vs], writes=[B_vnew[vi]])
                    pN, bpN = pget(hold=True)
                    pD, bpD = pget(hold=True)
                    first = True
                    tl = []
                    tl.append((0, 0, ck0[b, :, :]))
                    for r in range(4):
                        tl.append((1, 1 + r, ck1[b, sl(r, 128, 4), :]))
                    for r in range(4):
                        tl.append((2, 5 + r, ck2[b, sl(r, 128, 16), :]))
                    for (g, bi, src) in tl:
                        ci = scc["c"] % 2; scc["c"] += 1
                        em.dma("sp", ckt[ci][:, :], src, writes=[B_ckt[ci]])
                        ki = scc["k"] % 2; scc["k"] += 1
                        for h in range(4):
                            ptr, bptr = pget()
                            em.op("pe", lambda e: e.transpose(out=ptr[:, 0:128], in_=ckt[ci][:, h * 128:(h + 1) * 128], identity=ident_f),
                                  reads=[B_ckt[ci], B_cst], writes=[bptr])
                            copy_op(ev_eng(), kTs[ki][:, h, :], ptr[:, 0:128], [bptr], [B_kTs[ki]])
                        pS, bpS = pget()
                        for h in range(4):
                            em.op("pe", lambda e: e.matmul(pS[:, h * 4:(h + 1) * 4], lhsT=kTs[ki][:, h, :], rhs=QTs[:, g * 4 + h, 4 * b:4 * b + 4],
                                                           start=True, stop=True), reads=[B_kTs[ki], B_QT], writes=[bpS])
                        pi = scc["p"] % 4; scc["p"] += 1
                        em.op("dve", lambda e: e.tensor_tensor(out=pTs[pi][:, :], in0=pS[:, 0:16], in1=sbias[:, bi, :], op=ALU.add),
                              reads=[bpS, B_cst], writes=[B_pTs[pi]])
                        em.op("act", lambda e: e.activation(out=pTs[pi][:, :], in_=pTs[pi][:, :], func=AF.Exp),
                              reads=[B_pTs[pi]], writes=[B_pTs[pi]])
                        for h in range(4):
                            em.op("pe", lambda e: e.matmul(pN[:, h * 4:(h + 1) * 4], lhsT=ckt[ci][:, 512 + h * 128:512 + (h + 1) * 128],
                                                           rhs=pTs[pi][:, h * 4:(h + 1) * 4], start=(first and h == 0), stop=False),
                                  reads=[B_ckt[ci], B_pTs[pi]], writes=[bpN])
                        em.op("pe", lambda e: e.matmul(pD[:, 0:16], lhsT=ones_f, rhs=pTs[pi][:, :], start=first, stop=False),
                              reads=[B_cst, B_pTs[pi]], writes=[bpD])
                        first = False
                    for g in range(3):
                        pS, bpS = pget()
                        for h in range(4):
                            em.op("pe", lambda e: e.matmul(pS[0:4, h * 4:(h + 1) * 4], lhsT=KsT[:, g * 4 + h, 4 * b:4 * b + 4],
                                                           rhs=QTs[:, g * 4 + h, 4 * b:4 * b + 4], start=True, stop=True),
                                  reads=[B_KsT, B_QT], writes=[bpS])
                        pi = scc["p"] % 4; scc["p"] += 1
                        em.op("dve", lambda e: e.tensor_tensor(out=pTs[pi][0:4, :], in0=pS[0:4, 0:16], in1=sbiasn[:, g, :], op=ALU.add),
                              reads=[bpS, B_cst], writes=[B_pTs[pi]])
                        em.op("act", lambda e: e.activation(out=pTs[pi][0:4, :], in_=pTs[pi][0:4, :], func=AF.Exp),
                              reads=[B_pTs[pi]], writes=[B_pTs[pi]])
                        for h in range(4):
                            em.op("pe", lambda e: e.matmul(pN[:, h * 4:(h + 1) * 4], lhsT=vnew[vi][:, g * 512 + h * 128:g * 512 + (h + 1) * 128],
                                                           rhs=pTs[pi][0:4, h * 4:(h + 1) * 4], start=False, stop=(g == 2 and h == 3)),
                                  reads=[B_vnew[vi], B_pTs[pi]], writes=[bpN])
                        em.op("pe", lambda e: e.matmul(pD[:, 0:16], lhsT=ones_f[0:4, :], rhs=pTs[pi][0:4, :], start=False, stop=(g == 2)),
                              reads=[B_cst, B_pTs[pi]], writes=[bpD])
                    dsl = denT[:, :, 4 * b:4 * b + 4]
                    em.op("dve", lambda e: e.reciprocal(out=dsl, in_=pD[:, 0:16].rearrange("p (h s) -> p h s", s=4)),
                          reads=[bpD], writes=[B_nd])
                    em.op("dve", lambda e: e.tensor_tensor(out=oaT[:, :, 4 * b:4 * b + 4], in0=pN[:, 0:16].rearrange("p (h s) -> p h s", s=4),
                                                           in1=dsl, op=ALU.mult), reads=[bpN, B_nd], writes=[B_oaT])
                    prelease(bpN); prelease(bpD)
            if STAGE >= 9:
                post_mixer(stiles, NTS, hT, B_hT, actT[0], B_actT[0], actT[1], B_actT[1], lambda t: ys_out[:, :])

        allb = [B_scb] + B_xt + B_stf + B_stb
        if sample_branch and STAGE >= 7:
            allb += B_Ss[0] + B_Ss[1]
        if sample_branch:
            allb += [B_Vs]
        em.finish("sp", allb)
    print("[kernel] instructions emitted:", em.ninst, "dma sems:", em.n_dsem)
    return nc


def _slopes():
    return np.exp2(-8.0 * np.arange(1, 13, dtype=np.float64) / 12).astype(np.float64)


def _const_tables():
    sl_ = _slopes()
    cst = np.zeros((128, 5, 128), np.float32)
    i = np.arange(128)
    cst[:, 0, :] = np.eye(128)
    cst[:, 1, :] = 1.0
    cst[:, 2, :] = (i[:, None] <= i[None, :])
    cst[:, 3, :] = np.where(i[:, None] > i[None, :], 0.0, NEG)
    cst[:, 4, :] = np.where(i[None, :] >= i[:, None], 0.0, NEG)
    ab01 = np.zeros((128, 8, 2, 128), np.float32)
    k = i[:, None]; q = i[None, :]
    for h in range(8):
        dil = 1 if h < 4 else 4
        s = sl_[h]
        dprev = 128 + q - k
        ab01[:, h, 0, :] = np.where(dprev <= 128, -s * dil * dprev, NEG)
        dcur = q - k
        ab01[:, h, 1, :] = np.where(dcur >= 0, -s * dil * dcur, NEG)
    ab2a = np.zeros((128, 4, 32), np.float32)
    ab2b = np.zeros((32, 4, 32), np.float32)
    a = np.arange(128)[:, None]; qi = np.arange(32)[None, :]; c = np.arange(32)[:, None]
    for h in range(4):
        s = sl_[8 + h]
        d = 128 + qi - a
        ab2a[:, h, :] = np.where(d <= 128, -s * 16 * d, NEG)
        d2 = qi - c
        ab2b[:, h, :] = np.where(d2 >= 0, -s * 16 * d2, NEG)
    sbias = np.full((128, 9, 16), NEG, np.float32)
    p = np.arange(128)
    for h in range(4):
        for s_ in range(4):
            d = 128 + s_ - p
            sbias[:, 0, h * 4 + s_] = np.where(p >= s_, -sl_[h] * d, NEG)
            sbias[:, 1 + s_, h * 4 + s_] = -sl_[4 + h] * 4 * (128 - p)
            sbias[:, 5 + s_, h * 4 + s_] = -sl_[8 + h] * 16 * (128 - p)
    sbn = np.full((4, 3, 16), NEG, np.float32)
    for h in range(4):
        for s_ in range(4):
            for sp in range(4):
                if sp <= s_:
                    sbn[sp, 0, h * 4 + s_] = -sl_[h] * (s_ - sp)
                if sp == s_:
                    sbn[sp, 1, h * 4 + s_] = 0.0
                    sbn[sp, 2, h * 4 + s_] = 0.0
    return cst, ab01, ab2a, ab2b, sbias, sbn


_NC_CACHE = {}


def kernel(x_prompt, x_sample, cache_kv_w128, cache_kv_w512, cache_kv_w2048, state_conv, state_ssm,
           norm_ffn1, w_ffn1_gu, w_ffn1_down, norm_mix, w_in, conv_w, gdn_a_log, gdn_dt_bias, gdn_norm,
           w_proj_a, w_proj_b, w_out, norm_ffn2, w_ffn2_gu, w_ffn2_down, norm_out):
    f = lambda a: np.ascontiguousarray(np.asarray(a, dtype=np.float32))
    x_prompt = f(x_prompt); x_sample = f(x_sample)
    if "nc" not in _NC_CACHE:
        _NC_CACHE["nc"] = build_program()
    nc = _NC_CACHE["nc"]
    cst, ab01, ab2a, ab2b, sbias, sbn = _const_tables()
    normT = np.stack([f(norm_ffn1)[0], f(norm_mix)[0], f(norm_ffn2)[0]], 0).reshape(3, 8, 128).transpose(2, 0, 1)
    normout = np.broadcast_to(f(norm_out)[None, :], (128, D))
    convw = f(conv_w)[0].reshape(4, 24, 128).transpose(2, 1, 0)
    gnorm = f(gdn_norm)[0].reshape(128, 1)
    alog = np.broadcast_to(f(gdn_a_log)[0][None, :], (128, 8))
    dtb = np.broadcast_to(f(gdn_dt_bias)[0][None, :], (128, 8))
    shared = {
        "w_gu1": f(w_ffn1_gu)[0], "w_d1": f(w_ffn1_down)[0], "w_in": f(w_in)[0], "w_pa": f(w_proj_a)[0],
        "w_pb": f(w_proj_b)[0], "w_out": f(w_out)[0], "w_gu2": f(w_ffn2_gu)[0], "w_d2": f(w_ffn2_down)[0],
        "normT": f(normT), "normout": f(normout), "convw": f(convw), "gnorm": f(gnorm), "alog": f(alog), "dtb": f(dtb),
        "cst_f": cst, "ab01": ab01, "ab2a": ab2a, "ab2b": ab2b, "sbias": sbias, "sbiasn": sbn,
    }
    ck0 = f(cache_kv_w128)[0].reshape(128, 128, 1024)
    ck1 = f(cache_kv_w512)[0].reshape(128, 512, 1024)
    ck2 = f(cache_kv_w2048)[0].reshape(128, 2048, 1024)
    sconv = f(state_conv)[0]
    sssm = f(state_ssm)[0]
    in_maps = []
    for c in range(8):
        b, hf = c // 2, c % 2
        if hf == 0:
            xseq = np.concatenate([np.zeros((2048, D), np.float32), x_prompt[b, 0:2048]], 0)
        else:
            xseq = x_prompt[b]
        pm = np.zeros((128, 8), np.float32)
        if hf == 0:
            a = np.arange(128)
            pm[:, 0] = NEG
            pm[:, 1] = np.where(a < 96, NEG, 0.0)
            pm[:, 2] = np.where(a < 64, NEG, 0.0)
            pm[:, 3] = np.where(a < 32, NEG, 0.0)
        m = dict(shared)
        m.update({
            "xseq": f(xseq), "xs": f(x_sample[16 * c:16 * c + 16].reshape(64, D)),
            "ck0": f(ck0[16 * c:16 * c + 16]), "ck1": f(ck1[16 * c:16 * c + 16]), "ck2": f(ck2[16 * c:16 * c + 16, 0:(64 if os.environ.get('DBG_SMALL') else 2048)]),
            "sconv": f(sconv[16 * c:16 * c + 16].reshape(48, 3072)), "sssm": f(sssm[16 * c:16 * c + 16]),
            "pmask": pm,
        })
        in_maps.append(m)
    res = run_bass_kernel_spmd(nc, in_maps, core_ids=list(range(8)))
    R = res.results
    y_prompt = np.zeros((4, 4096, D), np.float32)
    for c in range(8):
        b, hf = c // 2, c % 2
        y_prompt[b, hf * 2048:(hf + 1) * 2048] = R[c]["y"]
    y_sample = np.concatenate([R[c]["ys"].reshape(16, 4, D) for c in range(8)], 0)
    kvp = []
    for nm, W in (("kv0", 128), ("kv1", 512), ("kv2", 2048)):
        kvp.append(np.stack([R[2 * b + 1][nm].reshape(W, 2, 4, 128) for b in range(4)], 0)[None])
    convp = np.stack([R[2 * b + 1]["convp"] for b in range(4)], 0)[None]
    ssmp = np.stack([R[2 * b + 1]["ssmp"] for b in range(4)], 0)[None]
    kvs = []
    for g in range(3):
        kvs.append(np.concatenate([R[c]["kvs%d" % g].reshape(16, 4, 2, 4, 128) for c in range(8)], 0)[None])
    convs = np.concatenate([R[c]["convs"] for c in range(8)], 0)[None]
    ssms = np.concatenate([R[c]["ssms"] for c in range(8)], 0)[None]
    outs = (y_prompt, y_sample, kvp[0], kvp[1], kvp[2], convp, ssmp, kvs[0], kvs[1], kvs[2], convs, ssms)
    return tuple(np.ascontiguousarray(o.astype(np.float32)) for o in outs)
```
